# Optimizing a Trainium2 kernel written in Bass

```python
import math
import jax, jax.numpy as jnp
from jax import lax
import numpy as np

D_MODEL = 1024
BATCH = 4
SEQ = 8192
DEPTH = 1
DEC_BATCH = 32
DEC_SEQ = 32
PAST_LEN = 2048

CHUNK = 64
Q_BLOCK = 128
PLE_DIM = 256
D_FF = 4 * D_MODEL
EPS = 1e-6
NEG_INF = -1e30

FOX_HEADS = 8
FOX_HEAD_DIM = 64
FOX_WIDTH = FOX_HEADS * FOX_HEAD_DIM
MLA_HEADS = 8
MLA_NOPE_DIM = 64
MLA_ROPE_DIM = 32
MLA_V_DIM = 64
MLA_Q_RANK = 384
MLA_KV_RANK = 256
MLA_WIDTH = MLA_HEADS * MLA_V_DIM
ROPE_BASE = 10000.0
D_MIX = FOX_WIDTH + MLA_WIDTH
IN_SIZES = (FOX_WIDTH, FOX_WIDTH, FOX_WIDTH, FOX_HEADS, MLA_Q_RANK, MLA_KV_RANK, MLA_ROPE_DIM)
D_IN = FOX_WIDTH * 3 + FOX_HEADS + MLA_Q_RANK + MLA_KV_RANK + MLA_ROPE_DIM

kernel_name = 'fox_mla_hybrid_stream_step'

F32 = jnp.float32


def _rmsnorm(x, g):
    xf = x.astype(F32)
    y = xf * lax.rsqrt(jnp.mean(xf * xf, axis=-1, keepdims=True) + EPS)
    return (y * g.astype(F32)).astype(x.dtype)


def _apply_rope(x, pos):
    half = x.shape[-1] // 2
    inv = ROPE_BASE ** (-jnp.arange(half, dtype=F32) / half)
    ang = pos.astype(F32)[:, None] * inv
    ang = ang.reshape(ang.shape[:1] + (1,) * (x.ndim - 3) + (half,))
    cos, sin = jnp.cos(ang), jnp.sin(ang)
    x1 = x[..., :half].astype(F32)
    x2 = x[..., half:].astype(F32)
    return jnp.concatenate([x1 * cos - x2 * sin, x1 * sin + x2 * cos], axis=-1).astype(x.dtype)


def _project(xn, pos, lw):
    B, S, _ = xn.shape
    offs = []
    acc = 0
    for n in IN_SIZES[:-1]:
        acc += n
        offs.append(acc)
    z = xn @ lw['w_in']
    fq, fk, fv, ff, mq, mkv, mkr = jnp.split(z, offs, axis=-1)
    fox_q = fq.reshape(B, S, FOX_HEADS, FOX_HEAD_DIM)
    fox_k = fk.reshape(B, S, FOX_HEADS, FOX_HEAD_DIM)
    fox_v = fv.reshape(B, S, FOX_HEADS, FOX_HEAD_DIM)
    fox_logf = jax.nn.log_sigmoid((ff + lw['b_fox_f']).astype(F32))
    q = (_rmsnorm(mq, lw['g_mla_q']) @ lw['w_mla_q_up']).reshape(B, S, MLA_HEADS, MLA_NOPE_DIM + MLA_ROPE_DIM)
    q_nope = q[..., :MLA_NOPE_DIM]
    q_pe = _apply_rope(q[..., MLA_NOPE_DIM:], pos)
    ckv = _rmsnorm(mkv, lw['g_mla_kv'])
    kpe = _apply_rope(mkr, pos)
    return fox_q, fox_k, fox_v, fox_logf, q_nope, q_pe, ckv, kpe


def _fox_prompt(q, k, v, logf):
    B, S, H, D = q.shape
    scale = 1.0 / math.sqrt(D)
    c = jnp.cumsum(logf, axis=1)
    cT = c.transpose(0, 2, 1)
    nb = S // Q_BLOCK
    qb = q.reshape(B, nb, Q_BLOCK, H, D).transpose(1, 0, 2, 3, 4)
    cb = c.reshape(B, nb, Q_BLOCK, H).transpose(1, 0, 2, 3)
    kpos = jnp.arange(S)

    def one(args):
        i, qi, ci = args
        s = jnp.einsum('bqhd,bkhd->bhqk', qi, k, preferred_element_type=F32) * scale
        s = s + ci.transpose(0, 2, 1)[..., None] - cT[:, :, None, :]
        qpos = i * Q_BLOCK + jnp.arange(Q_BLOCK)
        s = jnp.where(kpos[None, :] <= qpos[:, None], s, NEG_INF)
        p = jax.nn.softmax(s, axis=-1)
        return jnp.einsum('bhqk,bkhd->bqhd', p.astype(v.dtype), v)

    o = lax.map(one, (jnp.arange(nb), qb, cb))
    return o.transpose(1, 0, 2, 3, 4).reshape(B, S, H, D)


def _fox_sample(q, k, v, logf, cache_k, cache_v, cache_logf):
    B, T, H, D = q.shape
    P = cache_k.shape[1]
    scale = 1.0 / math.sqrt(D)
    k_all = jnp.concatenate([cache_k, k], axis=1)
    v_all = jnp.concatenate([cache_v, v], axis=1)
    c = jnp.cumsum(jnp.concatenate([cache_logf.astype(F32), logf], axis=1), axis=1)
    cT = c.transpose(0, 2, 1)
    s = jnp.einsum('bqhd,bkhd->bhqk', q, k_all, preferred_element_type=F32) * scale
    s = s + cT[:, :, P:, None] - cT[:, :, None, :]
    kpos = jnp.arange(P + T)
    qpos = P + jnp.arange(T)
    s = jnp.where(kpos[None, :] <= qpos[:, None], s, NEG_INF)
    p = jax.nn.softmax(s, axis=-1)
    return jnp.einsum('bhqk,bkhd->bqhd', p.astype(v_all.dtype), v_all)


def _mla_prompt(q_nope, q_pe, ckv, kpe, w_kv_up):
    B, S, H, _ = q_nope.shape
    scale = 1.0 / math.sqrt(MLA_NOPE_DIM + MLA_ROPE_DIM)
    kv = (ckv @ w_kv_up).reshape(B, S, H, MLA_NOPE_DIM + MLA_V_DIM)
    k_nope = kv[..., :MLA_NOPE_DIM]
    v = kv[..., MLA_NOPE_DIM:]
    nb = S // Q_BLOCK
    qnb = q_nope.reshape(B, nb, Q_BLOCK, H, MLA_NOPE_DIM).transpose(1, 0, 2, 3, 4)
    qpb = q_pe.reshape(B, nb, Q_BLOCK, H, MLA_ROPE_DIM).transpose(1, 0, 2, 3, 4)
    kchunk = jnp.arange(S) // CHUNK

    def one(args):
        i, qn, qp = args
        s = (jnp.einsum('bqhn,bkhn->bhqk', qn, k_nope, preferred_element_type=F32)
             + jnp.einsum('bqhr,bkr->bhqk', qp, kpe, preferred_element_type=F32)) * scale
        qchunk = (i * Q_BLOCK + jnp.arange(Q_BLOCK)) // CHUNK
        s = jnp.where(kchunk[None, :] <= qchunk[:, None], s, NEG_INF)
        p = jax.nn.softmax(s, axis=-1)
        return jnp.einsum('bhqk,bkhv->bqhv', p.astype(v.dtype), v)

    o = lax.map(one, (jnp.arange(nb), qnb, qpb))
    return o.transpose(1, 0, 2, 3, 4).reshape(B, S, H, MLA_V_DIM)


def _mla_sample(q_nope, q_pe, ckv, kpe, cache_ckv, cache_kpe, w_kv_up):
    scale = 1.0 / math.sqrt(MLA_NOPE_DIM + MLA_ROPE_DIM)
    ckv_all = jnp.concatenate([cache_ckv, ckv], axis=1)
    kpe_all = jnp.concatenate([cache_kpe, kpe], axis=1)
    w_kv = w_kv_up.reshape(MLA_KV_RANK, MLA_HEADS, MLA_NOPE_DIM + MLA_V_DIM)
    w_uk = w_kv[..., :MLA_NOPE_DIM]
    w_uv = w_kv[..., MLA_NOPE_DIM:]
    q_lat = jnp.einsum('bqhn,chn->bqhc', q_nope, w_uk)
    s = (jnp.einsum('bqhc,bkc->bhqk', q_lat, ckv_all, preferred_element_type=F32)
         + jnp.einsum('bqhr,bkr->bhqk', q_pe, kpe_all, preferred_element_type=F32)) * scale
    p = jax.nn.softmax(s, axis=-1)
    o_lat = jnp.einsum('bhqk,bkc->bqhc', p.astype(ckv_all.dtype), ckv_all)
    return jnp.einsum('bqhc,chv->bqhv', o_lat, w_uv)


def _layer(h, p_l, pos, lw, cache):
    B, S, _ = h.shape
    xn = _rmsnorm(h, lw['g_pre_mix'])
    fq, fk, fv, flogf, q_nope, q_pe, ckv, kpe = _project(xn, pos, lw)
    if cache is None:
        o_fox = _fox_prompt(fq, fk, fv, flogf)
        o_mla = _mla_prompt(q_nope, q_pe, ckv, kpe, lw['w_mla_kv_up'])
    else:
        c_k, c_v, c_logf, c_ckv, c_kpe = cache
        o_fox = _fox_sample(fq, fk, fv, flogf, c_k, c_v, c_logf)
        o_mla = _mla_sample(q_nope, q_pe, ckv, kpe, c_ckv, c_kpe, lw['w_mla_kv_up'])
    of = _rmsnorm(o_fox.reshape(B, S, FOX_WIDTH), lw['g_fox_out'])
    om = _rmsnorm(o_mla.reshape(B, S, MLA_WIDTH), lw['g_mla_out'])
    mix = jnp.concatenate([of, om], axis=-1) @ lw['w_out']
    h = h + _rmsnorm(mix, lw['g_post_mix'])
    u = _rmsnorm(h, lw['g_pre_mlp']) @ lw['w_up']
    h = h + _rmsnorm(jnp.square(jax.nn.relu(u)) @ lw['w_down'], lw['g_post_mlp'])
    gate = jax.nn.sigmoid(_rmsnorm(h, lw['g_ple']) @ lw['w_ple_gate'])
    h = h + _rmsnorm((p_l @ lw['w_ple_proj']) * gate, lw['g_ple_post'])
    return h, (fk, fv, flogf, ckv, kpe)


def setup_inputs(seed: int = 0) -> dict:
    key = jax.random.key(seed)
    ks = jax.random.split(key, 32)
    nrm = lambda k, shape, s: jax.random.normal(k, shape, F32) * s
    gain = lambda k, n: 1.0 + 0.05 * jax.random.normal(k, (DEPTH, n), F32)
    return {
        'x_prompt': nrm(ks[0], (BATCH, SEQ, D_MODEL), 1.0),
        'x_sample': nrm(ks[1], (DEC_BATCH, DEC_SEQ, D_MODEL), 1.0),
        'p_prompt': nrm(ks[2], (DEPTH, BATCH, SEQ, PLE_DIM), 1.0),
        'p_sample': nrm(ks[3], (DEPTH, DEC_BATCH, DEC_SEQ, PLE_DIM), 1.0),
        'cache_fox_k': nrm(ks[4], (DEPTH, DEC_BATCH, PAST_LEN, FOX_HEADS, FOX_HEAD_DIM), 1.0),
        'cache_fox_v': nrm(ks[5], (DEPTH, DEC_BATCH, PAST_LEN, FOX_HEADS, FOX_HEAD_DIM), 1.0),
        'cache_fox_logf': jax.nn.log_sigmoid(2.0 + jax.random.normal(ks[6], (DEPTH, DEC_BATCH, PAST_LEN, FOX_HEADS), F32)),
        'cache_mla_ckv': nrm(ks[7], (DEPTH, DEC_BATCH, PAST_LEN, MLA_KV_RANK), 1.0),
        'cache_mla_kpe': nrm(ks[8], (DEPTH, DEC_BATCH, PAST_LEN, MLA_ROPE_DIM), 1.0),
        'g_pre_mix': gain(ks[9], D_MODEL),
        'w_in': nrm(ks[10], (DEPTH, D_MODEL, D_IN), D_MODEL ** -0.5),
        'b_fox_f': 2.0 + 0.5 * jax.random.normal(ks[11], (DEPTH, FOX_HEADS), F32),
        'g_mla_q': gain(ks[12], MLA_Q_RANK),
        'w_mla_q_up': nrm(ks[13], (DEPTH, MLA_Q_RANK, MLA_HEADS * (MLA_NOPE_DIM + MLA_ROPE_DIM)), MLA_Q_RANK ** -0.5),
        'g_mla_kv': gain(ks[14], MLA_KV_RANK),
        'w_mla_kv_up': nrm(ks[15], (DEPTH, MLA_KV_RANK, MLA_HEADS * (MLA_NOPE_DIM + MLA_V_DIM)), MLA_KV_RANK ** -0.5),
        'g_fox_out': gain(ks[16], FOX_WIDTH),
        'g_mla_out': gain(ks[17], MLA_WIDTH),
        'w_out': nrm(ks[18], (DEPTH, D_MIX, D_MODEL), D_MIX ** -0.5),
        'g_post_mix': gain(ks[19], D_MODEL),
        'g_pre_mlp': gain(ks[20], D_MODEL),
        'w_up': nrm(ks[21], (DEPTH, D_MODEL, D_FF), D_MODEL ** -0.5),
        'w_down': nrm(ks[22], (DEPTH, D_FF, D_MODEL), D_FF ** -0.5),
        'g_post_mlp': gain(ks[23], D_MODEL),
        'g_ple': gain(ks[24], D_MODEL),
        'w_ple_gate': nrm(ks[25], (DEPTH, D_MODEL, D_MODEL), D_MODEL ** -0.5),
        'w_ple_proj': nrm(ks[26], (DEPTH, PLE_DIM, D_MODEL), PLE_DIM ** -0.5),
        'g_ple_post': gain(ks[27], D_MODEL),
    }


def reference(x_prompt, x_sample, p_prompt, p_sample, cache_fox_k, cache_fox_v, cache_fox_logf,
              cache_mla_ckv, cache_mla_kpe, g_pre_mix, w_in, b_fox_f, g_mla_q, w_mla_q_up, g_mla_kv,
              w_mla_kv_up, g_fox_out, g_mla_out, w_out, g_post_mix, g_pre_mlp, w_up, w_down, g_post_mlp,
              g_ple, w_ple_gate, w_ple_proj, g_ple_post):
    pos_prompt = jnp.arange(x_prompt.shape[1])
    pos_sample = cache_fox_k.shape[2] + jnp.arange(x_sample.shape[1])
    hp, hs = x_prompt, x_sample
    st_p, st_s = [], []
    for l in range(DEPTH):
        lw = dict(g_pre_mix=g_pre_mix[l], w_in=w_in[l], b_fox_f=b_fox_f[l], g_mla_q=g_mla_q[l],
                  w_mla_q_up=w_mla_q_up[l], g_mla_kv=g_mla_kv[l], w_mla_kv_up=w_mla_kv_up[l],
                  g_fox_out=g_fox_out[l], g_mla_out=g_mla_out[l], w_out=w_out[l],
                  g_post_mix=g_post_mix[l], g_pre_mlp=g_pre_mlp[l], w_up=w_up[l], w_down=w_down[l],
                  g_post_mlp=g_post_mlp[l], g_ple=g_ple[l], w_ple_gate=w_ple_gate[l],
                  w_ple_proj=w_ple_proj[l], g_ple_post=g_ple_post[l])
        hp, sp = _layer(hp, p_prompt[l], pos_prompt, lw, None)
        hs, ss = _layer(hs, p_sample[l], pos_sample, lw,
                        (cache_fox_k[l], cache_fox_v[l], cache_fox_logf[l], cache_mla_ckv[l], cache_mla_kpe[l]))
        st_p.append(sp)
        st_s.append(ss)
    stk = lambda lst, j: jnp.stack([e[j] for e in lst], axis=0)
    return (hp, hs,
            stk(st_p, 0), stk(st_p, 1), stk(st_p, 2), stk(st_p, 3), stk(st_p, 4),
            stk(st_s, 0), stk(st_s, 1), stk(st_s, 2), stk(st_s, 3), stk(st_s, 4))
```

```python
from contextlib import ExitStack
import math
import numpy as np
import ml_dtypes
import concourse.bass as bass
import concourse.mybir as mybir
from concourse.bass_utils import run_bass_kernel_spmd

F32 = mybir.dt.float32
BF16 = mybir.dt.bfloat16
AF = mybir.ActivationFunctionType
ALU = mybir.AluOpType

NDMA_SEMS = 8
EPS = 1e-6
NEG = -30000.0


class Res:
    __slots__ = ("name", "w", "r", "excl")

    def __init__(self, name):
        self.name = name
        self.w = []
        self.r = []
        self.excl = False


class Eng:
    def __init__(self, name, handle):
        self.name = name
        self.h = handle
        self.n = 0
        self.sem = None
        self.seen = {}
        self.ndma = 0
        self.dsems = []


class Kern:
    def __init__(self, nc, stack):
        self.nc = nc
        self.stack = stack
        self.engs = {}
        for name, h in (("pe", nc.tensor), ("act", nc.scalar), ("dve", nc.vector),
                        ("pool", nc.gpsimd), ("sp", nc.sync)):
            e = Eng(name, h)
            e.sem = stack.enter_context(nc.semaphore("s_" + name))
            self.engs[name] = e
        for qn in ("sp", "pool"):
            e = self.engs[qn]
            e.dsems = [stack.enter_context(nc.semaphore("d_%s%d" % (qn, i))) for i in range(NDMA_SEMS)]

    def _wait(self, eng, tok):
        sem, val, src, idx = tok
        key = id(sem)
        if eng.seen.get(key, 0) >= val:
            return
        if src is eng and idx is not None:
            if eng.name == "pe" or idx < eng.n - 2:
                return
        eng.h.wait_ge(sem, val)
        eng.seen[key] = val

    def _deps(self, eng, reads, writes, pwrites):
        for r in reads:
            for t in r.w:
                self._wait(eng, t)
        for r in writes:
            for t in r.w:
                self._wait(eng, t)
            for t in r.r:
                self._wait(eng, t)
        for r in pwrites:
            for t in r.r:
                self._wait(eng, t)

    def _commit(self, tok, reads, writes, pwrites):
        for r in reads:
            if tok[3] is not None:
                r.r = [t for t in r.r if not (t[2] is tok[2] and t[3] is not None)]
            r.r.append(tok)
        for r in writes:
            r.w = [tok]
            r.r = []
        for r in pwrites:
            if r.r:
                r.w = [tok]
                r.r = []
            else:
                r.w.append(tok)

    def op(self, engname, fn, reads=(), writes=(), pwrites=()):
        eng = self.engs[engname]
        if any(r.excl for r in reads):
            writes = list(writes) + [r for r in reads if r.excl and r not in writes]
            reads = [r for r in reads if not r.excl]
        self._deps(eng, reads, writes, pwrites)
        ins = fn(eng.h)
        ins.then_inc(eng.sem, 1)
        idx = eng.n
        eng.n += 1
        tok = (eng.sem, idx + 1, eng, idx)
        self._commit(tok, reads, writes, pwrites)
        return tok

    def dma(self, qname, out, in_, reads=(), writes=(), pwrites=()):
        eng = self.engs[qname]
        i = eng.ndma
        sem = eng.dsems[i % NDMA_SEMS]
        prev = 16 * (i // NDMA_SEMS)
        if prev > 0 and eng.seen.get(id(sem), 0) < prev:
            eng.h.wait_ge(sem, prev)
            eng.seen[id(sem)] = prev
        self._deps(eng, reads, writes, pwrites)
        ins = eng.h.dma_start(out=out, in_=in_)
        ins.then_inc(sem, 16)
        eng.ndma += 1
        tok = (sem, 16 * (i // NDMA_SEMS + 1), eng, None)
        self._commit(tok, reads, writes, pwrites)
        return tok

    def barrier(self):
        for e in self.engs.values():
            for f in self.engs.values():
                if f is not e and f.n > 0 and e.seen.get(id(f.sem), 0) < f.n:
                    e.h.wait_ge(f.sem, f.n)
                    e.seen[id(f.sem)] = f.n
            for qn in ("sp", "pool"):
                q = self.engs[qn]
                for j in range(min(q.ndma, NDMA_SEMS)):
                    last = ((q.ndma - 1 - j) // NDMA_SEMS) * NDMA_SEMS + j
                    val = 16 * (last // NDMA_SEMS + 1)
                    sem = q.dsems[j]
                    if e.seen.get(id(sem), 0) < val:
                        e.h.wait_ge(sem, val)
                        e.seen[id(sem)] = val

    def finish(self):
        sp = self.engs["sp"]
        for qn in ("sp", "pool"):
            e = self.engs[qn]
            for j in range(min(e.ndma, NDMA_SEMS)):
                last = ((e.ndma - 1 - j) // NDMA_SEMS) * NDMA_SEMS + j
                val = 16 * (last // NDMA_SEMS + 1)
                sem = e.dsems[j]
                if sp.seen.get(id(sem), 0) < val:
                    sp.h.wait_ge(sem, val)
                    sp.seen[id(sem)] = val
        for name in ("pe", "act", "dve", "pool"):
            e = self.engs[name]
            if e.n > 0 and sp.seen.get(id(e.sem), 0) < e.n:
                sp.h.wait_ge(e.sem, e.n)
                sp.seen[id(e.sem)] = e.n


class Tile:
    uid = 0

    def __init__(self, K, stack, name, shape, dtype, nres=1, psum=False):
        nc = K.nc
        Tile.uid += 1
        name = "t%d_%s" % (Tile.uid, name)
        if psum:
            self.t = stack.enter_context(nc.psum_tensor(name, shape, dtype))
        else:
            self.t = stack.enter_context(nc.sbuf_tensor(name, shape, dtype))
        self.res = [Res("%s.%d" % (name, i)) for i in range(nres)]
        if psum:
            for r in self.res:
                r.excl = True

    def __getitem__(self, idx):
        return self.t[idx]

    @property
    def all(self):
        return self.res


class DT:
    def __init__(self, ap, name):
        self.ap = ap
        self.res = [Res(name)]

    @property
    def all(self):
        return self.res


D = 1024
NBLK = 16
NSLOT = 8
SEQ = 8192
NQ = NSLOT * 512
PAST = 2048
TS = 17
NKS = 4 * TS * 128
DIN = 2216
SC_F = 1.0 / 8.0
SC_M = 1.0 / math.sqrt(96.0)


DBG = dict(stop=None)


def build():
    nc = bass.Bass("TRN2", target_bir_lowering=False)

    def din(name, shape, dt=F32):
        return nc.dram_tensor(name, shape, dt, kind="ExternalInput").ap()

    def dout(name, shape):
        return nc.dram_tensor(name, shape, F32, kind="ExternalOutput").ap()

    def dscr(name, shape, dt=BF16):
        return DT(nc.dram_tensor(name, shape, dt, kind="Internal").ap(), name)

    xall = din("xall", [SEQ, D]); xown = din("xown", [NQ, D]); pown = din("pown", [NQ, 256])
    xs = din("xs", [128, D]); ps_in = din("ps", [128, 256])
    ck = din("ck", [4, PAST, 512]); cv = din("cv", [4, PAST, 512]); clf = din("clf", [4, PAST, 8])
    cckv = din("cckv", [4, PAST, 256]); ckpe = din("ckpe", [4, PAST, 32])
    w_in = din("w_in", [D, DIN]); w_qup = din("w_qup", [384, 768]); w_kvup = din("w_kvup", [256, 1024])
    w_out = din("w_out", [D, D]); w_up = din("w_up", [D, 4096]); w_down = din("w_down", [4096, D])
    w_gate = din("w_gate", [D, D]); w_proj = din("w_proj", [256, D])
    g_pre_mix = din("g_pre_mix", [D]); g_mla_q = din("g_mla_q", [384]); g_mla_kv = din("g_mla_kv", [256])
    g_out = din("g_out", [D]); g_post_mix = din("g_post_mix", [D]); g_pre_mlp = din("g_pre_mlp", [D])
    g_post_mlp = din("g_post_mlp", [D]); g_ple = din("g_ple", [D]); g_ple_post = din("g_ple_post", [D])
    b_f = din("b_f", [8])
    ident_d = din("ident", [128, 128]); su_d = din("su", [128, 128])
    rope_kv = din("rope_kv", [SEQ, 64]); rope_q = din("rope_q", [NQ, 32])
    rope_kv_s = din("rope_kv_s", [128, 64]); rope_q_s = din("rope_q_s", [128, 32])
    masks_d = din("masks", [2, 2, 8, 128, 512], BF16); smask_d = din("smask", [128, 32], BF16)
    sel_d = din("sel", [8, NSLOT * NBLK])

    y_own = dout("y_own", [NQ, D]); fk_o = dout("fk_o", [SEQ, 512]); fv_o = dout("fv_o", [SEQ, 512])
    lf_o = dout("lf_o", [SEQ, 8]); ckv_o = dout("ckv_o", [SEQ, 256]); kpe_o = dout("kpe_o", [SEQ, 32])
    ys_o = dout("ys_o", [128, D]); fks_o = dout("fks_o", [128, 512]); fvs_o = dout("fvs_o", [128, 512])
    lfs_o = dout("lfs_o", [128, 8]); ckvs_o = dout("ckvs_o", [128, 256]); kpes_o = dout("kpes_o", [128, 32])

    KTf = dscr("KTf", [4, 128, SEQ]); KTm = dscr("KTm", [4, 128, SEQ]); KPE = dscr("KPE", [32, SEQ])
    Vf = dscr("Vf", [SEQ, 512]); Vm = dscr("Vm", [SEQ, 512])
    QTf = dscr("QTf", [4, 128, NQ]); QTm = dscr("QTm", [8, 96, NQ]); CQ = dscr("CQ", [8, NQ])
    OS = dscr("OS", [NQ, D], F32)
    SKf = dscr("SKf", [3, 8, SEQ])
    KTfs = dscr("KTfs", [4, 128, NKS]); KTms = dscr("KTms", [4, 128, NKS]); KPEs = dscr("KPEs", [32, NKS])
    Vfs = dscr("Vfs", [NKS, 512]); Vms = dscr("Vms", [NKS, 512])
    QTfs = dscr("QTfs", [4, 128, 128]); QTms = dscr("QTms", [8, 96, 128]); CQs = dscr("CQs", [8, 128])
    OSs = dscr("OSs", [128, D], F32)
    WOUT = dscr("WOUT", [128, 8, D]); WGATE = dscr("WGATE", [128, 8, D]); WPROJ = dscr("WPROJ", [128, 2, D])
    WUP = dscr("WUP", [8, 128, 8, 512]); WDN = dscr("WDN", [8, 128, 32, 128])

    with ExitStack() as top:
        top.enter_context(nc.allow_non_contiguous_dma(reason="small strided layout DMAs"))
        K = Kern(nc, top)

        def T(stack, name, shape, dt, nres=1, psum=False):
            return Tile(K, stack, name, shape, dt, nres, psum)

        identf = T(top, "identf", [128, 128], F32)
        identb = T(top, "identb", [128, 128], BF16)
        suf = T(top, "suf", [128, 128], F32)
        onesf = T(top, "onesf", [128, 128], F32)
        onesrow = T(top, "onesrow", [8, 512], F32)
        WA = T(top, "WA", [128, 22656], BF16)
        gcol = T(top, "gcol", [128, 48], F32)
        gbc = T(top, "gbc", [128, 3 * D + 256], F32)
        bcol = T(top, "bcol", [8, 2], F32)
        negc = T(top, "negc", [128, 64 * 8], F32)
        negcs = T(top, "negcs", [128, 4 * TS * 8], F32)
        CB = T(top, "CB", [8, NBLK], F32)
        CO = T(top, "CO", [8, NSLOT], F32)
        selt = T(top, "selt", [8, NSLOT * NBLK], F32)
        smask = T(top, "smask", [128, 32], BF16)

        K.dma("sp", identf[:], ident_d, writes=identf.all)
        K.dma("sp", suf[:], su_d, writes=suf.all)
        K.dma("sp", selt[:], sel_d, writes=selt.all)
        K.dma("sp", smask[:], smask_d, writes=smask.all)
        K.op("dve", lambda e: e.tensor_copy(out=identb[:], in_=identf[:]), reads=identf.all, writes=identb.all)
        K.op("dve", lambda e: e.memset(onesf[:], 1.0), writes=onesf.all)
        K.op("dve", lambda e: e.memset(onesrow[:], 1.0), writes=onesrow.all)
        K.op("dve", lambda e: e.memset(CB[:], 0.0), writes=CB.all)
        K.dma("sp", gcol[:, 0:8], g_pre_mix.rearrange("(k p) -> p k", p=128), pwrites=gcol.all)
        K.dma("sp", gcol[:, 8:11], g_mla_q.rearrange("(k p) -> p k", p=128), pwrites=gcol.all)
        K.dma("sp", gcol[:, 11:19], g_out.rearrange("(k p) -> p k", p=128), pwrites=gcol.all)
        K.dma("sp", gcol[:, 19:27], g_pre_mlp.rearrange("(k p) -> p k", p=128), pwrites=gcol.all)
        K.dma("sp", gcol[:, 27:35], g_ple.rearrange("(k p) -> p k", p=128), pwrites=gcol.all)
        for i, g in enumerate((g_post_mix, g_post_mlp, g_ple_post)):
            K.dma("sp", gbc[:, i * D:(i + 1) * D], g.rearrange("(o n) -> o n", o=1).broadcast_to([128, D]), pwrites=gbc.all)
        K.dma("sp", gbc[:, 3 * D:3 * D + 256], g_mla_kv.rearrange("(o n) -> o n", o=1).broadcast_to([128, 256]), pwrites=gbc.all)
        K.dma("sp", bcol[:, 0:1], b_f.rearrange("(h o) -> h o", o=1), pwrites=bcol.all)
        K.op("dve", lambda e: e.tensor_scalar(out=bcol[:, 1:2], in0=bcol[:, 0:1], scalar1=-1.0, scalar2=None, op0=ALU.mult),
             reads=bcol.all, writes=bcol.all)

        w_in_sb = WA[:, 0:8 * DIN].rearrange("p (k n) -> p k n", k=8)
        o1 = 8 * DIN
        w_q_sb = WA[:, o1:o1 + 3 * 768].rearrange("p (k n) -> p k n", k=3)
        o2 = o1 + 3 * 768
        w_kn_sb = WA[:, o2:o2 + 1024].rearrange("p (k n) -> p k n", k=2)
        o3 = o2 + 1024
        w_kv_sb = WA[:, o3:o3 + 1024].rearrange("p (k n) -> p k n", k=2)
        w_out_sb = WA[:, 0:8192].rearrange("p (k n) -> p k n", k=8)
        w_gate_sb = WA[:, 8192:16384].rearrange("p (k n) -> p k n", k=8)
        w_proj_sb = WA[:, 16384:18432].rearrange("p (k n) -> p k n", k=2)

        p0 = ExitStack()
        if True:
            stg = [T(p0, "stg%d" % i, [128, DIN], F32) for i in range(2)]
            cvb = [T(p0, "cvb%d" % i, [128, 1024], BF16) for i in range(2)]
            cnt = [0]
            cnts = [0]

            def conv(src_ap, ncols, dst_fn, scale_ap=None):
                i = cnts[0] % 2
                cnts[0] += 1
                s = stg[i]
                K.dma("pool" if (cnts[0] <= 13 and i == 1) else "sp", s[:, 0:ncols], src_ap, writes=s.all)
                return s

            for kc in range(8):
                s = conv(w_in[kc * 128:(kc + 1) * 128, :], DIN, None)
                K.op("act", lambda e, s=s, kc=kc: e.activation(out=w_in_sb[:, kc, :], in_=s[:, 0:DIN], func=AF.Copy, scale=gcol[:, kc:kc + 1]),
                     reads=s.all + gcol.all, pwrites=WA.all)
            for kc in range(3):
                s = conv(w_qup[kc * 128:(kc + 1) * 128, :], 768, None)
                K.op("act", lambda e, s=s, kc=kc: e.activation(out=w_q_sb[:, kc, :], in_=s[:, 0:768], func=AF.Copy, scale=gcol[:, 8 + kc:9 + kc]),
                     reads=s.all + gcol.all, pwrites=WA.all)
            for kc in range(2):
                s = conv(w_kvup[kc * 128:(kc + 1) * 128, :], 1024, None)
                sv = s[:, 0:1024].rearrange("p (h t d) -> p h t d", h=8, t=2)
                K.op("act", lambda e, sv=sv, kc=kc: e.activation(out=w_kn_sb[:, kc, :].rearrange("p (h d) -> p h d", h=8), in_=sv[:, :, 0, :], func=AF.Copy),
                     reads=s.all, pwrites=WA.all)
                K.op("act", lambda e, sv=sv, kc=kc: e.activation(out=w_kv_sb[:, kc, :].rearrange("p (h d) -> p h d", h=8), in_=sv[:, :, 1, :], func=AF.Copy),
                     reads=s.all, pwrites=WA.all)

            def conv_to_scr(src_ap, ncols, dst_ap, dst_dt, scale_col):
                def load_fn():
                    return conv(src_ap, ncols, None)

                def fin_fn(s):
                    j = cnt[0] % 2
                    cnt[0] += 1
                    c = cvb[j]
                    if scale_col is None:
                        K.op("act", lambda e: e.activation(out=c[:, 0:ncols], in_=s[:, 0:ncols], func=AF.Copy), reads=s.all, writes=c.all)
                    else:
                        K.op("act", lambda e: e.activation(out=c[:, 0:ncols], in_=s[:, 0:ncols], func=AF.Copy, scale=gcol[:, scale_col:scale_col + 1]),
                             reads=s.all + gcol.all, writes=c.all)
                    K.dma("pool", dst_ap, c[:, 0:ncols] if dst_ap.shape[-1] == ncols else c[:, 0:ncols].rearrange("p (a b) -> p a b", b=dst_ap.shape[-1]),
                          reads=c.all, pwrites=dst_dt.all)
                return (load_fn, fin_fn)

            conv_state = {}

            def conv_step():
                if "pending" in conv_state:
                    conv_state.pop("pending")()
                if conv_jobs:
                    load_fn, fin_fn = conv_jobs.pop(0)()
                    s_ = load_fn()
                    conv_state["pending"] = lambda: fin_fn(s_)

            def conv_flush():
                while conv_jobs or "pending" in conv_state:
                    conv_step()

            conv_jobs = []
            for kc in range(8):
                conv_jobs.append(lambda kc=kc: conv_to_scr(w_out[kc * 128:(kc + 1) * 128, :], 1024, WOUT.ap[:, kc, :], WOUT, 11 + kc))
                conv_jobs.append(lambda kc=kc: conv_to_scr(w_gate[kc * 128:(kc + 1) * 128, :], 1024, WGATE.ap[:, kc, :], WGATE, 27 + kc))
            for kc in range(2):
                conv_jobs.append(lambda kc=kc: conv_to_scr(w_proj[kc * 128:(kc + 1) * 128, :], 1024, WPROJ.ap[:, kc, :], WPROJ, None))
            for kc in range(8):
                for q4 in range(4):
                    conv_jobs.append(lambda kc=kc, q4=q4: conv_to_scr(w_up[kc * 128:(kc + 1) * 128, q4 * 1024:(q4 + 1) * 1024], 1024,
                                                                     WUP.ap[2 * q4:2 * q4 + 2, :, kc, :].rearrange("c p n -> p c n"), WUP, 19 + kc))
            for kc in range(32):
                conv_jobs.append(lambda kc=kc: conv_to_scr(w_down[kc * 128:(kc + 1) * 128, :], 1024,
                                                           WDN.ap[:, :, kc, :].rearrange("c p n -> p c n"), WDN, None))

        def _done():
            K.finish()
            global _LASTK
            _LASTK = K
            return nc
        if DBG["stop"] == "0":
            conv_flush()
            return _done()

        def run_pairs(gens):
            gens = list(gens)
            for i in range(0, len(gens), 2):
                live = gens[i:i + 2]
                while live:
                    for g_ in list(live):
                        if DBG.get("gstop") is not None:
                            DBG["gcount"] = DBG.get("gcount", 0) + 1
                            if DBG["gcount"] > DBG["gstop"]:
                                return
                        try:
                            next(g_)
                        except StopIteration:
                            live.remove(g_)

        def rstd_from_ss(ss_ap, n, out_ap, tmp_ap, rd, wr):
            K.op("act", lambda e: e.activation(out=tmp_ap, in_=ss_ap, func=AF.Ln, scale=1.0 / n, bias=EPS), reads=rd, writes=wr)
            K.op("act", lambda e: e.activation(out=out_ap, in_=tmp_ap, func=AF.Exp, scale=-0.5), reads=wr, writes=wr)

        def phaseA(x_src, nblk, nsub, kvpass, out_aps, scr, rope_src, is_sample):
            W = nsub * 128
            with ExitStack() as pa:
                xt = [T(pa, "xA%d" % i, [128, nsub, D], F32, nres=nsub) for i in range(2)]
                xn = T(pa, "xnA", [128, nsub, D], BF16, nres=nsub)
                xnTs = [T(pa, "xnT%d" % i, [128, 8, W], BF16) for i in range(2)]
                junk = T(pa, "junkA", [128, D], BF16)
                sts = [T(pa, "stA%d" % i, [128, 16], F32) for i in range(2)]
                rp = T(pa, "ropeA", [128, nsub, 64], F32)
                ptr = [T(pa, "ptrA%d" % i, [128, 1024], BF16, psum=True) for i in range(2)]
                pfm = [T(pa, "pfmA%d" % i, [128, 512], F32, psum=True) for i in range(2)]
                ptm = [T(pa, "ptmA%d" % i, [128, 512], F32, psum=True) for i in range(4)]
                nptm = [0]

                def next_ptm():
                    t_ = ptm[nptm[0] % 4]
                    nptm[0] += 1
                    return t_
                fmb = [T(pa, "fmbA%d" % i, [128, 4, W], BF16) for i in range(2)]
                fmbm = [T(pa, "fmbmA%d" % i, [128, 4, W], BF16) for i in range(2)]
                pending_tail = []
                spT = T(pa, "spT", [8, W], F32)
                SsT = T(pa, "SsT", [8, W], F32)
                eT = T(pa, "eT", [8, W], F32)
                carry = T(pa, "carryA", [8, 1], F32)
                skb = T(pa, "skbA", [8, 3, W], BF16)
                skr = T(pa, "skrA", [8, W], F32)
                if kvpass:
                    stage = [T(pa, "stageA%d" % i, [128, 1024], F32) for i in range(2)]
                    vb = T(pa, "vbA", [128, nsub, 512], BF16, nres=nsub)
                    vmb = T(pa, "vmbA", [128, nsub, 512], BF16)
                    ckvf = [T(pa, "ckvfA%d" % i, [128, 256], F32) for i in range(2)]
                    ckvbs = [T(pa, "ckvbA%d" % i, [128, 256], BF16) for i in range(nsub)]
                    ckvT = T(pa, "ckvTA", [128, 2, W], BF16, nres=nsub)
                    kpef = [T(pa, "kpefA%d" % i, [128, 32], F32) for i in range(2)]
                    kpebs = [T(pa, "kpebA%d" % i, [128, 32], BF16) for i in range(nsub)]
                    kpets = [T(pa, "kpetA%d" % i, [128, 64], F32) for i in range(2)]
                    kpeT = T(pa, "kpeTA", [32, W], BF16, nres=nsub)
                    lst = T(pa, "lstA", [128, nsub, 8], F32)
                else:
                    cqb = T(pa, "cqbA", [8, W], BF16)
                    qnbs = [T(pa, "qnbA%d" % i, [128, 384], BF16) for i in range(nsub)]
                    qnTs = [T(pa, "qnTA%d" % i, [128, 3, 128], BF16) for i in range(2)]
                    qfs = [T(pa, "qfA%d" % i, [128, 768], F32) for i in range(2)]
                    qbs = [T(pa, "qbA%d" % i, [128, 768], BF16) for i in range(nsub)]
                    qt1s = [T(pa, "qt1A%d" % i, [128, 256], F32) for i in range(2)]
                    QmT = T(pa, "QmTA", [96, 8, W], BF16, nres=nsub)
                K.op("dve", lambda e: e.memset(carry[:], 0.0), writes=carry.all)

                nfm = [0]

                def load_x(blk):
                    xb = xt[blk % 2]
                    K.dma("sp", xb[:], x_src[blk * W:(blk + 1) * W, :].rearrange("(s p) d -> p s d", p=128), writes=xb.all)

                def emit_norm(blk):
                    xb = xt[blk % 2]
                    xnT = xnTs[blk % 2]

                    def sub_norm(s):
                        par = s % 2
                        st = sts[par]
                        K.op("act", lambda e: e.activation(out=junk[:], in_=xb[:, s, :], func=AF.Square, accum_out=st[:, 0:1]),
                             reads=[xb.res[s]], writes=st.all)
                        yield
                        rstd_from_ss(st[:, 0:1], D, st[:, 2:3], st[:, 1:2], st.all, st.all)
                        yield
                        K.op("dve", lambda e: e.tensor_scalar(out=xn[:, s, :], in0=xb[:, s, :], scalar1=st[:, 2:3], scalar2=None, op0=ALU.mult),
                             reads=[xb.res[s]] + st.all, writes=[xn.res[s]])
                        yield
                        pt = ptr[par]
                        for kc in range(8):
                            K.op("pe", lambda e, kc=kc: e.transpose(out=pt[:, kc * 128:(kc + 1) * 128], in_=xn[:, s, kc * 128:(kc + 1) * 128], identity=identb[:]),
                                 reads=[xn.res[s]] + identb.all, writes=pt.all)
                        yield
                        K.op("act", lambda e: e.activation(out=xnT[:, :, s * 128:(s + 1) * 128], in_=pt[:].rearrange("p (k n) -> p k n", k=8), func=AF.Copy),
                             reads=pt.all, pwrites=xnT.all)
                    run_pairs([sub_norm(s) for s in range(nsub)])

                load_x(0)
                emit_norm(0)
                for blk in range(nblk):
                    xnT = xnTs[blk % 2]
                    if blk + 1 < nblk:
                        load_x(blk + 1)
                    nr = 64 if kvpass else 32
                    K.dma("sp", rp[:, :, 0:nr], rope_src[blk * W:(blk + 1) * W, :].rearrange("(s p) d -> p s d", p=128), writes=rp.all)
                    if DBG.get("astop") == 1:
                        K.barrier()
                        return
                    cbase = 512 if kvpass else 0
                    fb = fmb[blk % 2]
                    for pr in range(4):
                        pf = pfm[nfm[0] % 2]
                        nfm[0] += 1
                        for kc in range(8):
                            K.op("pe", lambda e, pr=pr, kc=kc, pf=pf: e.matmul(pf[:, 0:W], lhsT=w_in_sb[:, kc, cbase + pr * 128:cbase + (pr + 1) * 128], rhs=xnT[:, kc, :], start=(kc == 0), stop=(kc == 7)),
                                 reads=WA.all + xnT.all, writes=pf.all)
                        if kvpass:
                            K.op("act", lambda e, pr=pr, pf=pf: e.activation(out=fb[:, pr, :], in_=pf[:, 0:W], func=AF.Copy), reads=pf.all, pwrites=fb.all)
                        else:
                            K.op("act", lambda e, pr=pr, pf=pf: e.activation(out=fb[:, pr, :], in_=pf[:, 0:W], func=AF.Copy, scale=SC_F), reads=pf.all, pwrites=fb.all)
                    if not is_sample:
                        dst = (scr["KTf"] if kvpass else scr["QTf"])
                        K.dma("pool", dst.ap[:, :, blk * W:(blk + 1) * W].rearrange("c p n -> p c n"), fb[:], reads=fb.all, pwrites=dst.all)
                    else:
                        if kvpass:
                            for sidx in range(4):
                                k0 = sidx * TS * 128 + PAST
                                K.dma("pool", scr["KTf"].ap[:, :, k0:k0 + 32].rearrange("c p n -> p c n"), fb[:, :, sidx * 32:(sidx + 1) * 32], reads=fb.all, pwrites=scr["KTf"].all)
                        else:
                            K.dma("pool", scr["QTf"].ap.rearrange("c p n -> p c n"), fb[:], reads=fb.all, pwrites=scr["QTf"].all)
                    if blk + 1 < nblk:
                        emit_norm(blk + 1)
                    while pending_tail:
                        pending_tail.pop(0)()
                    if DBG.get("astop") == 2:
                        K.barrier()
                        return
                    if kvpass and not is_sample:
                        conv_step()
                    pf = pfm[nfm[0] % 2]
                    nfm[0] += 1
                    for kc in range(8):
                        K.op("pe", lambda e, kc=kc, pf=pf: e.matmul(pf[0:8, 0:W], lhsT=w_in_sb[:, kc, 1536:1544], rhs=xnT[:, kc, :], start=(kc == 0), stop=(kc == 7)),
                             reads=WA.all + xnT.all, writes=pf.all)
                    K.op("act", lambda e, pf=pf: e.activation(out=eT[:], in_=pf[0:8, 0:W], func=AF.Exp, scale=-1.0, bias=bcol[:, 1:2]), reads=pf.all + bcol.all, writes=eT.all)
                    K.op("act", lambda e: e.activation(out=spT[:], in_=eT[:], func=AF.Ln, bias=1.0), reads=eT.all, writes=spT.all)
                    if is_sample:
                        for sidx in range(4):
                            K.op("dve", lambda e, sidx=sidx: e.tensor_tensor_scan(out=SsT[:, sidx * 32:(sidx + 1) * 32], data0=onesrow[:, 0:32], data1=spT[:, sidx * 32:(sidx + 1) * 32], initial=0.0, op0=ALU.mult, op1=ALU.add),
                                 reads=spT.all + onesrow.all, pwrites=SsT.all)
                    else:
                        if kvpass:
                            K.op("dve", lambda e, blk=blk: e.tensor_copy(out=CB[:, blk:blk + 1], in_=carry[:]), reads=carry.all, pwrites=CB.all)
                            init_ap = carry[:, 0:1]
                            rdi = carry.all
                        else:
                            init_ap = CO[:, blk:blk + 1]
                            rdi = CO.all
                        K.op("dve", lambda e, init_ap=init_ap: e.tensor_tensor_scan(out=SsT[:], data0=onesrow[:, 0:W], data1=spT[:], initial=init_ap, op0=ALU.mult, op1=ALU.add),
                             reads=spT.all + onesrow.all + rdi, writes=SsT.all)
                        if kvpass:
                            K.op("dve", lambda e: e.tensor_copy(out=carry[:], in_=SsT[:, W - 1:W]), reads=SsT.all, writes=carry.all)
                            K.op("dve", lambda e: e.tensor_copy(out=skb[:, 0, :], in_=SsT[:]), reads=SsT.all, writes=skb.all)
                            K.op("dve", lambda e: e.tensor_tensor(out=skr[:], in0=SsT[:], in1=skb[:, 0, :], op=ALU.subtract), reads=SsT.all + skb.all, writes=skr.all)
                            K.op("dve", lambda e: e.tensor_copy(out=skb[:, 1, :], in_=skr[:]), reads=skr.all, pwrites=skb.all)
                            K.op("dve", lambda e: e.tensor_tensor(out=skr[:], in0=skr[:], in1=skb[:, 1, :], op=ALU.subtract), reads=skb.all, writes=skr.all)
                            K.op("dve", lambda e: e.tensor_copy(out=skb[:, 2, :], in_=skr[:]), reads=skr.all, pwrites=skb.all)
                            K.dma("pool", SKf.ap[:, :, blk * W:(blk + 1) * W].rearrange("i h n -> h i n"), skb[:], reads=skb.all, pwrites=SKf.all)
                    if DBG.get("astop") == 3:
                        K.barrier()
                        return
                    if kvpass:
                        def lf_job(blk=blk):
                            psm = pfm[nfm[0] % 2]
                            nfm[0] += 1
                            for s in range(nsub):
                                K.op("pe", lambda e, s=s: e.transpose(out=psm[:, s * 8:(s + 1) * 8], in_=spT[0:8, s * 128:(s + 1) * 128], identity=identf[0:8, 0:8]),
                                     reads=spT.all + identf.all, writes=psm.all)
                                K.op("pe", lambda e, s=s: e.transpose(out=psm[:, 64 + s * 8:64 + (s + 1) * 8], in_=SsT[0:8, s * 128:(s + 1) * 128], identity=identf[0:8, 0:8]),
                                     reads=SsT.all + identf.all, writes=psm.all)
                            K.op("dve", lambda e: e.tensor_scalar(out=lst[:].rearrange("p s h -> p (s h)"), in0=psm[:, 0:nsub * 8], scalar1=-1.0, scalar2=None, op0=ALU.mult),
                                 reads=psm.all, writes=lst.all)
                            K.dma("pool", out_aps["lf"][blk * W:(blk + 1) * W, :].rearrange("(s p) h -> p s h", p=128), lst[:], reads=lst.all)
                            if is_sample:
                                for sidx in range(4):
                                    col = (sidx * TS + 16) * 8
                                    K.op("act", lambda e, sidx=sidx, col=col: e.activation(out=negcs[0:32, col:col + 8], in_=psm[sidx * 32:(sidx + 1) * 32, 64:72], func=AF.Copy),
                                         reads=psm.all, pwrites=negcs.all)
                            else:
                                K.op("dve", lambda e, blk=blk: e.tensor_copy(out=negc[:, blk * 32:(blk + 1) * 32], in_=psm[:, 64:64 + 32]), reads=psm.all, pwrites=negc.all)
                        pending_tail.append(lf_job)
                    else:
                        K.op("dve", lambda e: e.tensor_scalar(out=cqb[:], in0=SsT[:], scalar1=-1.0, scalar2=None, op0=ALU.mult), reads=SsT.all, writes=cqb.all)
                        if is_sample:
                            K.dma("pool", scr["CQ"].ap, cqb[:], reads=cqb.all, pwrites=scr["CQ"].all)
                        else:
                            K.dma("pool", scr["CQ"].ap[:, blk * W:(blk + 1) * W], cqb[:], reads=cqb.all, pwrites=scr["CQ"].all)
                    if DBG.get("astop") == 4:
                        K.barrier()
                        return
                    def sub_kv(s):
                        par = s % 2
                        st = sts[par]
                        r0 = blk * W + s * 128
                        sg = stage[par]
                        banks = []
                        for hf in range(2):
                            pb_ = next_ptm()
                            banks.append(pb_)
                            for kc in range(8):
                                K.op("pe", lambda e, hf=hf, kc=kc, pb_=pb_: e.matmul(pb_[:], lhsT=xnT[:, kc, s * 128:(s + 1) * 128], rhs=w_in_sb[:, kc, 512 + hf * 512:1024 + hf * 512], start=(kc == 0), stop=(kc == 7)),
                                     reads=WA.all + xnT.all, writes=pb_.all)
                        yield
                        K.op("act", lambda e: e.activation(out=sg[:, 0:512], in_=banks[0][:], func=AF.Copy), reads=banks[0].all, writes=sg.all)
                        K.op("act", lambda e: e.activation(out=sg[:, 512:1024], in_=banks[1][:], func=AF.Copy), reads=banks[1].all + sg.all, pwrites=sg.all)
                        K.op("dve", lambda e: e.tensor_copy(out=vb[:, s, :], in_=banks[1][:]), reads=banks[1].all, writes=[vb.res[s]])
                        yield
                        if not is_sample:
                            conv_step()
                        K.dma("pool", out_aps["fk"][r0:r0 + 128, :], sg[:, 0:512], reads=sg.all)
                        K.dma("pool", out_aps["fv"][r0:r0 + 128, :], sg[:, 512:1024], reads=sg.all)
                        p2 = next_ptm()
                        for kc in range(8):
                            K.op("pe", lambda e, kc=kc: e.matmul(p2[:, 0:288], lhsT=xnT[:, kc, s * 128:(s + 1) * 128], rhs=w_in_sb[:, kc, 1928:2216], start=(kc == 0), stop=(kc == 7)),
                                 reads=WA.all + xnT.all, writes=p2.all)
                        yield
                        K.op("act", lambda e: e.activation(out=junk[:, 0:256], in_=p2[:, 0:256], func=AF.Square, accum_out=st[:, 4:5]), reads=p2.all, writes=st.all)
                        yield
                        rstd_from_ss(st[:, 4:5], 256, st[:, 6:7], st[:, 5:6], st.all, st.all)
                        yield
                        cf = ckvf[par]
                        K.op("dve", lambda e: e.scalar_tensor_tensor(out=cf[:], in0=p2[:, 0:256], scalar=st[:, 6:7], in1=gbc[:, 3 * D:3 * D + 256], op0=ALU.mult, op1=ALU.mult),
                             reads=p2.all + st.all + gbc.all, writes=cf.all)
                        kf = kpef[par]
                        kpet = kpets[par]
                        K.op("dve", lambda e: e.tensor_tensor(out=kpet[:, 0:32], in0=p2[:, 256:288], in1=rp[:, s, 0:32], op=ALU.mult), reads=p2.all + rp.all, writes=kpet.all)
                        K.op("dve", lambda e: e.tensor_tensor(out=kpet[:, 32:48], in0=p2[:, 272:288], in1=rp[:, s, 32:48], op=ALU.mult), reads=p2.all + rp.all, pwrites=kpet.all)
                        K.op("dve", lambda e: e.tensor_tensor(out=kpet[:, 48:64], in0=p2[:, 256:272], in1=rp[:, s, 48:64], op=ALU.mult), reads=p2.all + rp.all, pwrites=kpet.all)
                        yield
                        K.dma("pool", out_aps["ckv"][r0:r0 + 128, :], cf[:], reads=cf.all)
                        ckvb = ckvbs[s]
                        kpeb = kpebs[s]
                        K.op("dve", lambda e: e.tensor_copy(out=ckvb[:], in_=cf[:]), reads=cf.all, writes=ckvb.all)
                        K.op("dve", lambda e: e.tensor_tensor(out=kf[:], in0=kpet[:, 0:32], in1=kpet[:, 32:64], op=ALU.add), reads=kpet.all, writes=kf.all)
                        yield
                        K.dma("pool", out_aps["kpe"][r0:r0 + 128, :], kf[:], reads=kf.all)
                        K.op("dve", lambda e: e.tensor_copy(out=kpeb[:], in_=kf[:]), reads=kf.all, writes=kpeb.all)

                        def part2():
                            pt = ptr[par]
                            for kc in range(2):
                                K.op("pe", lambda e, kc=kc: e.transpose(out=pt[:, kc * 128:(kc + 1) * 128], in_=ckvb[:, kc * 128:(kc + 1) * 128], identity=identb[:]),
                                     reads=ckvb.all + identb.all, writes=pt.all)
                            K.op("pe", lambda e: e.transpose(out=pt[0:32, 256:384], in_=kpeb[:, 0:32], identity=identb[:]), reads=kpeb.all + identb.all, writes=pt.all)
                            K.op("act", lambda e: e.activation(out=ckvT[:, :, s * 128:(s + 1) * 128], in_=pt[:, 0:256].rearrange("p (k n) -> p k n", k=2), func=AF.Copy),
                                 reads=pt.all, writes=[ckvT.res[s]])
                            K.op("act", lambda e: e.activation(out=kpeT[:, s * 128:(s + 1) * 128], in_=pt[0:32, 256:384], func=AF.Copy), reads=pt.all, writes=[kpeT.res[s]])
                        pending_tail.append(part2)

                    def q_partA(s):
                        par = s % 2
                        st = sts[par]
                        qnb = qnbs[s]
                        p2 = next_ptm()
                        for kc in range(8):
                            K.op("pe", lambda e, kc=kc: e.matmul(p2[:, 0:384], lhsT=xnT[:, kc, s * 128:(s + 1) * 128], rhs=w_in_sb[:, kc, 1544:1928], start=(kc == 0), stop=(kc == 7)),
                                 reads=WA.all + xnT.all, writes=p2.all)
                        yield
                        K.op("act", lambda e: e.activation(out=junk[:, 0:384], in_=p2[:, 0:384], func=AF.Square, accum_out=st[:, 4:5]), reads=p2.all, writes=st.all)
                        yield
                        rstd_from_ss(st[:, 4:5], 384, st[:, 6:7], st[:, 5:6], st.all, st.all)
                        yield
                        K.op("dve", lambda e: e.tensor_scalar(out=qnb[:], in0=p2[:, 0:384], scalar1=st[:, 6:7], scalar2=None, op0=ALU.mult), reads=p2.all + st.all, writes=qnb.all)

                    def q_partB(s):
                        par = s % 2
                        qnb, qnT, qf, qb, qt1 = qnbs[s], qnTs[par], qfs[par], qbs[s], qt1s[par]
                        pt = ptr[par]
                        for kc in range(3):
                            K.op("pe", lambda e, kc=kc: e.transpose(out=pt[:, kc * 128:(kc + 1) * 128], in_=qnb[:, kc * 128:(kc + 1) * 128], identity=identb[:]),
                                 reads=qnb.all + identb.all, writes=pt.all)
                        yield
                        K.op("act", lambda e: e.activation(out=qnT[:].rearrange("p k n -> p (k n)"), in_=pt[:, 0:384], func=AF.Copy), reads=pt.all, writes=qnT.all)
                        yield
                        banks = []
                        for hf in range(2):
                            pb_ = next_ptm()
                            banks.append(pb_)
                            for kc in range(3):
                                K.op("pe", lambda e, hf=hf, kc=kc, pb_=pb_: e.matmul(pb_[:, 0:384], lhsT=qnT[:, kc, :], rhs=w_q_sb[:, kc, hf * 384:(hf + 1) * 384], start=(kc == 0), stop=(kc == 2)),
                                     reads=WA.all + qnT.all, writes=pb_.all)
                        yield
                        K.op("act", lambda e: e.activation(out=qf[:, 0:384], in_=banks[0][:, 0:384], func=AF.Copy), reads=banks[0].all, writes=qf.all)
                        K.op("act", lambda e: e.activation(out=qf[:, 384:768], in_=banks[1][:, 0:384], func=AF.Copy), reads=banks[1].all + qf.all, pwrites=qf.all)
                        yield
                        qv = qf[:].rearrange("p (h n) -> p h n", h=8)
                        qbv = qb[:].rearrange("p (h n) -> p h n", h=8)
                        cs_ = rp[:, s, 0:16].rearrange("p (o n) -> p o n", o=1).broadcast_to([128, 8, 16])
                        sn_ = rp[:, s, 16:32].rearrange("p (o n) -> p o n", o=1).broadcast_to([128, 8, 16])
                        t1v = qt1[:, 0:128].rearrange("p (h n) -> p h n", h=8)
                        t2v = qt1[:, 128:256].rearrange("p (h n) -> p h n", h=8)
                        K.op("dve", lambda e: e.tensor_scalar(out=qbv[:, :, 0:64], in0=qv[:, :, 0:64], scalar1=SC_M, scalar2=None, op0=ALU.mult), reads=qf.all, writes=qb.all)
                        K.op("dve", lambda e: e.tensor_tensor(out=t1v, in0=qv[:, :, 64:80], in1=cs_, op=ALU.mult), reads=qf.all + rp.all, writes=qt1.all)
                        K.op("dve", lambda e: e.tensor_tensor(out=t2v, in0=qv[:, :, 80:96], in1=sn_, op=ALU.mult), reads=qf.all + rp.all, pwrites=qt1.all)
                        yield
                        K.op("dve", lambda e: e.tensor_tensor(out=qbv[:, :, 64:80], in0=t1v, in1=t2v, op=ALU.subtract), reads=qt1.all, pwrites=qb.all)
                        yield
                        K.op("dve", lambda e: e.tensor_tensor(out=t1v, in0=qv[:, :, 64:80], in1=sn_, op=ALU.mult), reads=qf.all + rp.all + qb.all, writes=qt1.all)
                        K.op("dve", lambda e: e.tensor_tensor(out=t2v, in0=qv[:, :, 80:96], in1=cs_, op=ALU.mult), reads=qf.all + rp.all, pwrites=qt1.all)
                        yield
                        K.op("dve", lambda e: e.tensor_tensor(out=qbv[:, :, 80:96], in0=t1v, in1=t2v, op=ALU.add), reads=qt1.all, pwrites=qb.all)

                    def q_partC(s):
                        par = s % 2
                        qb = qbs[s]
                        pt = ptr[par]
                        for h in range(8):
                            K.op("pe", lambda e, h=h: e.transpose(out=pt[0:96, h * 128:(h + 1) * 128], in_=qb[:, h * 96:(h + 1) * 96], identity=identb[:]),
                                 reads=qb.all + identb.all, writes=pt.all)
                        yield
                        K.op("act", lambda e: e.activation(out=QmT[:, :, s * 128:(s + 1) * 128], in_=pt[0:96, :].rearrange("p (h n) -> p h n", h=8), func=AF.Copy),
                             reads=pt.all, writes=[QmT.res[s]])

                    if kvpass:
                        run_pairs([sub_kv(s) for s in range(nsub)])
                    else:
                        run_pairs([q_partA(s) for s in range(nsub)])
                        run_pairs([q_partB(s) for s in range(nsub)])
                        run_pairs([q_partC(s) for s in range(nsub)])
                    if DBG.get("astop") == 6:
                        K.barrier()
                        return
                    def tail(blk=blk):
                        if kvpass:
                            if not is_sample:
                                K.dma("pool", scr["Vf"].ap[blk * W:(blk + 1) * W, :].rearrange("(s p) c -> p s c", p=128), vb[:], reads=vb.all, pwrites=scr["Vf"].all)
                            else:
                                for sidx in range(4):
                                    K.dma("pool", scr["Vf"].ap[sidx * TS * 128 + PAST:sidx * TS * 128 + PAST + 32, :], vb[sidx * 32:(sidx + 1) * 32, 0, :], reads=vb.all, pwrites=scr["Vf"].all)
                            fb2 = fmbm[blk % 2]
                            for pr in range(4):
                                pf = pfm[nfm[0] % 2]
                                nfm[0] += 1
                                for kc in range(2):
                                    K.op("pe", lambda e, pr=pr, kc=kc, pf=pf: e.matmul(pf[:, 0:W], lhsT=w_kn_sb[:, kc, pr * 128:(pr + 1) * 128], rhs=ckvT[:, kc, :], start=(kc == 0), stop=(kc == 1)),
                                         reads=WA.all + ckvT.all, writes=pf.all)
                                K.op("act", lambda e, pr=pr, pf=pf: e.activation(out=fb2[:, pr, :], in_=pf[:, 0:W], func=AF.Copy), reads=pf.all, pwrites=fb2.all)
                            for s in range(nsub):
                                p2 = next_ptm()
                                for kc in range(2):
                                    K.op("pe", lambda e, s=s, kc=kc, p2=p2: e.matmul(p2[:, 0:512], lhsT=ckvT[:, kc, s * 128:(s + 1) * 128], rhs=w_kv_sb[:, kc, :], start=(kc == 0), stop=(kc == 1)),
                                         reads=WA.all + ckvT.all, writes=p2.all)
                                K.op("dve", lambda e, s=s, p2=p2: e.tensor_copy(out=vmb[:, s, :], in_=p2[:, 0:512]), reads=p2.all, pwrites=vmb.all)
                            if not is_sample:
                                K.dma("pool", scr["KTm"].ap[:, :, blk * W:(blk + 1) * W].rearrange("c p n -> p c n"), fb2[:], reads=fb2.all, pwrites=scr["KTm"].all)
                                K.dma("pool", scr["KPE"].ap[:, blk * W:(blk + 1) * W], kpeT[:], reads=kpeT.all, pwrites=scr["KPE"].all)
                                K.dma("pool", scr["Vm"].ap[blk * W:(blk + 1) * W, :].rearrange("(s p) c -> p s c", p=128), vmb[:], reads=vmb.all, pwrites=scr["Vm"].all)
                            else:
                                for sidx in range(4):
                                    k0 = sidx * TS * 128 + PAST
                                    K.dma("pool", scr["KTm"].ap[:, :, k0:k0 + 32].rearrange("c p n -> p c n"), fb2[:, :, sidx * 32:(sidx + 1) * 32], reads=fb2.all, pwrites=scr["KTm"].all)
                                    K.dma("pool", scr["KPE"].ap[:, k0:k0 + 32], kpeT[:, sidx * 32:(sidx + 1) * 32], reads=kpeT.all, pwrites=scr["KPE"].all)
                                    K.dma("pool", scr["Vm"].ap[k0:k0 + 32, :], vmb[sidx * 32:(sidx + 1) * 32, 0, :], reads=vmb.all, pwrites=scr["Vm"].all)
                        else:
                            if is_sample:
                                K.dma("pool", scr["QTm"].ap.rearrange("h r n -> r h n"), QmT[:], reads=QmT.all, pwrites=scr["QTm"].all)
                            else:
                                K.dma("pool", scr["QTm"].ap[:, :, blk * W:(blk + 1) * W].rearrange("h r n -> r h n"), QmT[:], reads=QmT.all, pwrites=scr["QTm"].all)
                    pending_tail.append(tail)
                    if kvpass and not is_sample:
                        conv_step()
                while pending_tail:
                    pending_tail.pop(0)()

        _phaseA_raw = phaseA

        def phaseA(*a):
            _phaseA_raw(*a)
            K.barrier()

        scrP = dict(KTf=KTf, KTm=KTm, KPE=KPE, Vf=Vf, Vm=Vm, QTf=QTf, QTm=QTm, CQ=CQ)
        scrS = dict(KTf=KTfs, KTm=KTms, KPE=KPEs, Vf=Vfs, Vm=Vms, QTf=QTfs, QTm=QTms, CQ=CQs)
        outP = dict(fk=fk_o, fv=fv_o, lf=lf_o, ckv=ckv_o, kpe=kpe_o)
        outS = dict(fk=fks_o, fv=fvs_o, lf=lfs_o, ckv=ckvs_o, kpe=kpes_o)

        _phaseA_raw(xall, DBG.get("nblk", NBLK), 4, True, outP, scrP, rope_kv, False)
        conv_flush()
        K.barrier()
        p0.close()
        if DBG["stop"] == "A1":
            return _done()
        with ExitStack() as pc:
            tmpc = T(pc, "tmpc", [8, NSLOT * NBLK], F32)
            K.op("dve", lambda e: e.tensor_tensor(out=tmpc[:].rearrange("h (s b) -> h s b", s=NSLOT), in0=selt[:].rearrange("h (s b) -> h s b", s=NSLOT),
                                                  in1=CB[:].rearrange("h (o b) -> h o b", o=1).broadcast_to([8, NSLOT, NBLK]), op=ALU.mult),
                 reads=selt.all + CB.all, writes=tmpc.all)
            K.op("dve", lambda e: e.tensor_reduce(out=CO[:], in_=tmpc[:].rearrange("h (s b) -> h s b", s=NSLOT), axis=mybir.AxisListType.X, op=ALU.add),
                 reads=tmpc.all, writes=CO.all)
            K.barrier()
        phaseA(xown, DBG.get("nslot", NSLOT), 4, False, None, scrP, rope_q, False)
        if DBG["stop"] == "A2":
            return _done()
        phaseA(xs, 1, 1, True, outS, scrS, rope_kv_s, True)
        phaseA(xs, 1, 1, False, None, scrS, rope_q_s, True)
        if DBG["stop"] == "A4":
            return _done()

        with ExitStack() as pS:
            kcb = T(pS, "kcb", [128, 16, 512], BF16)
            ktS = [T(pS, "ktS%d" % i, [128, 2048], BF16) for i in range(2)]
            lfc = T(pS, "lfc", [128, 16 * 8], F32)
            lfs2 = T(pS, "lfs2", [128, 16 * 8], F32)
            ccb = T(pS, "ccb", [128, 16, 256], BF16)
            ckT = T(pS, "ckT", [128, 2, 2048], BF16)
            kpb = T(pS, "kpb", [128, 16, 32], BF16)
            kpf = T(pS, "kpf", [128, 16, 32], F32)
            kpT = T(pS, "kpT", [32, 2048], BF16)
            vsb = T(pS, "vsb", [128, 16, 512], BF16)
            vms = T(pS, "vms", [128, 16, 512], BF16)
            pT_ = [T(pS, "pTS%d" % i, [128, 1024], BF16, psum=True) for i in range(2)]
            pF_ = [T(pS, "pFS%d" % i, [128, 512], F32, psum=True) for i in range(2)]
            pB_ = T(pS, "pBS", [128, 512], F32, psum=True)
            npt = [0]
            npf = [0]
            for sidx in range(4):
                k0 = sidx * TS * 128
                K.dma("pool", kcb[:], ck[sidx].rearrange("(t p) c -> p t c", p=128), writes=kcb.all)
                for pr in range(4):
                    kt_ = ktS[pr % 2]
                    for half in range(2):
                        pt = pT_[npt[0] % 2]
                        npt[0] += 1
                        for j in range(8):
                            t = half * 8 + j
                            K.op("pe", lambda e, t=t, j=j, pr=pr, pt=pt: e.transpose(out=pt[:, j * 128:(j + 1) * 128], in_=kcb[:, t, pr * 128:(pr + 1) * 128], identity=identb[:]),
                                 reads=kcb.all + identb.all, writes=pt.all)
                        K.op("act", lambda e, half=half, pt=pt, kt_=kt_: e.activation(out=kt_[:, half * 1024:(half + 1) * 1024], in_=pt[:], func=AF.Copy), reads=pt.all, pwrites=kt_.all)
                    K.dma("sp", KTfs.ap[pr, :, k0:k0 + PAST], kt_[:], reads=kt_.all, pwrites=KTfs.all)
                if DBG.get("sstop") == 1:
                    break
                K.dma("pool", vsb[:], cv[sidx].rearrange("(t p) c -> p t c", p=128), writes=vsb.all)
                K.dma("sp", Vfs.ap[k0:k0 + PAST, :].rearrange("(t p) c -> p t c", p=128), vsb[:], reads=vsb.all, pwrites=Vfs.all)
                if DBG.get("sstop") == 2:
                    break
                K.dma("sp", lfc[:].rearrange("p (t h) -> p t h", h=8), clf[sidx].rearrange("(t p) h -> p t h", p=128), writes=lfc.all)
                K.op("dve", lambda e: e.memset(lfs2[:, 15 * 8:16 * 8], 0.0), pwrites=lfs2.all)
                for t in range(14, -1, -1):
                    K.op("dve", lambda e, t=t: e.tensor_tensor(out=lfs2[:, t * 8:(t + 1) * 8], in0=lfs2[:, (t + 1) * 8:(t + 2) * 8], in1=lfc[:, (t + 1) * 8:(t + 2) * 8], op=ALU.add),
                         reads=lfc.all + lfs2.all, writes=lfs2.all)
                K.op("pe", lambda e: e.matmul(pB_[:, 0:128], lhsT=suf[:], rhs=lfc[:], start=True, stop=False), reads=suf.all + lfc.all, writes=pB_.all)
                K.op("pe", lambda e: e.matmul(pB_[:, 0:128], lhsT=onesf[:], rhs=lfs2[:], start=False, stop=True), reads=onesf.all + lfs2.all, writes=pB_.all)
                K.op("dve", lambda e, sidx=sidx: e.tensor_copy(out=negcs[:, sidx * TS * 8:(sidx * TS + 16) * 8], in_=pB_[:, 0:128]), reads=pB_.all, pwrites=negcs.all)
                if DBG.get("sstop") == 3:
                    break
                K.dma("pool", ccb[:], cckv[sidx].rearrange("(t p) c -> p t c", p=128), writes=ccb.all)
                for kc in range(2):
                    for half in range(2):
                        pt = pT_[npt[0] % 2]
                        npt[0] += 1
                        for j in range(8):
                            t = half * 8 + j
                            K.op("pe", lambda e, t=t, j=j, kc=kc, pt=pt: e.transpose(out=pt[:, j * 128:(j + 1) * 128], in_=ccb[:, t, kc * 128:(kc + 1) * 128], identity=identb[:]),
                                 reads=ccb.all + identb.all, writes=pt.all)
                        K.op("act", lambda e, kc=kc, half=half, pt=pt: e.activation(out=ckT[:, kc, half * 1024:(half + 1) * 1024], in_=pt[:], func=AF.Copy), reads=pt.all, pwrites=ckT.all)
                if DBG.get("sstop") == 4:
                    break
                K.dma("sp", kpf[:], ckpe[sidx].rearrange("(t p) c -> p t c", p=128), writes=kpf.all)
                K.op("dve", lambda e: e.tensor_copy(out=kpb[:].rearrange("p t c -> p (t c)"), in_=kpf[:].rearrange("p t c -> p (t c)")), reads=kpf.all, writes=kpb.all)
                if DBG.get("sstop") == 9:
                    break
                for half in range(2):
                    pt = pT_[npt[0] % 2]
                    npt[0] += 1
                    for j in range(8):
                        t = half * 8 + j
                        K.op("pe", lambda e, t=t, j=j, pt=pt: e.transpose(out=pt[0:32, j * 128:(j + 1) * 128], in_=kpb[:, t, :], identity=identb[:]),
                             reads=kpb.all + identb.all, writes=pt.all)
                    K.op("act", lambda e, half=half, pt=pt: e.activation(out=kpT[:, half * 1024:(half + 1) * 1024], in_=pt[0:32, :], func=AF.Copy), reads=pt.all, pwrites=kpT.all)
                if DBG.get("sstop") == 8:
                    break
                K.dma("sp", KPEs.ap[:, k0:k0 + PAST], kpT[:], reads=kpT.all, pwrites=KPEs.all)
                if DBG.get("sstop") == 5:
                    break
                for pr in range(4):
                    kt_ = ktS[pr % 2]
                    for cb in range(4):
                        pf = pF_[npf[0] % 2]
                        npf[0] += 1
                        for kc in range(2):
                            K.op("pe", lambda e, pr=pr, cb=cb, kc=kc, pf=pf: e.matmul(pf[:], lhsT=w_kn_sb[:, kc, pr * 128:(pr + 1) * 128], rhs=ckT[:, kc, cb * 512:(cb + 1) * 512], start=(kc == 0), stop=(kc == 1)),
                                 reads=WA.all + ckT.all, writes=pf.all)
                        K.op("act", lambda e, cb=cb, pf=pf, kt_=kt_: e.activation(out=kt_[:, cb * 512:(cb + 1) * 512], in_=pf[:], func=AF.Copy), reads=pf.all, pwrites=kt_.all)
                    K.dma("sp", KTms.ap[pr, :, k0:k0 + PAST], kt_[:], reads=kt_.all, pwrites=KTms.all)
                if DBG.get("sstop") == 6:
                    break
                for t in range(16):
                    pf = pF_[npf[0] % 2]
                    npf[0] += 1
                    for kc in range(2):
                        K.op("pe", lambda e, t=t, kc=kc, pf=pf: e.matmul(pf[:], lhsT=ckT[:, kc, t * 128:(t + 1) * 128], rhs=w_kv_sb[:, kc, :], start=(kc == 0), stop=(kc == 1)),
                             reads=WA.all + ckT.all, writes=pf.all)
                    K.op("dve", lambda e, t=t, pf=pf: e.tensor_copy(out=vms[:, t, :], in_=pf[:]), reads=pf.all, pwrites=vms.all)
                K.dma("sp", Vms.ap[k0:k0 + PAST, :].rearrange("(t p) c -> p t c", p=128), vms[:], reads=vms.all, pwrites=Vms.all)
                if DBG.get("sstop") == 7:
                    break

        K.barrier()
        if DBG["stop"] == "S":
            return _done()
        with ExitStack() as pb:
            NKT = 4 * TS
            KTb = [T(pb, "KTb%d" % i, [96, NKS], BF16) for i in range(2)]
            Vb = [T(pb, "Vb%d" % i, [128, NKT, 65], BF16) for i in range(2)]
            QTb = [T(pb, "QTb%d" % i, [96, NQ], BF16) for i in range(2)]
            mk = T(pb, "mk", [128, 2 * 2 * 8 * 512], BF16)
            PT = [T(pb, "PT%d" % i, [128, 512], BF16) for i in range(6)]
            osb = [T(pb, "osb%d" % i, [128, 4, 64], F32) for i in range(2)]
            rcp = [T(pb, "rcp%d" % i, [128, 4], F32) for i in range(2)]
            pS_ = [T(pb, "pSB%d" % i, [128, 512], F32, psum=True) for i in range(6)]
            pO_raw = [T(pb, "pOB%d" % i, [128, 512], F32, psum=True) for i in range(2)]

            class _PO:
                def __init__(self, t):
                    self.t = t
                    self.all = t.all
                    self.v = t[:, 0:260].rearrange("p (s d) -> p s d", d=65)

                def __getitem__(self, idx):
                    return self.v[idx]
            pO_ = [_PO(t) for t in pO_raw]
            K.dma("sp", mk[:].rearrange("p (a n) -> p a n", n=512), masks_d.rearrange("g r t p n -> p (g r t) n"), writes=mk.all)
            for i in range(2):
                K.op("dve", lambda e, i=i: e.memset(KTb[i][64:65, :], 1.0), pwrites=KTb[i].all)
                K.op("dve", lambda e, i=i: e.memset(QTb[i][64:68, :], 1.0), pwrites=QTb[i].all)
                K.op("dve", lambda e, i=i: e.memset(Vb[i][:, :, 64:65], 1.0), pwrites=Vb[i].all)
            cnt = dict(u=0, o=0, po=0)
            fifo = []
            LAG = 5

            def attend(g, kt_ap, q_ap, W, kn, v_ap, bias_ap, bias_res, mask_ap, first, last, po, nq_sub, c0=0, last_fn=None):
                u = cnt["u"]
                cnt["u"] += 1
                ps = pS_[u % 6]
                pt = PT[u % 6]
                K.op("pe", lambda e: e.matmul(ps[0:kn, c0:W], lhsT=kt_ap, rhs=q_ap[:, c0:W], start=True, stop=(mask_ap is None)),
                     reads=g["kt"].all + g["qt"].all, writes=ps.all)
                if mask_ap is not None:
                    K.op("pe", lambda e: e.matmul(ps[0:kn, c0:W], lhsT=identb[0:kn, 0:kn], rhs=mask_ap[:, c0:W], start=False, stop=True),
                         reads=identb.all + mk.all + smask.all, writes=ps.all)
                if bias_ap is None:
                    K.op("act", lambda e: e.activation(out=pt[0:kn, c0:W], in_=ps[0:kn, c0:W], func=AF.Exp), reads=ps.all, writes=pt.all)
                else:
                    K.op("act", lambda e: e.activation(out=pt[0:kn, c0:W], in_=ps[0:kn, c0:W], func=AF.Exp, bias=bias_ap), reads=ps.all + bias_res, writes=pt.all)

                def stage2():
                    for sb in range(c0 // 128, nq_sub):
                        wq = min(128, W)
                        lst_ = last if last_fn is None else last_fn(sb)
                        K.op("pe", lambda e, sb=sb, lst_=lst_: e.matmul(po[0:wq, sb, :], lhsT=pt[0:kn, sb * 128:sb * 128 + wq], rhs=v_ap, start=(first and sb == 0), stop=lst_),
                             reads=pt.all + g["v"].all, writes=po.all)
                fifo.append(stage2)
                while len(fifo) > LAG:
                    fifo.pop(0)()

            def finish_o(po, wq, nq_sub, dst_ap, dst_dt):
                fifo.append(lambda: finish_o_now(po, wq, nq_sub, dst_ap, dst_dt))
                while len(fifo) > LAG:
                    fifo.pop(0)()

            def finish_o_now(po, wq, nq_sub, dst_ap, dst_dt):
                o = cnt["o"]
                cnt["o"] += 1
                rc = rcp[o % 2]
                ob = osb[o % 2]
                K.op("dve", lambda e: e.reciprocal(out=rc[0:wq, 0:nq_sub], in_=po[0:wq, 0:nq_sub, 64]), reads=po.all, writes=rc.all)
                K.op("dve", lambda e: e.tensor_tensor(out=ob[0:wq, 0:nq_sub, :], in0=po[0:wq, 0:nq_sub, 0:64],
                                                      in1=rc[0:wq, 0:nq_sub].rearrange("p (s o) -> p s o", o=1).broadcast_to([wq, nq_sub, 64]), op=ALU.mult),
                     reads=po.all + rc.all, writes=ob.all)
                K.dma("sp", dst_ap, ob[0:wq, 0:nq_sub, :], reads=ob.all, pwrites=dst_dt.all)

            def load_head(hh, sample):
                i = hh % 2
                kt, vt, qt = KTb[i], Vb[i], QTb[i]
                S = scrS if sample else scrP
                nk = NKS if sample else SEQ
                nkt = NKT if sample else 64
                nq = 128 if sample else NQ
                if hh < 8:
                    pr, two = hh // 2, hh % 2
                    K.dma("sp", kt[0:64, 0:nk], S["KTf"].ap[pr, two * 64:(two + 1) * 64, :], reads=S["KTf"].all, pwrites=kt.all)
                    K.dma("sp", vt[:, 0:nkt, 0:64], S["Vf"].ap[:, hh * 64:(hh + 1) * 64].rearrange("(t p) d -> p t d", p=128), reads=S["Vf"].all, pwrites=vt.all)
                    K.dma("sp", qt[0:64, 0:nq], S["QTf"].ap[pr, two * 64:(two + 1) * 64, :], reads=S["QTf"].all, pwrites=qt.all)
                    K.dma("sp", qt[64:65, 0:nq], S["CQ"].ap[hh:hh + 1, :], reads=S["CQ"].all, pwrites=qt.all)
                    if not sample:
                        K.dma("sp", kt[65:68, 0:nk], SKf.ap[:, hh, :], reads=SKf.all, pwrites=kt.all)
                else:
                    h = hh - 8
                    pr, two = h // 2, h % 2
                    K.dma("sp", kt[0:64, 0:nk], S["KTm"].ap[pr, two * 64:(two + 1) * 64, :], reads=S["KTm"].all, pwrites=kt.all)
                    K.dma("sp", kt[64:96, 0:nk], S["KPE"].ap, reads=S["KPE"].all, pwrites=kt.all)
                    K.dma("sp", vt[:, 0:nkt, 0:64], S["Vm"].ap[:, h * 64:(h + 1) * 64].rearrange("(t p) d -> p t d", p=128), reads=S["Vm"].all, pwrites=vt.all)
                    K.dma("sp", qt[0:96, 0:nq], S["QTm"].ap[h], reads=S["QTm"].all, pwrites=qt.all)
                return dict(kt=kt, v=vt, qt=qt)

            seq = [(hh, False) for hh in range(8)] + [(hh, True) for hh in range(8)] + [(hh, True) for hh in range(8, 16)] + [(hh, False) for hh in range(8, 16)]
            loaded = {}
            loaded[0] = load_head(*seq[0])
            for n, (hh, sample) in enumerate(seq):
                while fifo:
                    fifo.pop(0)()
                if n + 1 < len(seq):
                    loaded[n + 1] = load_head(*seq[n + 1])
                g = loaded.pop(n)
                fox = hh < 8
                R = 65 if fox else 96
                h = hh if fox else hh - 8
                gtype = 0 if fox else 1
                ocol = hh * 64
                if not sample:
                    if fox:
                        R = 68
                    for slot in range(NSLOT):
                        po = pO_[cnt["po"] % 2]
                        cnt["po"] += 1
                        nkt = 8 * (slot + 1)
                        for kt_i in range(nkt):
                            mask_ap = None
                            if kt_i >= 8 * slot:
                                mi = ((gtype * 2 + slot % 2) * 8 + (kt_i - 8 * slot)) * 512
                                mask_ap = mk[:, mi:mi + 512]
                            bias_ap = None
                            jd = kt_i - (8 * slot + 4)
                            c0 = 128 * jd if jd > 0 else 0
                            attend(g, g["kt"][0:R, kt_i * 128:(kt_i + 1) * 128], g["qt"][0:R, slot * 512:(slot + 1) * 512], 512, 128,
                                   g["v"][:, kt_i, :], bias_ap, negc.all, mask_ap, kt_i == 0, kt_i == nkt - 1, po, 4,
                                   c0=c0, last_fn=(lambda sb, kt_i=kt_i, slot=slot: kt_i == 8 * slot + 4 + sb))
                        finish_o(po, 128, 4, OS.ap[slot * 512:(slot + 1) * 512, ocol:ocol + 64].rearrange("(s p) d -> p s d", p=128), OS)
                else:
                    for sidx in range(4):
                        po = pO_[cnt["po"] % 2]
                        cnt["po"] += 1
                        for t in range(TS):
                            kn = 128 if t < 16 else 32
                            kt_i = sidx * TS + t
                            mask_ap = smask[0:32, :] if (fox and t == 16) else None
                            bias_ap = negcs[0:kn, kt_i * 8 + h:kt_i * 8 + h + 1] if fox else None
                            attend(g, g["kt"][0:R, kt_i * 128:kt_i * 128 + kn], g["qt"][0:R, sidx * 32:(sidx + 1) * 32], 32, kn,
                                   g["v"][0:kn, kt_i, :], bias_ap, negcs.all, mask_ap, t == 0, t == TS - 1, po, 1)
                        finish_o(po, 32, 1, OSs.ap[sidx * 32:(sidx + 1) * 32, ocol:ocol + 64].rearrange("(s p) d -> p s d", p=32), OSs)
            while fifo:
                fifo.pop(0)()

        K.barrier()
        if DBG["stop"] == "B":
            return _done()
        K.dma("sp", w_out_sb, WOUT.ap, reads=WOUT.all + WA.all, writes=WA.all)
        K.dma("sp", w_gate_sb, WGATE.ap, reads=WGATE.all, pwrites=WA.all)
        K.dma("sp", w_proj_sb, WPROJ.ap, reads=WPROJ.all, pwrites=WA.all)

        def run_all(gens):
            live = list(gens)
            while live:
                for g_ in list(live):
                    try:
                        next(g_)
                    except StopIteration:
                        live.remove(g_)

        class View:
            def __init__(self, ap, res):
                self.ap_ = ap
                self.all = res

            def __getitem__(self, idx):
                return self.ap_[idx]

        def phaseC(x_src, p_src, o_src, y_dst, nblk, nsub):
            W = nsub * 128
            with ExitStack() as pc:
                xb = T(pc, "xC", [128, nsub, D], F32, nres=nsub)
                ob = T(pc, "oC", [128, nsub * D], F32)
                pb_ = T(pc, "pC", [128, nsub, 256], F32)
                abs_ = [T(pc, "aC%d" % i, [128, D], BF16) for i in range(nsub)]
                aT = T(pc, "aTC", [128, 8, W], BF16, nres=nsub)
                pT2 = T(pc, "pT2C", [128, 2, W], BF16, nres=nsub)
                u2T = T(pc, "u2TC", [128, 32, W], BF16)
                rl = [T(pc, "rlC%d" % i, [128, W], F32) for i in range(2)]
                tmp_real = [T(pc, "tmpC%d" % i, [128, D], F32) for i in range(2)]
                junk = T(pc, "junkC", [128, D], BF16)
                sts = [T(pc, "stC%d" % i, [128, 16], F32) for i in range(nsub)]
                wus = [T(pc, "wusC%d" % i, [128, 8, 512], BF16) for i in range(2)]
                wds = [T(pc, "wdsC%d" % i, [128, 32, 128], BF16) for i in range(2)]
                pq = [T(pc, "pqC%d" % i, [128, 1024], F32, psum=True, nres=2) for i in range(4)]
                tmp = [tmp_real[0], tmp_real[1]]
                for i in range(2):
                    tmp.append(View(wds[i][:].rearrange("p c w -> p (c w)").bitcast(F32)[:, 0:D], wds[i].all))
                ptrv = [View(pq[i][:, 0:512].bitcast(BF16), [pq[i].res[0]]) for i in range(4)]
                pfm = [View(pq[i][:, 0:512], [pq[i].res[0]]) for i in range(2)]
                c = dict(fm=0)
                u2flat = u2T[:].rearrange("p c w -> p (c w)").bitcast(F32)
                ostg = u2flat[:, 0:nsub * D]
                xstg = u2flat[:, nsub * D:2 * nsub * D].rearrange("p (s d) -> p s d", s=nsub)

                def prefetch(blk):
                    K.dma("sp", ostg.rearrange("p (s d) -> p s d", s=nsub), o_src.ap[blk * W:(blk + 1) * W, :].rearrange("(s p) d -> p s d", p=128), reads=o_src.all, writes=u2T.all)
                    K.dma("sp", xstg, x_src[blk * W:(blk + 1) * W, :].rearrange("(s p) d -> p s d", p=128), pwrites=u2T.all)

                def transposes(src_tile, ncol_chunks, dstT, s):
                    pt = ptrv[s]
                    for kc in range(ncol_chunks):
                        K.op("pe", lambda e, kc=kc: e.transpose(out=pt[:, kc * 128:(kc + 1) * 128], in_=src_tile[:, kc * 128:(kc + 1) * 128], identity=identb[:]),
                             reads=src_tile.all + identb.all, writes=pt.all)
                    yield
                    K.op("act", lambda e: e.activation(out=dstT[:, :, s * 128:(s + 1) * 128], in_=pt[:, 0:ncol_chunks * 128].rearrange("p (k n) -> p k n", k=ncol_chunks), func=AF.Copy),
                         reads=pt.all, writes=[dstT.res[s]])
                    yield

                def post_norm_residual(pk_ap, pk_res, s, gi, xin=None, xin_res=None):
                    st = sts[s]
                    K.op("act", lambda e: e.activation(out=junk[:], in_=pk_ap, func=AF.Square, accum_out=st[:, 0:1]), reads=pk_res, writes=st.all)
                    yield
                    rstd_from_ss(st[:, 0:1], D, st[:, 2:3], st[:, 1:2], st.all, st.all)
                    yield
                    tm = tmp[s]
                    K.op("dve", lambda e: e.scalar_tensor_tensor(out=tm[:], in0=pk_ap, scalar=st[:, 2:3], in1=gbc[:, gi * D:(gi + 1) * D], op0=ALU.mult, op1=ALU.mult),
                         reads=pk_res + st.all + gbc.all, writes=tm.all)
                    yield
                    if xin is None:
                        K.op("dve", lambda e: e.tensor_tensor(out=xb[:, s, :], in0=xb[:, s, :], in1=tm[:], op=ALU.add), reads=tm.all, writes=[xb.res[s]])
                    else:
                        K.op("dve", lambda e: e.tensor_tensor(out=xb[:, s, :], in0=xin, in1=tm[:], op=ALU.add), reads=tm.all + xin_res, writes=[xb.res[s]])
                    yield

                def pre_norm_T(s, dstT):
                    st = sts[s]
                    ab = abs_[s]
                    K.op("act", lambda e: e.activation(out=junk[:], in_=xb[:, s, :], func=AF.Square, accum_out=st[:, 4:5]), reads=[xb.res[s]], writes=st.all)
                    yield
                    rstd_from_ss(st[:, 4:5], D, st[:, 6:7], st[:, 5:6], st.all, st.all)
                    yield
                    K.op("dve", lambda e: e.tensor_scalar(out=ab[:], in0=xb[:, s, :], scalar1=st[:, 6:7], scalar2=None, op0=ALU.mult), reads=[xb.res[s]] + st.all, writes=ab.all)
                    yield
                    yield from transposes(ab, 8, dstT, s)

                def front(s):
                    st = sts[s]
                    ab = abs_[s]
                    for hf in range(2):
                        K.op("act", lambda e, hf=hf: e.activation(out=junk[:, 0:512], in_=ostg[:, s * D + hf * 512:s * D + (hf + 1) * 512], func=AF.Square, accum_out=st[:, 8 + hf:9 + hf]),
                             reads=u2T.all, writes=st.all)
                    yield
                    rstd_from_ss(st[:, 8:10], 512, st[:, 12:14], st[:, 10:12], st.all, st.all)
                    yield
                    for hf in range(2):
                        K.op("dve", lambda e, hf=hf: e.tensor_scalar(out=ab[:, hf * 512:(hf + 1) * 512], in0=ostg[:, s * D + hf * 512:s * D + (hf + 1) * 512], scalar1=st[:, 12 + hf:13 + hf], scalar2=None, op0=ALU.mult),
                             reads=u2T.all + st.all, writes=ab.all)
                    yield
                    yield from transposes(ab, 8, aT, s)
                    pk = pq[s]
                    for hf in range(2):
                        for kc in range(8):
                            K.op("pe", lambda e, hf=hf, kc=kc: e.matmul(pk[:, hf * 512:(hf + 1) * 512], lhsT=aT[:, kc, s * 128:(s + 1) * 128], rhs=w_out_sb[:, kc, hf * 512:(hf + 1) * 512], start=(kc == 0), stop=(kc == 7)),
                                 reads=[aT.res[s]] + WA.all, writes=[pk.res[hf]])
                    yield
                    yield from post_norm_residual(pk[:], pk.all, s, 0, xin=xstg[:, s, :], xin_res=u2T.all)
                    yield from pre_norm_T(s, aT)

                def back(s, blk, yT):
                    ab = abs_[s]
                    pk = pq[s]
                    for n_ in range(8):
                        K.op("pe", lambda e, n_=n_: e.transpose(out=pk[:, n_ * 128:(n_ + 1) * 128], in_=yT[:, n_, s * 128:(s + 1) * 128], identity=identf[:]),
                             reads=ob.all + identf.all, writes=[pk.res[n_ // 4]])
                    yield
                    yield from post_norm_residual(pk[:], pk.all, s, 1)
                    yield from pre_norm_T(s, aT)
                    K.op("pool", lambda e: e.tensor_copy(out=ab[:, 0:256], in_=pb_[:, s, :]), reads=pb_.all, writes=ab.all)
                    yield
                    yield from transposes(ab, 2, pT2, s)
                    for hf in range(2):
                        for kc in range(8):
                            K.op("pe", lambda e, hf=hf, kc=kc: e.matmul(pk[:, hf * 512:(hf + 1) * 512], lhsT=aT[:, kc, s * 128:(s + 1) * 128], rhs=w_gate_sb[:, kc, hf * 512:(hf + 1) * 512], start=(kc == 0), stop=(kc == 7)),
                                 reads=[aT.res[s]] + WA.all, writes=[pk.res[hf]])
                    yield
                    tg = tmp[s]
                    K.op("act", lambda e: e.activation(out=tg[:], in_=pk[:], func=AF.Exp, scale=-1.0), reads=pk.all, writes=tg.all)
                    yield
                    K.op("act", lambda e: e.activation(out=tg[:], in_=tg[:], func=AF.Ln, bias=1.0), reads=tg.all, writes=tg.all)
                    yield
                    K.op("act", lambda e: e.activation(out=tg[:], in_=tg[:], func=AF.Exp, scale=-1.0), reads=tg.all, writes=tg.all)
                    for hf in range(2):
                        for kc in range(2):
                            K.op("pe", lambda e, hf=hf, kc=kc: e.matmul(pk[:, hf * 512:(hf + 1) * 512], lhsT=pT2[:, kc, s * 128:(s + 1) * 128], rhs=w_proj_sb[:, kc, hf * 512:(hf + 1) * 512], start=(kc == 0), stop=(kc == 1)),
                                 reads=[pT2.res[s]] + WA.all, writes=[pk.res[hf]])
                    yield
                    K.op("dve", lambda e: e.tensor_tensor(out=tg[:], in0=pk[:], in1=tg[:], op=ALU.mult), reads=pk.all + tg.all, writes=tg.all)
                    yield
                    yield from post_norm_residual(tg[:], tg.all, s, 2)
                    K.dma("sp", y_dst[blk * W + s * 128:blk * W + (s + 1) * 128, :], xb[:, s, :], reads=[xb.res[s]])

                prefetch(0)
                for blk in range(nblk):
                    K.dma("sp", pb_[:], p_src[blk * W:(blk + 1) * W, :].rearrange("(s p) d -> p s d", p=128), writes=pb_.all)
                    run_all([front(s) for s in range(nsub)])
                    for cc in range(8):
                        wu = wus[cc % 2]
                        K.dma("sp", wu[:], WUP.ap[cc], reads=WUP.all, writes=wu.all)
                        for j in range(4):
                            ch = cc * 4 + j
                            pf = pfm[c["fm"] % 2]
                            c["fm"] += 1
                            for kc in range(8):
                                K.op("pe", lambda e, j=j, kc=kc: e.matmul(pf[:, 0:W], lhsT=wu[:, kc, j * 128:(j + 1) * 128], rhs=aT[:, kc, :], start=(kc == 0), stop=(kc == 7)),
                                     reads=wu.all + aT.all, writes=pf.all)
                            r_ = rl[ch % 2]
                            K.op("act", lambda e: e.activation(out=r_[:], in_=pf[:, 0:W], func=AF.Relu), reads=pf.all, writes=r_.all)
                            K.op("pool", lambda e, ch=ch: e.tensor_tensor(out=u2T[:, ch, :], in0=r_[:], in1=r_[:], op=ALU.mult), reads=r_.all, pwrites=u2T.all)
                    yT = ob[:].rearrange("p (n t) -> p n t", n=8)
                    for n_ in range(8):
                        wd = wds[n_ % 2]
                        K.dma("sp", wd[:], WDN.ap[n_], reads=WDN.all, writes=wd.all)
                        pf = pfm[c["fm"] % 2]
                        c["fm"] += 1
                        for kc in range(32):
                            K.op("pe", lambda e, kc=kc: e.matmul(pf[:, 0:W], lhsT=wd[:, kc, :], rhs=u2T[:, kc, :], start=(kc == 0), stop=(kc == 31)),
                                 reads=wd.all + u2T.all, writes=pf.all)
                        K.op("act", lambda e, n_=n_: e.activation(out=yT[:, n_, :], in_=pf[:, 0:W], func=AF.Copy), reads=pf.all, pwrites=ob.all)
                    if blk + 1 < nblk:
                        prefetch(blk + 1)
                    run_all([back(s, blk, yT) for s in range(nsub)])

        phaseC(xs, ps_in, OSs, ys_o, 1, 1)
        K.barrier()
        phaseC(xown, pown, OS, y_own, DBG.get("nslot", NSLOT), 4)
        K.finish()
        global _LASTK
        _LASTK = K
    return nc


_NC = None


def _own_blocks(r):
    out = []
    for i in range(NSLOT):
        odd = (i % 2 == 1)
        if r == 0:
            out.append(2 * i + (1 if odd else 0))
        else:
            out.append(2 * i + (0 if odd else 1))
    return out


def _consts():
    half = 16
    inv = (10000.0 ** (-np.arange(half, dtype=np.float32) / half)).astype(np.float32)

    def tables(pos):
        ang = pos.astype(np.float32)[:, None] * inv[None, :]
        return np.cos(ang).astype(np.float32), np.sin(ang).astype(np.float32)

    return tables


def _masks():
    k = np.arange(128)[:, None]
    q = np.arange(512)[None, :]
    out = np.zeros((2, 2, 8, 128, 512), np.float32)
    for g in range(2):
        for kind in range(2):
            for t in range(8):
                if kind == 0:
                    if t < 4:
                        m = np.zeros((128, 512), bool)
                    else:
                        kk = (t - 4) * 128 + k
                        m = (kk > q) if g == 0 else ((kk // 64) > (q // 64))
                else:
                    if t < 4:
                        kk = t * 128 + k
                        m = (kk > q) if g == 0 else ((kk // 64) > (q // 64))
                    else:
                        m = np.ones((128, 512), bool)
                out[g, kind, t] = np.where(m, NEG, 0.0)
    return out


def _prep(inp):
    f32 = np.float32
    bf = ml_dtypes.bfloat16
    tables = _consts()
    x_prompt = np.asarray(inp["x_prompt"], f32)
    x_sample = np.asarray(inp["x_sample"], f32)
    p_prompt = np.asarray(inp["p_prompt"], f32)[0]
    p_sample = np.asarray(inp["p_sample"], f32)[0]
    ident = np.eye(128, dtype=f32)
    su = (np.arange(128)[:, None] > np.arange(128)[None, :]).astype(f32)
    cos_all, sin_all = tables(np.arange(SEQ))
    rope_kv = np.concatenate([cos_all, cos_all, -sin_all, sin_all], axis=1).astype(f32)
    pos_s = PAST + (np.arange(128) % 32)
    cos_s, sin_s = tables(pos_s)
    rope_kv_s = np.concatenate([cos_s, cos_s, -sin_s, sin_s], axis=1).astype(f32)
    rope_q_s = np.concatenate([cos_s, sin_s], axis=1).astype(f32)
    masks_all = _masks()
    smask = np.zeros((128, 32), f32)
    smask[:32] = np.where(np.arange(32)[:, None] > np.arange(32)[None, :], NEG, 0.0)
    shared = dict(
        w_in=np.ascontiguousarray(inp["w_in"][0], f32), w_qup=np.ascontiguousarray(inp["w_mla_q_up"][0], f32),
        w_kvup=np.ascontiguousarray(inp["w_mla_kv_up"][0], f32), w_out=np.ascontiguousarray(inp["w_out"][0], f32),
        w_up=np.ascontiguousarray(inp["w_up"][0], f32), w_down=np.ascontiguousarray(inp["w_down"][0], f32),
        w_gate=np.ascontiguousarray(inp["w_ple_gate"][0], f32), w_proj=np.ascontiguousarray(inp["w_ple_proj"][0], f32),
        g_pre_mix=np.ascontiguousarray(inp["g_pre_mix"][0], f32), g_mla_q=np.ascontiguousarray(inp["g_mla_q"][0], f32),
        g_mla_kv=np.ascontiguousarray(inp["g_mla_kv"][0], f32),
        g_out=np.concatenate([inp["g_fox_out"][0], inp["g_mla_out"][0]]).astype(f32),
        g_post_mix=np.ascontiguousarray(inp["g_post_mix"][0], f32), g_pre_mlp=np.ascontiguousarray(inp["g_pre_mlp"][0], f32),
        g_post_mlp=np.ascontiguousarray(inp["g_post_mlp"][0], f32), g_ple=np.ascontiguousarray(inp["g_ple"][0], f32),
        g_ple_post=np.ascontiguousarray(inp["g_ple_post"][0], f32), b_f=np.ascontiguousarray(inp["b_fox_f"][0], f32),
        ident=ident, su=su, rope_kv=rope_kv, rope_kv_s=rope_kv_s, rope_q_s=rope_q_s, smask=smask.astype(bf),
    )
    in_maps = []
    owns = []
    for c in range(8):
        b, r = c // 2, c % 2
        own = _own_blocks(r)
        owns.append(own)
        rows = np.concatenate([np.arange(o * 512, (o + 1) * 512) for o in own])
        cq, sq = cos_all[rows], sin_all[rows]
        sel = np.zeros((NSLOT, NBLK), f32)
        for i, o in enumerate(own):
            sel[i, o] = 1.0
        mk = np.zeros((2, 2, 8, 128, 512), f32)
        for par in range(2):
            kind = 0 if own[par] == 2 * par + 1 else 1
            mk[:, par] = masks_all[:, kind]
        m = dict(shared)
        m.update(
            xall=np.ascontiguousarray(x_prompt[b]), xown=np.ascontiguousarray(x_prompt[b][rows]),
            pown=np.ascontiguousarray(p_prompt[b][rows]),
            xs=np.ascontiguousarray(x_sample[4 * c:4 * c + 4].reshape(128, D)),
            ps=np.ascontiguousarray(p_sample[4 * c:4 * c + 4].reshape(128, 256)),
            ck=np.ascontiguousarray(inp["cache_fox_k"][0, 4 * c:4 * c + 4].reshape(4, PAST, 512), f32),
            cv=np.ascontiguousarray(inp["cache_fox_v"][0, 4 * c:4 * c + 4].reshape(4, PAST, 512), f32),
            clf=np.ascontiguousarray(inp["cache_fox_logf"][0, 4 * c:4 * c + 4], f32),
            cckv=np.ascontiguousarray(inp["cache_mla_ckv"][0, 4 * c:4 * c + 4], f32),
            ckpe=np.ascontiguousarray(inp["cache_mla_kpe"][0, 4 * c:4 * c + 4], f32),
            rope_q=np.concatenate([cq * SC_M, sq * SC_M], axis=1).astype(f32),
            masks=mk.astype(bf),
            sel=np.ascontiguousarray(np.broadcast_to(sel.reshape(1, -1), (8, NSLOT * NBLK)), f32),
        )
        m["rope_q_s"] = (rope_q_s * SC_M).astype(f32)
        in_maps.append(m)
    return in_maps, owns


def kernel(**inp):
    global _NC
    if _NC is None:
        _NC = build()
    nc = _NC
    f32 = np.float32
    in_maps, owns = _prep(inp)
    res = run_bass_kernel_spmd(nc, in_maps, core_ids=list(range(8)))
    R = res.results
    y_p = np.zeros((4, SEQ, D), f32)
    for c in range(8):
        b = c // 2
        for i, o in enumerate(owns[c]):
            y_p[b, o * 512:(o + 1) * 512] = R[c]["y_own"][i * 512:(i + 1) * 512]
    y_s = np.concatenate([R[c]["ys_o"].reshape(4, 32, D) for c in range(8)], axis=0)
    fk_p = np.stack([R[2 * b]["fk_o"] for b in range(4)]).reshape(1, 4, SEQ, 8, 64)
    fv_p = np.stack([R[2 * b]["fv_o"] for b in range(4)]).reshape(1, 4, SEQ, 8, 64)
    lf_p = np.stack([R[2 * b]["lf_o"] for b in range(4)]).reshape(1, 4, SEQ, 8)
    ckv_p = np.stack([R[2 * b]["ckv_o"] for b in range(4)]).reshape(1, 4, SEQ, 256)
    kpe_p = np.stack([R[2 * b]["kpe_o"] for b in range(4)]).reshape(1, 4, SEQ, 32)
    fk_s = np.concatenate([R[c]["fks_o"].reshape(4, 32, 8, 64) for c in range(8)], axis=0)[None]
    fv_s = np.concatenate([R[c]["fvs_o"].reshape(4, 32, 8, 64) for c in range(8)], axis=0)[None]
    lf_s = np.concatenate([R[c]["lfs_o"].reshape(4, 32, 8) for c in range(8)], axis=0)[None]
    ckv_s = np.concatenate([R[c]["ckvs_o"].reshape(4, 32, 256) for c in range(8)], axis=0)[None]
    kpe_s = np.concatenate([R[c]["kpes_o"].reshape(4, 32, 32) for c in range(8)], axis=0)[None]
    return (y_p, y_s, fk_p, fv_p, lf_p, ckv_p, kpe_p, fk_s, fv_s, lf_s, ckv_s, kpe_s)
```

```python
from contextlib import ExitStack
import math
import numpy as np
import ml_dtypes
import concourse.bass as bass
import concourse.mybir as mybir
from concourse.bass_utils import run_bass_kernel_spmd

F32 = mybir.dt.float32
BF16 = mybir.dt.bfloat16
AF = mybir.ActivationFunctionType
ALU = mybir.AluOpType

NDMA_SEMS = 8
EPS = 1e-6
NEG = -30000.0


class Res:
    __slots__ = ("name", "w", "r", "excl")

    def __init__(self, name):
        self.name = name
        self.w = []
        self.r = []
        self.excl = False


class Eng:
    def __init__(self, name, handle):
        self.name = name
        self.h = handle
        self.n = 0
        self.sem = None
        self.seen = {}
        self.ndma = 0
        self.dsems = []


class Kern:
    def __init__(self, nc, stack):
        self.nc = nc
        self.stack = stack
        self.engs = {}
        for name, h in (("pe", nc.tensor), ("act", nc.scalar), ("dve", nc.vector),
                        ("pool", nc.gpsimd), ("sp", nc.sync)):
            e = Eng(name, h)
            e.sem = stack.enter_context(nc.semaphore("s_" + name))
            self.engs[name] = e
        for qn in ("sp", "pool"):
            e = self.engs[qn]
            e.dsems = [stack.enter_context(nc.semaphore("d_%s%d" % (qn, i))) for i in range(NDMA_SEMS)]

    def _wait(self, eng, tok):
        sem, val, src, idx = tok
        key = id(sem)
        if eng.seen.get(key, 0) >= val:
            return
        if src is eng and idx is not None:
            if eng.name == "pe" or idx < eng.n - 2:
                return
        eng.h.wait_ge(sem, val)
        eng.seen[key] = val

    def _deps(self, eng, reads, writes, pwrites):
        for r in reads:
            for t in r.w:
                self._wait(eng, t)
        for r in writes:
            for t in r.w:
                self._wait(eng, t)
            for t in r.r:
                self._wait(eng, t)
        for r in pwrites:
            for t in r.r:
                self._wait(eng, t)

    def _commit(self, tok, reads, writes, pwrites):
        for r in reads:
            if tok[3] is not None:
                r.r = [t for t in r.r if not (t[2] is tok[2] and t[3] is not None)]
            r.r.append(tok)
        for r in writes:
            r.w = [tok]
            r.r = []
        for r in pwrites:
            if r.r:
                r.w = [tok]
                r.r = []
            else:
                r.w.append(tok)

    def op(self, engname, fn, reads=(), writes=(), pwrites=()):
        eng = self.engs[engname]
        if any(r.excl for r in reads):
            writes = list(writes) + [r for r in reads if r.excl and r not in writes]
            reads = [r for r in reads if not r.excl]
        self._deps(eng, reads, writes, pwrites)
        ins = fn(eng.h)
        ins.then_inc(eng.sem, 1)
        idx = eng.n
        eng.n += 1
        tok = (eng.sem, idx + 1, eng, idx)
        self._commit(tok, reads, writes, pwrites)
        return tok

    def dma(self, qname, out, in_, reads=(), writes=(), pwrites=()):
        eng = self.engs[qname]
        i = eng.ndma
        sem = eng.dsems[i % NDMA_SEMS]
        prev = 16 * (i // NDMA_SEMS)
        if prev > 0 and eng.seen.get(id(sem), 0) < prev:
            eng.h.wait_ge(sem, prev)
            eng.seen[id(sem)] = prev
        self._deps(eng, reads, writes, pwrites)
        ins = eng.h.dma_start(out=out, in_=in_)
        ins.then_inc(sem, 16)
        eng.ndma += 1
        tok = (sem, 16 * (i // NDMA_SEMS + 1), eng, None)
        self._commit(tok, reads, writes, pwrites)
        return tok

    def barrier(self):
        for e in self.engs.values():
            for f in self.engs.values():
                if f is not e and f.n > 0 and e.seen.get(id(f.sem), 0) < f.n:
                    e.h.wait_ge(f.sem, f.n)
                    e.seen[id(f.sem)] = f.n
            for qn in ("sp", "pool"):
                q = self.engs[qn]
                for j in range(min(q.ndma, NDMA_SEMS)):
                    last = ((q.ndma - 1 - j) // NDMA_SEMS) * NDMA_SEMS + j
                    val = 16 * (last // NDMA_SEMS + 1)
                    sem = q.dsems[j]
                    if e.seen.get(id(sem), 0) < val:
                        e.h.wait_ge(sem, val)
                        e.seen[id(sem)] = val

    def finish(self):
        sp = self.engs["sp"]
        for qn in ("sp", "pool"):
            e = self.engs[qn]
            for j in range(min(e.ndma, NDMA_SEMS)):
                last = ((e.ndma - 1 - j) // NDMA_SEMS) * NDMA_SEMS + j
                val = 16 * (last // NDMA_SEMS + 1)
                sem = e.dsems[j]
                if sp.seen.get(id(sem), 0) < val:
                    sp.h.wait_ge(sem, val)
                    sp.seen[id(sem)] = val
        for name in ("pe", "act", "dve", "pool"):
            e = self.engs[name]
            if e.n > 0 and sp.seen.get(id(e.sem), 0) < e.n:
                sp.h.wait_ge(e.sem, e.n)
                sp.seen[id(e.sem)] = e.n


class Tile:
    uid = 0

    def __init__(self, K, stack, name, shape, dtype, nres=1, psum=False):
        nc = K.nc
        Tile.uid += 1
        name = "t%d_%s" % (Tile.uid, name)
        if psum:
            self.t = stack.enter_context(nc.psum_tensor(name, shape, dtype))
        else:
            self.t = stack.enter_context(nc.sbuf_tensor(name, shape, dtype))
        self.res = [Res("%s.%d" % (name, i)) for i in range(nres)]
        if psum:
            for r in self.res:
                r.excl = True

    def __getitem__(self, idx):
        return self.t[idx]

    @property
    def all(self):
        return self.res


class DT:
    def __init__(self, ap, name):
        self.ap = ap
        self.res = [Res(name)]

    @property
    def all(self):
        return self.res


D = 1024
NBLK = 16
NSLOT = 8
SEQ = 8192
NQ = NSLOT * 512
PAST = 2048
TS = 17
NKS = 4 * TS * 128
DIN = 2216
SC_F = 1.0 / 8.0
SC_M = 1.0 / math.sqrt(96.0)


DBG = dict(stop=None)


def build():
    nc = bass.Bass("TRN2", target_bir_lowering=False)

    def din(name, shape, dt=F32):
        return nc.dram_tensor(name, shape, dt, kind="ExternalInput").ap()

    def dout(name, shape):
        return nc.dram_tensor(name, shape, F32, kind="ExternalOutput").ap()

    def dscr(name, shape, dt=BF16):
        return DT(nc.dram_tensor(name, shape, dt, kind="Internal").ap(), name)

    xall = din("xall", [SEQ, D]); xown = din("xown", [NQ, D]); pown = din("pown", [NQ, 256])
    xs = din("xs", [128, D]); ps_in = din("ps", [128, 256])
    ck = din("ck", [4, PAST, 512]); cv = din("cv", [4, PAST, 512]); clf = din("clf", [4, PAST, 8])
    cckv = din("cckv", [4, PAST, 256]); ckpe = din("ckpe", [4, PAST, 32])
    w_in = din("w_in", [D, DIN]); w_qup = din("w_qup", [384, 768]); w_kvup = din("w_kvup", [256, 1024])
    w_out = din("w_out", [D, D]); w_up = din("w_up", [D, 4096]); w_down = din("w_down", [4096, D])
    w_gate = din("w_gate", [D, D]); w_proj = din("w_proj", [256, D])
    g_pre_mix = din("g_pre_mix", [D]); g_mla_q = din("g_mla_q", [384]); g_mla_kv = din("g_mla_kv", [256])
    g_out = din("g_out", [D]); g_post_mix = din("g_post_mix", [D]); g_pre_mlp = din("g_pre_mlp", [D])
    g_post_mlp = din("g_post_mlp", [D]); g_ple = din("g_ple", [D]); g_ple_post = din("g_ple_post", [D])
    b_f = din("b_f", [8])
    ident_d = din("ident", [128, 128]); su_d = din("su", [128, 128])
    rope_kv = din("rope_kv", [SEQ, 64]); rope_q = din("rope_q", [NQ, 32])
    rope_kv_s = din("rope_kv_s", [128, 64]); rope_q_s = din("rope_q_s", [128, 32])
    masks_d = din("masks", [2, 2, 8, 128, 512], BF16); smask_d = din("smask", [128, 32], BF16)
    sel_d = din("sel", [8, NSLOT * NBLK])

    y_own = dout("y_own", [NQ, D]); fk_o = dout("fk_o", [SEQ, 512]); fv_o = dout("fv_o", [SEQ, 512])
    lf_o = dout("lf_o", [SEQ, 8]); ckv_o = dout("ckv_o", [SEQ, 256]); kpe_o = dout("kpe_o", [SEQ, 32])
    ys_o = dout("ys_o", [128, D]); fks_o = dout("fks_o", [128, 512]); fvs_o = dout("fvs_o", [128, 512])
    lfs_o = dout("lfs_o", [128, 8]); ckvs_o = dout("ckvs_o", [128, 256]); kpes_o = dout("kpes_o", [128, 32])

    KTf = dscr("KTf", [4, 128, SEQ]); KTm = dscr("KTm", [4, 128, SEQ]); KPE = dscr("KPE", [32, SEQ])
    Vf = dscr("Vf", [SEQ, 512]); Vm = dscr("Vm", [SEQ, 512])
    QTf = dscr("QTf", [4, 128, NQ]); QTm = dscr("QTm", [8, 96, NQ]); CQ = dscr("CQ", [8, NQ])
    OS = dscr("OS", [NQ, D], F32)
    SKf = dscr("SKf", [3, 8, SEQ])
    KTfs = dscr("KTfs", [4, 128, NKS]); KTms = dscr("KTms", [4, 128, NKS]); KPEs = dscr("KPEs", [32, NKS])
    Vfs = dscr("Vfs", [NKS, 512]); Vms = dscr("Vms", [NKS, 512])
    QTfs = dscr("QTfs", [4, 128, 128]); QTms = dscr("QTms", [8, 96, 128]); CQs = dscr("CQs", [8, 128])
    OSs = dscr("OSs", [128, D], F32)
    WOUT = dscr("WOUT", [128, 8, D]); WGATE = dscr("WGATE", [128, 8, D]); WPROJ = dscr("WPROJ", [128, 2, D])
    WUP = dscr("WUP", [8, 128, 8, 512]); WDN = dscr("WDN", [8, 128, 32, 128])

    with ExitStack() as top:
        top.enter_context(nc.allow_non_contiguous_dma(reason="small strided layout DMAs"))
        K = Kern(nc, top)

        def T(stack, name, shape, dt, nres=1, psum=False):
            return Tile(K, stack, name, shape, dt, nres, psum)

        identf = T(top, "identf", [128, 128], F32)
        identb = T(top, "identb", [128, 128], BF16)
        suf = T(top, "suf", [128, 128], F32)
        onesf = T(top, "onesf", [128, 128], F32)
        onesrow = T(top, "onesrow", [8, 512], F32)
        WA = T(top, "WA", [128, 22656], BF16)
        gcol = T(top, "gcol", [128, 48], F32)
        gbc = T(top, "gbc", [128, 3 * D + 256], F32)
        bcol = T(top, "bcol", [8, 2], F32)
        negc = T(top, "negc", [128, 64 * 8], F32)
        negcs = T(top, "negcs", [128, 4 * TS * 8], F32)
        CB = T(top, "CB", [8, NBLK], F32)
        CO = T(top, "CO", [8, NSLOT], F32)
        selt = T(top, "selt", [8, NSLOT * NBLK], F32)
        smask = T(top, "smask", [128, 32], BF16)

        K.dma("sp", identf[:], ident_d, writes=identf.all)
        K.dma("sp", suf[:], su_d, writes=suf.all)
        K.dma("sp", selt[:], sel_d, writes=selt.all)
        K.dma("sp", smask[:], smask_d, writes=smask.all)
        K.op("dve", lambda e: e.tensor_copy(out=identb[:], in_=identf[:]), reads=identf.all, writes=identb.all)
        K.op("dve", lambda e: e.memset(onesf[:], 1.0), writes=onesf.all)
        K.op("dve", lambda e: e.memset(onesrow[:], 1.0), writes=onesrow.all)
        K.op("dve", lambda e: e.memset(CB[:], 0.0), writes=CB.all)
        K.dma("sp", gcol[:, 0:8], g_pre_mix.rearrange("(k p) -> p k", p=128), pwrites=gcol.all)
        K.dma("sp", gcol[:, 8:11], g_mla_q.rearrange("(k p) -> p k", p=128), pwrites=gcol.all)
        K.dma("sp", gcol[:, 11:19], g_out.rearrange("(k p) -> p k", p=128), pwrites=gcol.all)
        K.dma("sp", gcol[:, 19:27], g_pre_mlp.rearrange("(k p) -> p k", p=128), pwrites=gcol.all)
        K.dma("sp", gcol[:, 27:35], g_ple.rearrange("(k p) -> p k", p=128), pwrites=gcol.all)
        for i, g in enumerate((g_post_mix, g_post_mlp, g_ple_post)):
            K.dma("sp", gbc[:, i * D:(i + 1) * D], g.rearrange("(o n) -> o n", o=1).broadcast_to([128, D]), pwrites=gbc.all)
        K.dma("sp", gbc[:, 3 * D:3 * D + 256], g_mla_kv.rearrange("(o n) -> o n", o=1).broadcast_to([128, 256]), pwrites=gbc.all)
        K.dma("sp", bcol[:, 0:1], b_f.rearrange("(h o) -> h o", o=1), pwrites=bcol.all)
        K.op("dve", lambda e: e.tensor_scalar(out=bcol[:, 1:2], in0=bcol[:, 0:1], scalar1=-1.0, scalar2=None, op0=ALU.mult),
             reads=bcol.all, writes=bcol.all)

        w_in_sb = WA[:, 0:8 * DIN].rearrange("p (k n) -> p k n", k=8)
        o1 = 8 * DIN
        w_q_sb = WA[:, o1:o1 + 3 * 768].rearrange("p (k n) -> p k n", k=3)
        o2 = o1 + 3 * 768
        w_kn_sb = WA[:, o2:o2 + 1024].rearrange("p (k n) -> p k n", k=2)
        o3 = o2 + 1024
        w_kv_sb = WA[:, o3:o3 + 1024].rearrange("p (k n) -> p k n", k=2)
        w_out_sb = WA[:, 0:8192].rearrange("p (k n) -> p k n", k=8)
        w_gate_sb = WA[:, 8192:16384].rearrange("p (k n) -> p k n", k=8)
        w_proj_sb = WA[:, 16384:18432].rearrange("p (k n) -> p k n", k=2)

        p0 = ExitStack()
        if True:
            stg = [T(p0, "stg%d" % i, [128, DIN], F32) for i in range(2)]
            cvb = [T(p0, "cvb%d" % i, [128, 1024], BF16) for i in range(2)]
            cnt = [0]
            cnts = [0]

            def conv(src_ap, ncols, dst_fn, scale_ap=None):
                i = cnts[0] % 2
                cnts[0] += 1
                s = stg[i]
                K.dma("pool" if (cnts[0] <= 13 and i == 1) else "sp", s[:, 0:ncols], src_ap, writes=s.all)
                return s

            for kc in range(8):
                s = conv(w_in[kc * 128:(kc + 1) * 128, :], DIN, None)
                K.op("act", lambda e, s=s, kc=kc: e.activation(out=w_in_sb[:, kc, :], in_=s[:, 0:DIN], func=AF.Copy, scale=gcol[:, kc:kc + 1]),
                     reads=s.all + gcol.all, pwrites=WA.all)
            for kc in range(3):
                s = conv(w_qup[kc * 128:(kc + 1) * 128, :], 768, None)
                K.op("act", lambda e, s=s, kc=kc: e.activation(out=w_q_sb[:, kc, :], in_=s[:, 0:768], func=AF.Copy, scale=gcol[:, 8 + kc:9 + kc]),
                     reads=s.all + gcol.all, pwrites=WA.all)
            for kc in range(2):
                s = conv(w_kvup[kc * 128:(kc + 1) * 128, :], 1024, None)
                sv = s[:, 0:1024].rearrange("p (h t d) -> p h t d", h=8, t=2)
                K.op("act", lambda e, sv=sv, kc=kc: e.activation(out=w_kn_sb[:, kc, :].rearrange("p (h d) -> p h d", h=8), in_=sv[:, :, 0, :], func=AF.Copy),
                     reads=s.all, pwrites=WA.all)
                K.op("act", lambda e, sv=sv, kc=kc: e.activation(out=w_kv_sb[:, kc, :].rearrange("p (h d) -> p h d", h=8), in_=sv[:, :, 1, :], func=AF.Copy),
                     reads=s.all, pwrites=WA.all)

            def conv_to_scr(src_ap, ncols, dst_ap, dst_dt, scale_col):
                def load_fn():
                    return conv(src_ap, ncols, None)

                def fin_fn(s):
                    j = cnt[0] % 2
                    cnt[0] += 1
                    c = cvb[j]
                    if scale_col is None:
                        K.op("act", lambda e: e.activation(out=c[:, 0:ncols], in_=s[:, 0:ncols], func=AF.Copy), reads=s.all, writes=c.all)
                    else:
                        K.op("act", lambda e: e.activation(out=c[:, 0:ncols], in_=s[:, 0:ncols], func=AF.Copy, scale=gcol[:, scale_col:scale_col + 1]),
                             reads=s.all + gcol.all, writes=c.all)
                    K.dma("pool", dst_ap, c[:, 0:ncols] if dst_ap.shape[-1] == ncols else c[:, 0:ncols].rearrange("p (a b) -> p a b", b=dst_ap.shape[-1]),
                          reads=c.all, pwrites=dst_dt.all)
                return (load_fn, fin_fn)

            conv_state = {}

            def conv_step():
                if "pending" in conv_state:
                    conv_state.pop("pending")()
                if conv_jobs:
                    load_fn, fin_fn = conv_jobs.pop(0)()
                    s_ = load_fn()
                    conv_state["pending"] = lambda: fin_fn(s_)

            def conv_flush():
                while conv_jobs or "pending" in conv_state:
                    conv_step()

            conv_jobs = []
            for kc in range(8):
                conv_jobs.append(lambda kc=kc: conv_to_scr(w_out[kc * 128:(kc + 1) * 128, :], 1024, WOUT.ap[:, kc, :], WOUT, 11 + kc))
                conv_jobs.append(lambda kc=kc: conv_to_scr(w_gate[kc * 128:(kc + 1) * 128, :], 1024, WGATE.ap[:, kc, :], WGATE, 27 + kc))
            for kc in range(2):
                conv_jobs.append(lambda kc=kc: conv_to_scr(w_proj[kc * 128:(kc + 1) * 128, :], 1024, WPROJ.ap[:, kc, :], WPROJ, None))
            for kc in range(8):
                for q4 in range(4):
                    conv_jobs.append(lambda kc=kc, q4=q4: conv_to_scr(w_up[kc * 128:(kc + 1) * 128, q4 * 1024:(q4 + 1) * 1024], 1024,
                                                                     WUP.ap[2 * q4:2 * q4 + 2, :, kc, :].rearrange("c p n -> p c n"), WUP, 19 + kc))
            for kc in range(32):
                conv_jobs.append(lambda kc=kc: conv_to_scr(w_down[kc * 128:(kc + 1) * 128, :], 1024,
                                                           WDN.ap[:, :, kc, :].rearrange("c p n -> p c n"), WDN, None))

        def _done():
            K.finish()
            global _LASTK
            _LASTK = K
            return nc
        if DBG["stop"] == "0":
            conv_flush()
            return _done()

        def run_pairs(gens):
            gens = list(gens)
            for i in range(0, len(gens), 2):
                live = gens[i:i + 2]
                while live:
                    for g_ in list(live):
                        if DBG.get("gstop") is not None:
                            DBG["gcount"] = DBG.get("gcount", 0) + 1
                            if DBG["gcount"] > DBG["gstop"]:
                                return
                        try:
                            next(g_)
                        except StopIteration:
                            live.remove(g_)

        def rstd_from_ss(ss_ap, n, out_ap, tmp_ap, rd, wr):
            K.op("act", lambda e: e.activation(out=tmp_ap, in_=ss_ap, func=AF.Ln, scale=1.0 / n, bias=EPS), reads=rd, writes=wr)
            K.op("act", lambda e: e.activation(out=out_ap, in_=tmp_ap, func=AF.Exp, scale=-0.5), reads=wr, writes=wr)

        def phaseA(x_src, nblk, nsub, kvpass, out_aps, scr, rope_src, is_sample):
            W = nsub * 128
            with ExitStack() as pa:
                xt = [T(pa, "xA%d" % i, [128, nsub, D], F32, nres=nsub) for i in range(2)]
                xn = T(pa, "xnA", [128, nsub, D], BF16, nres=nsub)
                xnTs = [T(pa, "xnT%d" % i, [128, 8, W], BF16) for i in range(2)]
                junk = T(pa, "junkA", [128, D], BF16)
                sts = [T(pa, "stA%d" % i, [128, 16], F32) for i in range(2)]
                rp = T(pa, "ropeA", [128, nsub, 64], F32)
                ptr = [T(pa, "ptrA%d" % i, [128, 1024], BF16, psum=True) for i in range(2)]
                pfm = [T(pa, "pfmA%d" % i, [128, 512], F32, psum=True) for i in range(2)]
                ptm = [T(pa, "ptmA%d" % i, [128, 512], F32, psum=True) for i in range(4)]
                nptm = [0]

                def next_ptm():
                    t_ = ptm[nptm[0] % 4]
                    nptm[0] += 1
                    return t_
                fmb = [T(pa, "fmbA%d" % i, [128, 4, W], BF16) for i in range(2)]
                fmbm = [T(pa, "fmbmA%d" % i, [128, 4, W], BF16) for i in range(2)]
                pending_tail = []
                spT = T(pa, "spT", [8, W], F32)
                SsT = T(pa, "SsT", [8, W], F32)
                eT = T(pa, "eT", [8, W], F32)
                carry = T(pa, "carryA", [8, 1], F32)
                skb = T(pa, "skbA", [8, 3, W], BF16)
                skr = T(pa, "skrA", [8, W], F32)
                if kvpass:
                    stage = [T(pa, "stageA%d" % i, [128, 1024], F32) for i in range(2)]
                    vb = T(pa, "vbA", [128, nsub, 512], BF16, nres=nsub)
                    vmb = T(pa, "vmbA", [128, nsub, 512], BF16)
                    ckvf = [T(pa, "ckvfA%d" % i, [128, 256], F32) for i in range(2)]
                    ckvbs = [T(pa, "ckvbA%d" % i, [128, 256], BF16) for i in range(nsub)]
                    ckvT = T(pa, "ckvTA", [128, 2, W], BF16, nres=nsub)
                    kpef = [T(pa, "kpefA%d" % i, [128, 32], F32) for i in range(2)]
                    kpebs = [T(pa, "kpebA%d" % i, [128, 32], BF16) for i in range(nsub)]
                    kpets = [T(pa, "kpetA%d" % i, [128, 64], F32) for i in range(2)]
                    kpeT = T(pa, "kpeTA", [32, W], BF16, nres=nsub)
                    lst = T(pa, "lstA", [128, nsub, 8], F32)
                else:
                    cqb = T(pa, "cqbA", [8, W], BF16)
                    qnbs = [T(pa, "qnbA%d" % i, [128, 384], BF16) for i in range(nsub)]
                    qnTs = [T(pa, "qnTA%d" % i, [128, 3, 128], BF16) for i in range(2)]
                    qfs = [T(pa, "qfA%d" % i, [128, 768], F32) for i in range(2)]
                    qbs = [T(pa, "qbA%d" % i, [128, 768], BF16) for i in range(nsub)]
                    qt1s = [T(pa, "qt1A%d" % i, [128, 256], F32) for i in range(2)]
                    QmT = T(pa, "QmTA", [96, 8, W], BF16, nres=nsub)
                K.op("dve", lambda e: e.memset(carry[:], 0.0), writes=carry.all)

                nfm = [0]

                def load_x(blk):
                    xb = xt[blk % 2]
                    K.dma("sp", xb[:], x_src[blk * W:(blk + 1) * W, :].rearrange("(s p) d -> p s d", p=128), writes=xb.all)

                def emit_norm(blk):
                    xb = xt[blk % 2]
                    xnT = xnTs[blk % 2]

                    def sub_norm(s):
                        par = s % 2
                        st = sts[par]
                        K.op("act", lambda e: e.activation(out=junk[:], in_=xb[:, s, :], func=AF.Square, accum_out=st[:, 0:1]),
                             reads=[xb.res[s]], writes=st.all)
                        yield
                        rstd_from_ss(st[:, 0:1], D, st[:, 2:3], st[:, 1:2], st.all, st.all)
                        yield
                        K.op("dve", lambda e: e.tensor_scalar(out=xn[:, s, :], in0=xb[:, s, :], scalar1=st[:, 2:3], scalar2=None, op0=ALU.mult),
                             reads=[xb.res[s]] + st.all, writes=[xn.res[s]])
                        yield
                        pt = ptr[par]
                        for kc in range(8):
                            K.op("pe", lambda e, kc=kc: e.transpose(out=pt[:, kc * 128:(kc + 1) * 128], in_=xn[:, s, kc * 128:(kc + 1) * 128], identity=identb[:]),
                                 reads=[xn.res[s]] + identb.all, writes=pt.all)
                        yield
                        K.op("act", lambda e: e.activation(out=xnT[:, :, s * 128:(s + 1) * 128], in_=pt[:].rearrange("p (k n) -> p k n", k=8), func=AF.Copy),
                             reads=pt.all, pwrites=xnT.all)
                    run_pairs([sub_norm(s) for s in range(nsub)])

                load_x(0)
                emit_norm(0)
                for blk in range(nblk):
                    xnT = xnTs[blk % 2]
                    if blk + 1 < nblk:
                        load_x(blk + 1)
                    nr = 64 if kvpass else 32
                    K.dma("sp", rp[:, :, 0:nr], rope_src[blk * W:(blk + 1) * W, :].rearrange("(s p) d -> p s d", p=128), writes=rp.all)
                    if DBG.get("astop") == 1:
                        K.barrier()
                        return
                    cbase = 512 if kvpass else 0
                    fb = fmb[blk % 2]
                    for pr in range(4):
                        pf = pfm[nfm[0] % 2]
                        nfm[0] += 1
                        for kc in range(8):
                            K.op("pe", lambda e, pr=pr, kc=kc, pf=pf: e.matmul(pf[:, 0:W], lhsT=w_in_sb[:, kc, cbase + pr * 128:cbase + (pr + 1) * 128], rhs=xnT[:, kc, :], start=(kc == 0), stop=(kc == 7)),
                                 reads=WA.all + xnT.all, writes=pf.all)
                        if kvpass:
                            K.op("act", lambda e, pr=pr, pf=pf: e.activation(out=fb[:, pr, :], in_=pf[:, 0:W], func=AF.Copy), reads=pf.all, pwrites=fb.all)
                        else:
                            K.op("act", lambda e, pr=pr, pf=pf: e.activation(out=fb[:, pr, :], in_=pf[:, 0:W], func=AF.Copy, scale=SC_F), reads=pf.all, pwrites=fb.all)
                    if not is_sample:
                        dst = (scr["KTf"] if kvpass else scr["QTf"])
                        K.dma("pool", dst.ap[:, :, blk * W:(blk + 1) * W].rearrange("c p n -> p c n"), fb[:], reads=fb.all, pwrites=dst.all)
                    else:
                        if kvpass:
                            for sidx in range(4):
                                k0 = sidx * TS * 128 + PAST
                                K.dma("pool", scr["KTf"].ap[:, :, k0:k0 + 32].rearrange("c p n -> p c n"), fb[:, :, sidx * 32:(sidx + 1) * 32], reads=fb.all, pwrites=scr["KTf"].all)
                        else:
                            K.dma("pool", scr["QTf"].ap.rearrange("c p n -> p c n"), fb[:], reads=fb.all, pwrites=scr["QTf"].all)
                    if blk + 1 < nblk:
                        emit_norm(blk + 1)
                    while pending_tail:
                        pending_tail.pop(0)()
                    if DBG.get("astop") == 2:
                        K.barrier()
                        return
                    if kvpass and not is_sample:
                        conv_step()
                    pf = pfm[nfm[0] % 2]
                    nfm[0] += 1
                    for kc in range(8):
                        K.op("pe", lambda e, kc=kc, pf=pf: e.matmul(pf[0:8, 0:W], lhsT=w_in_sb[:, kc, 1536:1544], rhs=xnT[:, kc, :], start=(kc == 0), stop=(kc == 7)),
                             reads=WA.all + xnT.all, writes=pf.all)
                    K.op("act", lambda e, pf=pf: e.activation(out=eT[:], in_=pf[0:8, 0:W], func=AF.Exp, scale=-1.0, bias=bcol[:, 1:2]), reads=pf.all + bcol.all, writes=eT.all)
                    K.op("act", lambda e: e.activation(out=spT[:], in_=eT[:], func=AF.Ln, bias=1.0), reads=eT.all, writes=spT.all)
                    if is_sample:
                        for sidx in range(4):
                            K.op("dve", lambda e, sidx=sidx: e.tensor_tensor_scan(out=SsT[:, sidx * 32:(sidx + 1) * 32], data0=onesrow[:, 0:32], data1=spT[:, sidx * 32:(sidx + 1) * 32], initial=0.0, op0=ALU.mult, op1=ALU.add),
                                 reads=spT.all + onesrow.all, pwrites=SsT.all)
                    else:
                        if kvpass:
                            K.op("dve", lambda e, blk=blk: e.tensor_copy(out=CB[:, blk:blk + 1], in_=carry[:]), reads=carry.all, pwrites=CB.all)
                            init_ap = carry[:, 0:1]
                            rdi = carry.all
                        else:
                            init_ap = CO[:, blk:blk + 1]
                            rdi = CO.all
                        K.op("dve", lambda e, init_ap=init_ap: e.tensor_tensor_scan(out=SsT[:], data0=onesrow[:, 0:W], data1=spT[:], initial=init_ap, op0=ALU.mult, op1=ALU.add),
                             reads=spT.all + onesrow.all + rdi, writes=SsT.all)
                        if kvpass:
                            K.op("dve", lambda e: e.tensor_copy(out=carry[:], in_=SsT[:, W - 1:W]), reads=SsT.all, writes=carry.all)
                            K.op("dve", lambda e: e.tensor_copy(out=skb[:, 0, :], in_=SsT[:]), reads=SsT.all, writes=skb.all)
                            K.op("dve", lambda e: e.tensor_tensor(out=skr[:], in0=SsT[:], in1=skb[:, 0, :], op=ALU.subtract), reads=SsT.all + skb.all, writes=skr.all)
                            K.op("dve", lambda e: e.tensor_copy(out=skb[:, 1, :], in_=skr[:]), reads=skr.all, pwrites=skb.all)
                            K.op("dve", lambda e: e.tensor_tensor(out=skr[:], in0=skr[:], in1=skb[:, 1, :], op=ALU.subtract), reads=skb.all, writes=skr.all)
                            K.op("dve", lambda e: e.tensor_copy(out=skb[:, 2, :], in_=skr[:]), reads=skr.all, pwrites=skb.all)
                            K.dma("pool", SKf.ap[:, :, blk * W:(blk + 1) * W].rearrange("i h n -> h i n"), skb[:], reads=skb.all, pwrites=SKf.all)
                    if DBG.get("astop") == 3:
                        K.barrier()
                        return
                    if kvpass:
                        def lf_job(blk=blk):
                            psm = pfm[nfm[0] % 2]
                            nfm[0] += 1
                            for s in range(nsub):
                                K.op("pe", lambda e, s=s: e.transpose(out=psm[:, s * 8:(s + 1) * 8], in_=spT[0:8, s * 128:(s + 1) * 128], identity=identf[0:8, 0:8]),
                                     reads=spT.all + identf.all, writes=psm.all)
                                K.op("pe", lambda e, s=s: e.transpose(out=psm[:, 64 + s * 8:64 + (s + 1) * 8], in_=SsT[0:8, s * 128:(s + 1) * 128], identity=identf[0:8, 0:8]),
                                     reads=SsT.all + identf.all, writes=psm.all)
                            K.op("dve", lambda e: e.tensor_scalar(out=lst[:].rearrange("p s h -> p (s h)"), in0=psm[:, 0:nsub * 8], scalar1=-1.0, scalar2=None, op0=ALU.mult),
                                 reads=psm.all, writes=lst.all)
                            K.dma("pool", out_aps["lf"][blk * W:(blk + 1) * W, :].rearrange("(s p) h -> p s h", p=128), lst[:], reads=lst.all)
                            if is_sample:
                                for sidx in range(4):
                                    col = (sidx * TS + 16) * 8
                                    K.op("act", lambda e, sidx=sidx, col=col: e.activation(out=negcs[0:32, col:col + 8], in_=psm[sidx * 32:(sidx + 1) * 32, 64:72], func=AF.Copy),
                                         reads=psm.all, pwrites=negcs.all)
                            else:
                                K.op("dve", lambda e, blk=blk: e.tensor_copy(out=negc[:, blk * 32:(blk + 1) * 32], in_=psm[:, 64:64 + 32]), reads=psm.all, pwrites=negc.all)
                        pending_tail.append(lf_job)
                    else:
                        K.op("dve", lambda e: e.tensor_scalar(out=cqb[:], in0=SsT[:], scalar1=-1.0, scalar2=None, op0=ALU.mult), reads=SsT.all, writes=cqb.all)
                        if is_sample:
                            K.dma("pool", scr["CQ"].ap, cqb[:], reads=cqb.all, pwrites=scr["CQ"].all)
                        else:
                            K.dma("pool", scr["CQ"].ap[:, blk * W:(blk + 1) * W], cqb[:], reads=cqb.all, pwrites=scr["CQ"].all)
                    if DBG.get("astop") == 4:
                        K.barrier()
                        return
                    def sub_kv(s):
                        par = s % 2
                        st = sts[par]
                        r0 = blk * W + s * 128
                        sg = stage[par]
                        banks = []
                        for hf in range(2):
                            pb_ = next_ptm()
                            banks.append(pb_)
                            for kc in range(8):
                                K.op("pe", lambda e, hf=hf, kc=kc, pb_=pb_: e.matmul(pb_[:], lhsT=xnT[:, kc, s * 128:(s + 1) * 128], rhs=w_in_sb[:, kc, 512 + hf * 512:1024 + hf * 512], start=(kc == 0), stop=(kc == 7)),
                                     reads=WA.all + xnT.all, writes=pb_.all)
                        yield
                        K.op("act", lambda e: e.activation(out=sg[:, 0:512], in_=banks[0][:], func=AF.Copy), reads=banks[0].all, writes=sg.all)
                        K.op("act", lambda e: e.activation(out=sg[:, 512:1024], in_=banks[1][:], func=AF.Copy), reads=banks[1].all + sg.all, pwrites=sg.all)
                        K.op("dve", lambda e: e.tensor_copy(out=vb[:, s, :], in_=banks[1][:]), reads=banks[1].all, writes=[vb.res[s]])
                        yield
                        if not is_sample:
                            conv_step()
                        K.dma("pool", out_aps["fk"][r0:r0 + 128, :], sg[:, 0:512], reads=sg.all)
                        K.dma("pool", out_aps["fv"][r0:r0 + 128, :], sg[:, 512:1024], reads=sg.all)
                        p2 = next_ptm()
                        for kc in range(8):
                            K.op("pe", lambda e, kc=kc: e.matmul(p2[:, 0:288], lhsT=xnT[:, kc, s * 128:(s + 1) * 128], rhs=w_in_sb[:, kc, 1928:2216], start=(kc == 0), stop=(kc == 7)),
                                 reads=WA.all + xnT.all, writes=p2.all)
                        yield
                        K.op("act", lambda e: e.activation(out=junk[:, 0:256], in_=p2[:, 0:256], func=AF.Square, accum_out=st[:, 4:5]), reads=p2.all, writes=st.all)
                        yield
                        rstd_from_ss(st[:, 4:5], 256, st[:, 6:7], st[:, 5:6], st.all, st.all)
                        yield
                        cf = ckvf[par]
                        K.op("dve", lambda e: e.scalar_tensor_tensor(out=cf[:], in0=p2[:, 0:256], scalar=st[:, 6:7], in1=gbc[:, 3 * D:3 * D + 256], op0=ALU.mult, op1=ALU.mult),
                             reads=p2.all + st.all + gbc.all, writes=cf.all)
                        kf = kpef[par]
                        kpet = kpets[par]
                        K.op("dve", lambda e: e.tensor_tensor(out=kpet[:, 0:32], in0=p2[:, 256:288], in1=rp[:, s, 0:32], op=ALU.mult), reads=p2.all + rp.all, writes=kpet.all)
                        K.op("dve", lambda e: e.tensor_tensor(out=kpet[:, 32:48], in0=p2[:, 272:288], in1=rp[:, s, 32:48], op=ALU.mult), reads=p2.all + rp.all, pwrites=kpet.all)
                        K.op("dve", lambda e: e.tensor_tensor(out=kpet[:, 48:64], in0=p2[:, 256:272], in1=rp[:, s, 48:64], op=ALU.mult), reads=p2.all + rp.all, pwrites=kpet.all)
                        yield
                        K.dma("pool", out_aps["ckv"][r0:r0 + 128, :], cf[:], reads=cf.all)
                        ckvb = ckvbs[s]
                        kpeb = kpebs[s]
                        K.op("dve", lambda e: e.tensor_copy(out=ckvb[:], in_=cf[:]), reads=cf.all, writes=ckvb.all)
                        K.op("dve", lambda e: e.tensor_tensor(out=kf[:], in0=kpet[:, 0:32], in1=kpet[:, 32:64], op=ALU.add), reads=kpet.all, writes=kf.all)
                        yield
                        K.dma("pool", out_aps["kpe"][r0:r0 + 128, :], kf[:], reads=kf.all)
                        K.op("dve", lambda e: e.tensor_copy(out=kpeb[:], in_=kf[:]), reads=kf.all, writes=kpeb.all)

                        def part2():
                            pt = ptr[par]
                            for kc in range(2):
                                K.op("pe", lambda e, kc=kc: e.transpose(out=pt[:, kc * 128:(kc + 1) * 128], in_=ckvb[:, kc * 128:(kc + 1) * 128], identity=identb[:]),
                                     reads=ckvb.all + identb.all, writes=pt.all)
                            K.op("pe", lambda e: e.transpose(out=pt[0:32, 256:384], in_=kpeb[:, 0:32], identity=identb[:]), reads=kpeb.all + identb.all, writes=pt.all)
                            K.op("act", lambda e: e.activation(out=ckvT[:, :, s * 128:(s + 1) * 128], in_=pt[:, 0:256].rearrange("p (k n) -> p k n", k=2), func=AF.Copy),
                                 reads=pt.all, writes=[ckvT.res[s]])
                            K.op("act", lambda e: e.activation(out=kpeT[:, s * 128:(s + 1) * 128], in_=pt[0:32, 256:384], func=AF.Copy), reads=pt.all, writes=[kpeT.res[s]])
                        pending_tail.append(part2)

                    def q_partA(s):
                        par = s % 2
                        st = sts[par]
                        qnb = qnbs[s]
                        p2 = next_ptm()
                        for kc in range(8):
                            K.op("pe", lambda e, kc=kc: e.matmul(p2[:, 0:384], lhsT=xnT[:, kc, s * 128:(s + 1) * 128], rhs=w_in_sb[:, kc, 1544:1928], start=(kc == 0), stop=(kc == 7)),
                                 reads=WA.all + xnT.all, writes=p2.all)
                        yield
                        K.op("act", lambda e: e.activation(out=junk[:, 0:384], in_=p2[:, 0:384], func=AF.Square, accum_out=st[:, 4:5]), reads=p2.all, writes=st.all)
                        yield
                        rstd_from_ss(st[:, 4:5], 384, st[:, 6:7], st[:, 5:6], st.all, st.all)
                        yield
                        K.op("dve", lambda e: e.tensor_scalar(out=qnb[:], in0=p2[:, 0:384], scalar1=st[:, 6:7], scalar2=None, op0=ALU.mult), reads=p2.all + st.all, writes=qnb.all)

                    def q_partB(s):
                        par = s % 2
                        qnb, qnT, qf, qb, qt1 = qnbs[s], qnTs[par], qfs[par], qbs[s], qt1s[par]
                        pt = ptr[par]
                        for kc in range(3):
                            K.op("pe", lambda e, kc=kc: e.transpose(out=pt[:, kc * 128:(kc + 1) * 128], in_=qnb[:, kc * 128:(kc + 1) * 128], identity=identb[:]),
                                 reads=qnb.all + identb.all, writes=pt.all)
                        yield
                        K.op("act", lambda e: e.activation(out=qnT[:].rearrange("p k n -> p (k n)"), in_=pt[:, 0:384], func=AF.Copy), reads=pt.all, writes=qnT.all)
                        yield
                        banks = []
                        for hf in range(2):
                            pb_ = next_ptm()
                            banks.append(pb_)
                            for kc in range(3):
                                K.op("pe", lambda e, hf=hf, kc=kc, pb_=pb_: e.matmul(pb_[:, 0:384], lhsT=qnT[:, kc, :], rhs=w_q_sb[:, kc, hf * 384:(hf + 1) * 384], start=(kc == 0), stop=(kc == 2)),
                                     reads=WA.all + qnT.all, writes=pb_.all)
                        yield
                        K.op("act", lambda e: e.activation(out=qf[:, 0:384], in_=banks[0][:, 0:384], func=AF.Copy), reads=banks[0].all, writes=qf.all)
                        K.op("act", lambda e: e.activation(out=qf[:, 384:768], in_=banks[1][:, 0:384], func=AF.Copy), reads=banks[1].all + qf.all, pwrites=qf.all)
                        yield
                        qv = qf[:].rearrange("p (h n) -> p h n", h=8)
                        qbv = qb[:].rearrange("p (h n) -> p h n", h=8)
                        cs_ = rp[:, s, 0:16].rearrange("p (o n) -> p o n", o=1).broadcast_to([128, 8, 16])
                        sn_ = rp[:, s, 16:32].rearrange("p (o n) -> p o n", o=1).broadcast_to([128, 8, 16])
                        t1v = qt1[:, 0:128].rearrange("p (h n) -> p h n", h=8)
                        t2v = qt1[:, 128:256].rearrange("p (h n) -> p h n", h=8)
                        K.op("dve", lambda e: e.tensor_scalar(out=qbv[:, :, 0:64], in0=qv[:, :, 0:64], scalar1=SC_M, scalar2=None, op0=ALU.mult), reads=qf.all, writes=qb.all)
                        K.op("dve", lambda e: e.tensor_tensor(out=t1v, in0=qv[:, :, 64:80], in1=cs_, op=ALU.mult), reads=qf.all + rp.all, writes=qt1.all)
                        K.op("dve", lambda e: e.tensor_tensor(out=t2v, in0=qv[:, :, 80:96], in1=sn_, op=ALU.mult), reads=qf.all + rp.all, pwrites=qt1.all)
                        yield
                        K.op("dve", lambda e: e.tensor_tensor(out=qbv[:, :, 64:80], in0=t1v, in1=t2v, op=ALU.subtract), reads=qt1.all, pwrites=qb.all)
                        yield
                        K.op("dve", lambda e: e.tensor_tensor(out=t1v, in0=qv[:, :, 64:80], in1=sn_, op=ALU.mult), reads=qf.all + rp.all + qb.all, writes=qt1.all)
                        K.op("dve", lambda e: e.tensor_tensor(out=t2v, in0=qv[:, :, 80:96], in1=cs_, op=ALU.mult), reads=qf.all + rp.all, pwrites=qt1.all)
                        yield
                        K.op("dve", lambda e: e.tensor_tensor(out=qbv[:, :, 80:96], in0=t1v, in1=t2v, op=ALU.add), reads=qt1.all, pwrites=qb.all)

                    def q_partC(s):
                        par = s % 2
                        qb = qbs[s]
                        pt = ptr[par]
                        for h in range(8):
                            K.op("pe", lambda e, h=h: e.transpose(out=pt[0:96, h * 128:(h + 1) * 128], in_=qb[:, h * 96:(h + 1) * 96], identity=identb[:]),
                                 reads=qb.all + identb.all, writes=pt.all)
                        yield
                        K.op("act", lambda e: e.activation(out=QmT[:, :, s * 128:(s + 1) * 128], in_=pt[0:96, :].rearrange("p (h n) -> p h n", h=8), func=AF.Copy),
                             reads=pt.all, writes=[QmT.res[s]])

                    if kvpass:
                        run_pairs([sub_kv(s) for s in range(nsub)])
                    else:
                        run_pairs([q_partA(s) for s in range(nsub)])
                        run_pairs([q_partB(s) for s in range(nsub)])
                        run_pairs([q_partC(s) for s in range(nsub)])
                    if DBG.get("astop") == 6:
                        K.barrier()
                        return
                    def tail(blk=blk):
                        if kvpass:
                            if not is_sample:
                                K.dma("pool", scr["Vf"].ap[blk * W:(blk + 1) * W, :].rearrange("(s p) c -> p s c", p=128), vb[:], reads=vb.all, pwrites=scr["Vf"].all)
                            else:
                                for sidx in range(4):
                                    K.dma("pool", scr["Vf"].ap[sidx * TS * 128 + PAST:sidx * TS * 128 + PAST + 32, :], vb[sidx * 32:(sidx + 1) * 32, 0, :], reads=vb.all, pwrites=scr["Vf"].all)
                            fb2 = fmbm[blk % 2]
                            for pr in range(4):
                                pf = pfm[nfm[0] % 2]
                                nfm[0] += 1
                                for kc in range(2):
                                    K.op("pe", lambda e, pr=pr, kc=kc, pf=pf: e.matmul(pf[:, 0:W], lhsT=w_kn_sb[:, kc, pr * 128:(pr + 1) * 128], rhs=ckvT[:, kc, :], start=(kc == 0), stop=(kc == 1)),
                                         reads=WA.all + ckvT.all, writes=pf.all)
                                K.op("act", lambda e, pr=pr, pf=pf: e.activation(out=fb2[:, pr, :], in_=pf[:, 0:W], func=AF.Copy), reads=pf.all, pwrites=fb2.all)
                            for s in range(nsub):
                                p2 = next_ptm()
                                for kc in range(2):
                                    K.op("pe", lambda e, s=s, kc=kc, p2=p2: e.matmul(p2[:, 0:512], lhsT=ckvT[:, kc, s * 128:(s + 1) * 128], rhs=w_kv_sb[:, kc, :], start=(kc == 0), stop=(kc == 1)),
                                         reads=WA.all + ckvT.all, writes=p2.all)
                                K.op("dve", lambda e, s=s, p2=p2: e.tensor_copy(out=vmb[:, s, :], in_=p2[:, 0:512]), reads=p2.all, pwrites=vmb.all)
                            if not is_sample:
                                K.dma("pool", scr["KTm"].ap[:, :, blk * W:(blk + 1) * W].rearrange("c p n -> p c n"), fb2[:], reads=fb2.all, pwrites=scr["KTm"].all)
                                K.dma("pool", scr["KPE"].ap[:, blk * W:(blk + 1) * W], kpeT[:], reads=kpeT.all, pwrites=scr["KPE"].all)
                                K.dma("pool", scr["Vm"].ap[blk * W:(blk + 1) * W, :].rearrange("(s p) c -> p s c", p=128), vmb[:], reads=vmb.all, pwrites=scr["Vm"].all)
                            else:
                                for sidx in range(4):
                                    k0 = sidx * TS * 128 + PAST
                                    K.dma("pool", scr["KTm"].ap[:, :, k0:k0 + 32].rearrange("c p n -> p c n"), fb2[:, :, sidx * 32:(sidx + 1) * 32], reads=fb2.all, pwrites=scr["KTm"].all)
                                    K.dma("pool", scr["KPE"].ap[:, k0:k0 + 32], kpeT[:, sidx * 32:(sidx + 1) * 32], reads=kpeT.all, pwrites=scr["KPE"].all)
                                    K.dma("pool", scr["Vm"].ap[k0:k0 + 32, :], vmb[sidx * 32:(sidx + 1) * 32, 0, :], reads=vmb.all, pwrites=scr["Vm"].all)
                        else:
                            if is_sample:
                                K.dma("pool", scr["QTm"].ap.rearrange("h r n -> r h n"), QmT[:], reads=QmT.all, pwrites=scr["QTm"].all)
                            else:
                                K.dma("pool", scr["QTm"].ap[:, :, blk * W:(blk + 1) * W].rearrange("h r n -> r h n"), QmT[:], reads=QmT.all, pwrites=scr["QTm"].all)
                    pending_tail.append(tail)
                    if kvpass and not is_sample:
                        conv_step()
                while pending_tail:
                    pending_tail.pop(0)()

        _phaseA_raw = phaseA

        def phaseA(*a):
            _phaseA_raw(*a)
            K.barrier()

        scrP = dict(KTf=KTf, KTm=KTm, KPE=KPE, Vf=Vf, Vm=Vm, QTf=QTf, QTm=QTm, CQ=CQ)
        scrS = dict(KTf=KTfs, KTm=KTms, KPE=KPEs, Vf=Vfs, Vm=Vms, QTf=QTfs, QTm=QTms, CQ=CQs)
        outP = dict(fk=fk_o, fv=fv_o, lf=lf_o, ckv=ckv_o, kpe=kpe_o)
        outS = dict(fk=fks_o, fv=fvs_o, lf=lfs_o, ckv=ckvs_o, kpe=kpes_o)

        _phaseA_raw(xall, DBG.get("nblk", NBLK), 4, True, outP, scrP, rope_kv, False)
        conv_flush()
        K.barrier()
        p0.close()
        if DBG["stop"] == "A1":
            return _done()
        with ExitStack() as pc:
            tmpc = T(pc, "tmpc", [8, NSLOT * NBLK], F32)
            K.op("dve", lambda e: e.tensor_tensor(out=tmpc[:].rearrange("h (s b) -> h s b", s=NSLOT), in0=selt[:].rearrange("h (s b) -> h s b", s=NSLOT),
                                                  in1=CB[:].rearrange("h (o b) -> h o b", o=1).broadcast_to([8, NSLOT, NBLK]), op=ALU.mult),
                 reads=selt.all + CB.all, writes=tmpc.all)
            K.op("dve", lambda e: e.tensor_reduce(out=CO[:], in_=tmpc[:].rearrange("h (s b) -> h s b", s=NSLOT), axis=mybir.AxisListType.X, op=ALU.add),
                 reads=tmpc.all, writes=CO.all)
            K.barrier()
        phaseA(xown, DBG.get("nslot", NSLOT), 4, False, None, scrP, rope_q, False)
        if DBG["stop"] == "A2":
            return _done()
        phaseA(xs, 1, 1, True, outS, scrS, rope_kv_s, True)
        phaseA(xs, 1, 1, False, None, scrS, rope_q_s, True)
        if DBG["stop"] == "A4":
            return _done()

        with ExitStack() as pS:
            kcbs = [T(pS, "kcb%d" % i, [128, 16, 512], BF16) for i in range(2)]
            ktS = [T(pS, "ktS%d" % i, [128, 2048], BF16) for i in range(2)]
            lfc = T(pS, "lfc", [128, 16 * 8], F32)
            lfs2 = T(pS, "lfs2", [128, 16 * 8], F32)
            ccbs = [T(pS, "ccb%d" % i, [128, 16, 256], BF16) for i in range(2)]
            ckT = T(pS, "ckT", [128, 2, 2048], BF16)
            kpb = T(pS, "kpb", [128, 16, 32], BF16)
            kpf = T(pS, "kpf", [128, 16, 32], F32)
            kpT = T(pS, "kpT", [32, 2048], BF16)
            vsbs = [T(pS, "vsb%d" % i, [128, 16, 512], BF16) for i in range(2)]
            vms = T(pS, "vms", [128, 16, 512], BF16)
            pT_ = [T(pS, "pTS%d" % i, [128, 1024], BF16, psum=True) for i in range(2)]
            pF_ = [T(pS, "pFS%d" % i, [128, 512], F32, psum=True) for i in range(2)]
            pB_ = T(pS, "pBS", [128, 512], F32, psum=True)
            npt = [0]
            npf = [0]
            for sidx in range(4):
                k0 = sidx * TS * 128
                kcb, ccb, vsb = kcbs[sidx % 2], ccbs[sidx % 2], vsbs[sidx % 2]
                K.dma("pool", kcb[:], ck[sidx].rearrange("(t p) c -> p t c", p=128), writes=kcb.all)
                for pr in range(4):
                    kt_ = ktS[pr % 2]
                    for half in range(2):
                        pt = pT_[npt[0] % 2]
                        npt[0] += 1
                        for j in range(8):
                            t = half * 8 + j
                            K.op("pe", lambda e, t=t, j=j, pr=pr, pt=pt: e.transpose(out=pt[:, j * 128:(j + 1) * 128], in_=kcb[:, t, pr * 128:(pr + 1) * 128], identity=identb[:]),
                                 reads=kcb.all + identb.all, writes=pt.all)
                        K.op("act", lambda e, half=half, pt=pt, kt_=kt_: e.activation(out=kt_[:, half * 1024:(half + 1) * 1024], in_=pt[:], func=AF.Copy), reads=pt.all, pwrites=kt_.all)
                    K.dma("sp", KTfs.ap[pr, :, k0:k0 + PAST], kt_[:], reads=kt_.all, pwrites=KTfs.all)
                if DBG.get("sstop") == 1:
                    break
                K.dma("pool", vsb[:], cv[sidx].rearrange("(t p) c -> p t c", p=128), writes=vsb.all)
                K.dma("sp", Vfs.ap[k0:k0 + PAST, :].rearrange("(t p) c -> p t c", p=128), vsb[:], reads=vsb.all, pwrites=Vfs.all)
                if DBG.get("sstop") == 2:
                    break
                K.dma("sp", lfc[:].rearrange("p (t h) -> p t h", h=8), clf[sidx].rearrange("(t p) h -> p t h", p=128), writes=lfc.all)
                K.op("dve", lambda e: e.memset(lfs2[:, 15 * 8:16 * 8], 0.0), pwrites=lfs2.all)
                for t in range(14, -1, -1):
                    K.op("dve", lambda e, t=t: e.tensor_tensor(out=lfs2[:, t * 8:(t + 1) * 8], in0=lfs2[:, (t + 1) * 8:(t + 2) * 8], in1=lfc[:, (t + 1) * 8:(t + 2) * 8], op=ALU.add),
                         reads=lfc.all + lfs2.all, writes=lfs2.all)
                K.op("pe", lambda e: e.matmul(pB_[:, 0:128], lhsT=suf[:], rhs=lfc[:], start=True, stop=False), reads=suf.all + lfc.all, writes=pB_.all)
                K.op("pe", lambda e: e.matmul(pB_[:, 0:128], lhsT=onesf[:], rhs=lfs2[:], start=False, stop=True), reads=onesf.all + lfs2.all, writes=pB_.all)
                K.op("dve", lambda e, sidx=sidx: e.tensor_copy(out=negcs[:, sidx * TS * 8:(sidx * TS + 16) * 8], in_=pB_[:, 0:128]), reads=pB_.all, pwrites=negcs.all)
                if DBG.get("sstop") == 3:
                    break
                K.dma("pool", ccb[:], cckv[sidx].rearrange("(t p) c -> p t c", p=128), writes=ccb.all)
                for kc in range(2):
                    for half in range(2):
                        pt = pT_[npt[0] % 2]
                        npt[0] += 1
                        for j in range(8):
                            t = half * 8 + j
                            K.op("pe", lambda e, t=t, j=j, kc=kc, pt=pt: e.transpose(out=pt[:, j * 128:(j + 1) * 128], in_=ccb[:, t, kc * 128:(kc + 1) * 128], identity=identb[:]),
                                 reads=ccb.all + identb.all, writes=pt.all)
                        K.op("act", lambda e, kc=kc, half=half, pt=pt: e.activation(out=ckT[:, kc, half * 1024:(half + 1) * 1024], in_=pt[:], func=AF.Copy), reads=pt.all, pwrites=ckT.all)
                if DBG.get("sstop") == 4:
                    break
                K.dma("sp", kpf[:], ckpe[sidx].rearrange("(t p) c -> p t c", p=128), writes=kpf.all)
                K.op("dve", lambda e: e.tensor_copy(out=kpb[:].rearrange("p t c -> p (t c)"), in_=kpf[:].rearrange("p t c -> p (t c)")), reads=kpf.all, writes=kpb.all)
                if DBG.get("sstop") == 9:
                    break
                for half in range(2):
                    pt = pT_[npt[0] % 2]
                    npt[0] += 1
                    for j in range(8):
                        t = half * 8 + j
                        K.op("pe", lambda e, t=t, j=j, pt=pt: e.transpose(out=pt[0:32, j * 128:(j + 1) * 128], in_=kpb[:, t, :], identity=identb[:]),
                             reads=kpb.all + identb.all, writes=pt.all)
                    K.op("act", lambda e, half=half, pt=pt: e.activation(out=kpT[:, half * 1024:(half + 1) * 1024], in_=pt[0:32, :], func=AF.Copy), reads=pt.all, pwrites=kpT.all)
                if DBG.get("sstop") == 8:
                    break
                K.dma("sp", KPEs.ap[:, k0:k0 + PAST], kpT[:], reads=kpT.all, pwrites=KPEs.all)
                if DBG.get("sstop") == 5:
                    break
                for pr in range(4):
                    kt_ = ktS[pr % 2]
                    for cb in range(4):
                        pf = pF_[npf[0] % 2]
                        npf[0] += 1
                        for kc in range(2):
                            K.op("pe", lambda e, pr=pr, cb=cb, kc=kc, pf=pf: e.matmul(pf[:], lhsT=w_kn_sb[:, kc, pr * 128:(pr + 1) * 128], rhs=ckT[:, kc, cb * 512:(cb + 1) * 512], start=(kc == 0), stop=(kc == 1)),
                                 reads=WA.all + ckT.all, writes=pf.all)
                        K.op("act", lambda e, cb=cb, pf=pf, kt_=kt_: e.activation(out=kt_[:, cb * 512:(cb + 1) * 512], in_=pf[:], func=AF.Copy), reads=pf.all, pwrites=kt_.all)
                    K.dma("sp", KTms.ap[pr, :, k0:k0 + PAST], kt_[:], reads=kt_.all, pwrites=KTms.all)
                if DBG.get("sstop") == 6:
                    break
                for t in range(16):
                    pf = pF_[npf[0] % 2]
                    npf[0] += 1
                    for kc in range(2):
                        K.op("pe", lambda e, t=t, kc=kc, pf=pf: e.matmul(pf[:], lhsT=ckT[:, kc, t * 128:(t + 1) * 128], rhs=w_kv_sb[:, kc, :], start=(kc == 0), stop=(kc == 1)),
                             reads=WA.all + ckT.all, writes=pf.all)
                    K.op("dve", lambda e, t=t, pf=pf: e.tensor_copy(out=vms[:, t, :], in_=pf[:]), reads=pf.all, pwrites=vms.all)
                K.dma("sp", Vms.ap[k0:k0 + PAST, :].rearrange("(t p) c -> p t c", p=128), vms[:], reads=vms.all, pwrites=Vms.all)
                if DBG.get("sstop") == 7:
                    break

        K.barrier()
        if DBG["stop"] == "S":
            return _done()
        with ExitStack() as pb:
            NKT = 4 * TS
            KTb = [T(pb, "KTb%d" % i, [96, NKS], BF16) for i in range(2)]
            Vb = [T(pb, "Vb%d" % i, [128, NKT, 65], BF16) for i in range(2)]
            QTb = [T(pb, "QTb%d" % i, [96, NQ], BF16) for i in range(2)]
            mk = T(pb, "mk", [128, 2 * 2 * 8 * 512], BF16)
            PT = [T(pb, "PT%d" % i, [128, 512], BF16) for i in range(6)]
            osb = [T(pb, "osb%d" % i, [128, 4, 64], F32) for i in range(2)]
            rcp = [T(pb, "rcp%d" % i, [128, 4], F32) for i in range(2)]
            pS_ = [T(pb, "pSB%d" % i, [128, 512], F32, psum=True) for i in range(6)]
            pO_raw = [T(pb, "pOB%d" % i, [128, 512], F32, psum=True) for i in range(2)]

            class _PO:
                def __init__(self, t):
                    self.t = t
                    self.all = t.all
                    self.v = t[:, 0:260].rearrange("p (s d) -> p s d", d=65)

                def __getitem__(self, idx):
                    return self.v[idx]
            pO_ = [_PO(t) for t in pO_raw]
            K.dma("sp", mk[:].rearrange("p (a n) -> p a n", n=512), masks_d.rearrange("g r t p n -> p (g r t) n"), writes=mk.all)
            for i in range(2):
                K.op("dve", lambda e, i=i: e.memset(KTb[i][64:65, :], 1.0), pwrites=KTb[i].all)
                K.op("dve", lambda e, i=i: e.memset(QTb[i][64:68, :], 1.0), pwrites=QTb[i].all)
                K.op("dve", lambda e, i=i: e.memset(Vb[i][:, :, 64:65], 1.0), pwrites=Vb[i].all)
            cnt = dict(u=0, o=0, po=0)
            fifo = []
            LAG = 5

            def attend(g, kt_ap, q_ap, W, kn, v_ap, bias_ap, bias_res, mask_ap, first, last, po, nq_sub, c0=0, last_fn=None):
                u = cnt["u"]
                cnt["u"] += 1
                ps = pS_[u % 6]
                pt = PT[u % 6]
                K.op("pe", lambda e: e.matmul(ps[0:kn, c0:W], lhsT=kt_ap, rhs=q_ap[:, c0:W], start=True, stop=(mask_ap is None)),
                     reads=g["kt"].all + g["qt"].all, writes=ps.all)
                if mask_ap is not None:
                    K.op("pe", lambda e: e.matmul(ps[0:kn, c0:W], lhsT=identb[0:kn, 0:kn], rhs=mask_ap[:, c0:W], start=False, stop=True),
                         reads=identb.all + mk.all + smask.all, writes=ps.all)
                if bias_ap is None:
                    K.op("act", lambda e: e.activation(out=pt[0:kn, c0:W], in_=ps[0:kn, c0:W], func=AF.Exp), reads=ps.all, writes=pt.all)
                else:
                    K.op("act", lambda e: e.activation(out=pt[0:kn, c0:W], in_=ps[0:kn, c0:W], func=AF.Exp, bias=bias_ap), reads=ps.all + bias_res, writes=pt.all)

                def stage2():
                    for sb in range(c0 // 128, nq_sub):
                        wq = min(128, W)
                        lst_ = last if last_fn is None else last_fn(sb)
                        K.op("pe", lambda e, sb=sb, lst_=lst_: e.matmul(po[0:wq, sb, :], lhsT=pt[0:kn, sb * 128:sb * 128 + wq], rhs=v_ap, start=(first and sb == 0), stop=lst_),
                             reads=pt.all + g["v"].all, writes=po.all)
                fifo.append(stage2)
                while len(fifo) > LAG:
                    fifo.pop(0)()

            def finish_o(po, wq, nq_sub, dst_ap, dst_dt):
                fifo.append(lambda: finish_o_now(po, wq, nq_sub, dst_ap, dst_dt))
                while len(fifo) > LAG:
                    fifo.pop(0)()

            def finish_o_now(po, wq, nq_sub, dst_ap, dst_dt):
                o = cnt["o"]
                cnt["o"] += 1
                rc = rcp[o % 2]
                ob = osb[o % 2]
                K.op("dve", lambda e: e.reciprocal(out=rc[0:wq, 0:nq_sub], in_=po[0:wq, 0:nq_sub, 64]), reads=po.all, writes=rc.all)
                K.op("dve", lambda e: e.tensor_tensor(out=ob[0:wq, 0:nq_sub, :], in0=po[0:wq, 0:nq_sub, 0:64],
                                                      in1=rc[0:wq, 0:nq_sub].rearrange("p (s o) -> p s o", o=1).broadcast_to([wq, nq_sub, 64]), op=ALU.mult),
                     reads=po.all + rc.all, writes=ob.all)
                K.dma("sp", dst_ap, ob[0:wq, 0:nq_sub, :], reads=ob.all, pwrites=dst_dt.all)

            def load_head(hh, sample):
                i = hh % 2
                kt, vt, qt = KTb[i], Vb[i], QTb[i]
                S = scrS if sample else scrP
                nk = NKS if sample else SEQ
                nkt = NKT if sample else 64
                nq = 128 if sample else NQ
                if hh < 8:
                    pr, two = hh // 2, hh % 2
                    K.dma("sp", kt[0:64, 0:nk], S["KTf"].ap[pr, two * 64:(two + 1) * 64, :], reads=S["KTf"].all, pwrites=kt.all)
                    K.dma("sp", vt[:, 0:nkt, 0:64], S["Vf"].ap[:, hh * 64:(hh + 1) * 64].rearrange("(t p) d -> p t d", p=128), reads=S["Vf"].all, pwrites=vt.all)
                    K.dma("sp", qt[0:64, 0:nq], S["QTf"].ap[pr, two * 64:(two + 1) * 64, :], reads=S["QTf"].all, pwrites=qt.all)
                    K.dma("sp", qt[64:65, 0:nq], S["CQ"].ap[hh:hh + 1, :], reads=S["CQ"].all, pwrites=qt.all)
                    if not sample:
                        K.dma("sp", kt[65:68, 0:nk], SKf.ap[:, hh, :], reads=SKf.all, pwrites=kt.all)
                else:
                    h = hh - 8
                    pr, two = h // 2, h % 2
                    K.dma("sp", kt[0:64, 0:nk], S["KTm"].ap[pr, two * 64:(two + 1) * 64, :], reads=S["KTm"].all, pwrites=kt.all)
                    K.dma("sp", kt[64:96, 0:nk], S["KPE"].ap, reads=S["KPE"].all, pwrites=kt.all)
                    K.dma("sp", vt[:, 0:nkt, 0:64], S["Vm"].ap[:, h * 64:(h + 1) * 64].rearrange("(t p) d -> p t d", p=128), reads=S["Vm"].all, pwrites=vt.all)
                    K.dma("sp", qt[0:96, 0:nq], S["QTm"].ap[h], reads=S["QTm"].all, pwrites=qt.all)
                return dict(kt=kt, v=vt, qt=qt)

            seq = [(hh, False) for hh in range(8)] + [(hh, True) for hh in range(8)] + [(hh, True) for hh in range(8, 16)] + [(hh, False) for hh in range(8, 16)]
            loaded = {}
            loaded[0] = load_head(*seq[0])
            for n, (hh, sample) in enumerate(seq):
                while fifo:
                    fifo.pop(0)()
                if n + 1 < len(seq):
                    loaded[n + 1] = load_head(*seq[n + 1])
                g = loaded.pop(n)
                fox = hh < 8
                R = 65 if fox else 96
                h = hh if fox else hh - 8
                gtype = 0 if fox else 1
                ocol = hh * 64
                if not sample:
                    if fox:
                        R = 68
                    for slot in range(NSLOT):
                        po = pO_[cnt["po"] % 2]
                        cnt["po"] += 1
                        nkt = 8 * (slot + 1)
                        for kt_i in range(nkt):
                            mask_ap = None
                            if kt_i >= 8 * slot:
                                mi = ((gtype * 2 + slot % 2) * 8 + (kt_i - 8 * slot)) * 512
                                mask_ap = mk[:, mi:mi + 512]
                            bias_ap = None
                            jd = kt_i - (8 * slot + 4)
                            c0 = 128 * jd if jd > 0 else 0
                            attend(g, g["kt"][0:R, kt_i * 128:(kt_i + 1) * 128], g["qt"][0:R, slot * 512:(slot + 1) * 512], 512, 128,
                                   g["v"][:, kt_i, :], bias_ap, negc.all, mask_ap, kt_i == 0, kt_i == nkt - 1, po, 4,
                                   c0=c0, last_fn=(lambda sb, kt_i=kt_i, slot=slot: kt_i == 8 * slot + 4 + sb))
                        finish_o(po, 128, 4, OS.ap[slot * 512:(slot + 1) * 512, ocol:ocol + 64].rearrange("(s p) d -> p s d", p=128), OS)
                else:
                    for sidx in range(4):
                        po = pO_[cnt["po"] % 2]
                        cnt["po"] += 1
                        for t in range(TS):
                            kn = 128 if t < 16 else 32
                            kt_i = sidx * TS + t
                            mask_ap = smask[0:32, :] if (fox and t == 16) else None
                            bias_ap = negcs[0:kn, kt_i * 8 + h:kt_i * 8 + h + 1] if fox else None
                            attend(g, g["kt"][0:R, kt_i * 128:kt_i * 128 + kn], g["qt"][0:R, sidx * 32:(sidx + 1) * 32], 32, kn,
                                   g["v"][0:kn, kt_i, :], bias_ap, negcs.all, mask_ap, t == 0, t == TS - 1, po, 1)
                        finish_o(po, 32, 1, OSs.ap[sidx * 32:(sidx + 1) * 32, ocol:ocol + 64].rearrange("(s p) d -> p s d", p=32), OSs)
            while fifo:
                fifo.pop(0)()

        K.barrier()
        if DBG["stop"] == "B":
            return _done()
        K.dma("sp", w_out_sb, WOUT.ap, reads=WOUT.all + WA.all, writes=WA.all)
        K.dma("sp", w_gate_sb, WGATE.ap, reads=WGATE.all, pwrites=WA.all)
        K.dma("sp", w_proj_sb, WPROJ.ap, reads=WPROJ.all, pwrites=WA.all)

        def run_all(gens):
            live = list(gens)
            while live:
                for g_ in list(live):
                    try:
                        next(g_)
                    except StopIteration:
                        live.remove(g_)

        class View:
            def __init__(self, ap, res):
                self.ap_ = ap
                self.all = res

            def __getitem__(self, idx):
                return self.ap_[idx]

        def phaseC(x_src, p_src, o_src, y_dst, nblk, nsub):
            W = nsub * 128
            with ExitStack() as pc:
                xb = T(pc, "xC", [128, nsub, D], F32, nres=nsub)
                ob = T(pc, "oC", [128, nsub * D], F32)
                pb_ = T(pc, "pC", [128, nsub, 256], F32)
                abs_ = [T(pc, "aC%d" % i, [128, D], BF16) for i in range(nsub)]
                aT = T(pc, "aTC", [128, 8, W], BF16, nres=nsub)
                pT2 = T(pc, "pT2C", [128, 2, W], BF16, nres=nsub)
                u2T = T(pc, "u2TC", [128, 32, W], BF16)
                rl = [T(pc, "rlC%d" % i, [128, W], F32) for i in range(2)]
                tmp_real = [T(pc, "tmpC%d" % i, [128, D], F32) for i in range(2)]
                junk = T(pc, "junkC", [128, D], BF16)
                sts = [T(pc, "stC%d" % i, [128, 16], F32) for i in range(nsub)]
                wus = [T(pc, "wusC%d" % i, [128, 8, 512], BF16) for i in range(2)]
                wds = [T(pc, "wdsC%d" % i, [128, 32, 128], BF16) for i in range(2)]
                pq = [T(pc, "pqC%d" % i, [128, 1024], F32, psum=True, nres=2) for i in range(4)]
                tmp = [tmp_real[0], tmp_real[1]]
                for i in range(2):
                    tmp.append(View(wds[i][:].rearrange("p c w -> p (c w)").bitcast(F32)[:, 0:D], wds[i].all))
                ptrv = [View(pq[i][:, 0:512].bitcast(BF16), [pq[i].res[0]]) for i in range(4)]
                pfm = [View(pq[i][:, 0:512], [pq[i].res[0]]) for i in range(2)]
                c = dict(fm=0)
                u2flat = u2T[:].rearrange("p c w -> p (c w)").bitcast(F32)
                ostg = u2flat[:, 0:nsub * D]
                xstg = u2flat[:, nsub * D:2 * nsub * D].rearrange("p (s d) -> p s d", s=nsub)

                def prefetch(blk):
                    K.dma("sp", ostg.rearrange("p (s d) -> p s d", s=nsub), o_src.ap[blk * W:(blk + 1) * W, :].rearrange("(s p) d -> p s d", p=128), reads=o_src.all, writes=u2T.all)
                    K.dma("sp", xstg, x_src[blk * W:(blk + 1) * W, :].rearrange("(s p) d -> p s d", p=128), pwrites=u2T.all)

                def transposes(src_tile, ncol_chunks, dstT, s, eng="act"):
                    pt = ptrv[s]
                    for kc in range(ncol_chunks):
                        K.op("pe", lambda e, kc=kc: e.transpose(out=pt[:, kc * 128:(kc + 1) * 128], in_=src_tile[:, kc * 128:(kc + 1) * 128], identity=identb[:]),
                             reads=src_tile.all + identb.all, writes=pt.all)
                    yield
                    if eng == "act":
                        K.op("act", lambda e: e.activation(out=dstT[:, :, s * 128:(s + 1) * 128], in_=pt[:, 0:ncol_chunks * 128].rearrange("p (k n) -> p k n", k=ncol_chunks), func=AF.Copy),
                             reads=pt.all, writes=[dstT.res[s]])
                    else:
                        K.op("dve", lambda e: e.tensor_copy(out=dstT[:, :, s * 128:(s + 1) * 128], in_=pt[:, 0:ncol_chunks * 128].rearrange("p (k n) -> p k n", k=ncol_chunks)),
                             reads=pt.all, writes=[dstT.res[s]])
                    yield

                def post_norm_residual(pk_ap, pk_res, s, gi, xin=None, xin_res=None):
                    st = sts[s]
                    K.op("act", lambda e: e.activation(out=junk[:], in_=pk_ap, func=AF.Square, accum_out=st[:, 0:1]), reads=pk_res, writes=st.all)
                    yield
                    rstd_from_ss(st[:, 0:1], D, st[:, 2:3], st[:, 1:2], st.all, st.all)
                    yield
                    tm = tmp[s]
                    K.op("dve", lambda e: e.scalar_tensor_tensor(out=tm[:], in0=pk_ap, scalar=st[:, 2:3], in1=gbc[:, gi * D:(gi + 1) * D], op0=ALU.mult, op1=ALU.mult),
                         reads=pk_res + st.all + gbc.all, writes=tm.all)
                    yield
                    if xin is None:
                        K.op("dve", lambda e: e.tensor_tensor(out=xb[:, s, :], in0=xb[:, s, :], in1=tm[:], op=ALU.add), reads=tm.all, writes=[xb.res[s]])
                    else:
                        K.op("dve", lambda e: e.tensor_tensor(out=xb[:, s, :], in0=xin, in1=tm[:], op=ALU.add), reads=tm.all + xin_res, writes=[xb.res[s]])
                    yield

                def pre_norm_T(s, dstT):
                    st = sts[s]
                    ab = abs_[s]
                    K.op("act", lambda e: e.activation(out=junk[:], in_=xb[:, s, :], func=AF.Square, accum_out=st[:, 4:5]), reads=[xb.res[s]], writes=st.all)
                    yield
                    rstd_from_ss(st[:, 4:5], D, st[:, 6:7], st[:, 5:6], st.all, st.all)
                    yield
                    K.op("dve", lambda e: e.tensor_scalar(out=ab[:], in0=xb[:, s, :], scalar1=st[:, 6:7], scalar2=None, op0=ALU.mult), reads=[xb.res[s]] + st.all, writes=ab.all)
                    yield
                    yield from transposes(ab, 8, dstT, s, eng="dve")

                def front(s):
                    st = sts[s]
                    ab = abs_[s]
                    for hf in range(2):
                        K.op("act", lambda e, hf=hf: e.activation(out=junk[:, 0:512], in_=ostg[:, s * D + hf * 512:s * D + (hf + 1) * 512], func=AF.Square, accum_out=st[:, 8 + hf:9 + hf]),
                             reads=u2T.all, writes=st.all)
                    yield
                    rstd_from_ss(st[:, 8:10], 512, st[:, 12:14], st[:, 10:12], st.all, st.all)
                    yield
                    for hf in range(2):
                        K.op("dve", lambda e, hf=hf: e.tensor_scalar(out=ab[:, hf * 512:(hf + 1) * 512], in0=ostg[:, s * D + hf * 512:s * D + (hf + 1) * 512], scalar1=st[:, 12 + hf:13 + hf], scalar2=None, op0=ALU.mult),
                             reads=u2T.all + st.all, writes=ab.all)
                    yield
                    yield from transposes(ab, 8, aT, s)
                    pk = pq[s]
                    for hf in range(2):
                        for kc in range(8):
                            K.op("pe", lambda e, hf=hf, kc=kc: e.matmul(pk[:, hf * 512:(hf + 1) * 512], lhsT=aT[:, kc, s * 128:(s + 1) * 128], rhs=w_out_sb[:, kc, hf * 512:(hf + 1) * 512], start=(kc == 0), stop=(kc == 7)),
                                 reads=[aT.res[s]] + WA.all, writes=[pk.res[hf]])
                    yield
                    yield from post_norm_residual(pk[:], pk.all, s, 0, xin=xstg[:, s, :], xin_res=u2T.all)
                    yield from pre_norm_T(s, aT)

                def back(s, blk, yT):
                    ab = abs_[s]
                    pk = pq[s]
                    for n_ in range(8):
                        K.op("pe", lambda e, n_=n_: e.transpose(out=pk[:, n_ * 128:(n_ + 1) * 128], in_=yT[:, n_, s * 128:(s + 1) * 128], identity=identf[:]),
                             reads=ob.all + identf.all, writes=[pk.res[n_ // 4]])
                    yield
                    yield from post_norm_residual(pk[:], pk.all, s, 1)
                    yield from pre_norm_T(s, aT)
                    K.op("pool", lambda e: e.tensor_copy(out=ab[:, 0:256], in_=pb_[:, s, :]), reads=pb_.all, writes=ab.all)
                    yield
                    yield from transposes(ab, 2, pT2, s)
                    for hf in range(2):
                        for kc in range(8):
                            K.op("pe", lambda e, hf=hf, kc=kc: e.matmul(pk[:, hf * 512:(hf + 1) * 512], lhsT=aT[:, kc, s * 128:(s + 1) * 128], rhs=w_gate_sb[:, kc, hf * 512:(hf + 1) * 512], start=(kc == 0), stop=(kc == 7)),
                                 reads=[aT.res[s]] + WA.all, writes=[pk.res[hf]])
                    yield
                    tg = tmp[s]
                    K.op("act", lambda e: e.activation(out=tg[:], in_=pk[:], func=AF.Exp, scale=-1.0), reads=pk.all, writes=tg.all)
                    yield
                    K.op("act", lambda e: e.activation(out=tg[:], in_=tg[:], func=AF.Ln, bias=1.0), reads=tg.all, writes=tg.all)
                    yield
                    K.op("act", lambda e: e.activation(out=tg[:], in_=tg[:], func=AF.Exp, scale=-1.0), reads=tg.all, writes=tg.all)
                    for hf in range(2):
                        for kc in range(2):
                            K.op("pe", lambda e, hf=hf, kc=kc: e.matmul(pk[:, hf * 512:(hf + 1) * 512], lhsT=pT2[:, kc, s * 128:(s + 1) * 128], rhs=w_proj_sb[:, kc, hf * 512:(hf + 1) * 512], start=(kc == 0), stop=(kc == 1)),
                                 reads=[pT2.res[s]] + WA.all, writes=[pk.res[hf]])
                    yield
                    K.op("dve", lambda e: e.tensor_tensor(out=tg[:], in0=pk[:], in1=tg[:], op=ALU.mult), reads=pk.all + tg.all, writes=tg.all)
                    yield
                    yield from post_norm_residual(tg[:], tg.all, s, 2)
                    K.dma("sp", y_dst[blk * W + s * 128:blk * W + (s + 1) * 128, :], xb[:, s, :], reads=[xb.res[s]])

                prefetch(0)
                for blk in range(nblk):
                    K.dma("sp", pb_[:], p_src[blk * W:(blk + 1) * W, :].rearrange("(s p) d -> p s d", p=128), writes=pb_.all)
                    run_all([front(s) for s in range(nsub)])
                    for cc in range(8):
                        wu = wus[cc % 2]
                        K.dma("sp", wu[:], WUP.ap[cc], reads=WUP.all, writes=wu.all)
                        for j in range(4):
                            ch = cc * 4 + j
                            pf = pfm[c["fm"] % 2]
                            c["fm"] += 1
                            for kc in range(8):
                                K.op("pe", lambda e, j=j, kc=kc: e.matmul(pf[:, 0:W], lhsT=wu[:, kc, j * 128:(j + 1) * 128], rhs=aT[:, kc, :], start=(kc == 0), stop=(kc == 7)),
                                     reads=wu.all + aT.all, writes=pf.all)
                            r_ = rl[ch % 2]
                            K.op("act", lambda e: e.activation(out=r_[:], in_=pf[:, 0:W], func=AF.Relu), reads=pf.all, writes=r_.all)
                            K.op("pool", lambda e, ch=ch: e.tensor_tensor(out=u2T[:, ch, :], in0=r_[:], in1=r_[:], op=ALU.mult), reads=r_.all, pwrites=u2T.all)
                    yT = ob[:].rearrange("p (n t) -> p n t", n=8)
                    for n_ in range(8):
                        wd = wds[n_ % 2]
                        K.dma("sp", wd[:], WDN.ap[n_], reads=WDN.all, writes=wd.all)
                        pf = pfm[c["fm"] % 2]
                        c["fm"] += 1
                        for kc in range(32):
                            K.op("pe", lambda e, kc=kc: e.matmul(pf[:, 0:W], lhsT=wd[:, kc, :], rhs=u2T[:, kc, :], start=(kc == 0), stop=(kc == 31)),
                                 reads=wd.all + u2T.all, writes=pf.all)
                        K.op("act", lambda e, n_=n_: e.activation(out=yT[:, n_, :], in_=pf[:, 0:W], func=AF.Copy), reads=pf.all, pwrites=ob.all)
                    if blk + 1 < nblk:
                        prefetch(blk + 1)
                    run_all([back(s, blk, yT) for s in range(nsub)])

        phaseC(xs, ps_in, OSs, ys_o, 1, 1)
        K.barrier()
        phaseC(xown, pown, OS, y_own, DBG.get("nslot", NSLOT), 4)
        K.finish()
        global _LASTK
        _LASTK = K
    return nc


_NC = None


def _own_blocks(r):
    out = []
    for i in range(NSLOT):
        odd = (i % 2 == 1)
        if r == 0:
            out.append(2 * i + (1 if odd else 0))
        else:
            out.append(2 * i + (0 if odd else 1))
    return out


def _consts():
    half = 16
    inv = (10000.0 ** (-np.arange(half, dtype=np.float32) / half)).astype(np.float32)

    def tables(pos):
        ang = pos.astype(np.float32)[:, None] * inv[None, :]
        return np.cos(ang).astype(np.float32), np.sin(ang).astype(np.float32)

    return tables


def _masks():
    k = np.arange(128)[:, None]
    q = np.arange(512)[None, :]
    out = np.zeros((2, 2, 8, 128, 512), np.float32)
    for g in range(2):
        for kind in range(2):
            for t in range(8):
                if kind == 0:
                    if t < 4:
                        m = np.zeros((128, 512), bool)
                    else:
                        kk = (t - 4) * 128 + k
                        m = (kk > q) if g == 0 else ((kk // 64) > (q // 64))
                else:
                    if t < 4:
                        kk = t * 128 + k
                        m = (kk > q) if g == 0 else ((kk // 64) > (q // 64))
                    else:
                        m = np.ones((128, 512), bool)
                out[g, kind, t] = np.where(m, NEG, 0.0)
    return out


def _prep(inp):
    f32 = np.float32
    bf = ml_dtypes.bfloat16
    tables = _consts()
    x_prompt = np.asarray(inp["x_prompt"], f32)
    x_sample = np.asarray(inp["x_sample"], f32)
    p_prompt = np.asarray(inp["p_prompt"], f32)[0]
    p_sample = np.asarray(inp["p_sample"], f32)[0]
    ident = np.eye(128, dtype=f32)
    su = (np.arange(128)[:, None] > np.arange(128)[None, :]).astype(f32)
    cos_all, sin_all = tables(np.arange(SEQ))
    rope_kv = np.concatenate([cos_all, cos_all, -sin_all, sin_all], axis=1).astype(f32)
    pos_s = PAST + (np.arange(128) % 32)
    cos_s, sin_s = tables(pos_s)
    rope_kv_s = np.concatenate([cos_s, cos_s, -sin_s, sin_s], axis=1).astype(f32)
    rope_q_s = np.concatenate([cos_s, sin_s], axis=1).astype(f32)
    masks_all = _masks()
    smask = np.zeros((128, 32), f32)
    smask[:32] = np.where(np.arange(32)[:, None] > np.arange(32)[None, :], NEG, 0.0)
    shared = dict(
        w_in=np.ascontiguousarray(inp["w_in"][0], f32), w_qup=np.ascontiguousarray(inp["w_mla_q_up"][0], f32),
        w_kvup=np.ascontiguousarray(inp["w_mla_kv_up"][0], f32), w_out=np.ascontiguousarray(inp["w_out"][0], f32),
        w_up=np.ascontiguousarray(inp["w_up"][0], f32), w_down=np.ascontiguousarray(inp["w_down"][0], f32),
        w_gate=np.ascontiguousarray(inp["w_ple_gate"][0], f32), w_proj=np.ascontiguousarray(inp["w_ple_proj"][0], f32),
        g_pre_mix=np.ascontiguousarray(inp["g_pre_mix"][0], f32), g_mla_q=np.ascontiguousarray(inp["g_mla_q"][0], f32),
        g_mla_kv=np.ascontiguousarray(inp["g_mla_kv"][0], f32),
        g_out=np.concatenate([inp["g_fox_out"][0], inp["g_mla_out"][0]]).astype(f32),
        g_post_mix=np.ascontiguousarray(inp["g_post_mix"][0], f32), g_pre_mlp=np.ascontiguousarray(inp["g_pre_mlp"][0], f32),
        g_post_mlp=np.ascontiguousarray(inp["g_post_mlp"][0], f32), g_ple=np.ascontiguousarray(inp["g_ple"][0], f32),
        g_ple_post=np.ascontiguousarray(inp["g_ple_post"][0], f32), b_f=np.ascontiguousarray(inp["b_fox_f"][0], f32),
        ident=ident, su=su, rope_kv=rope_kv, rope_kv_s=rope_kv_s, rope_q_s=rope_q_s, smask=smask.astype(bf),
    )
    in_maps = []
    owns = []
    for c in range(8):
        b, r = c // 2, c % 2
        own = _own_blocks(r)
        owns.append(own)
        rows = np.concatenate([np.arange(o * 512, (o + 1) * 512) for o in own])
        cq, sq = cos_all[rows], sin_all[rows]
        sel = np.zeros((NSLOT, NBLK), f32)
        for i, o in enumerate(own):
            sel[i, o] = 1.0
        mk = np.zeros((2, 2, 8, 128, 512), f32)
        for par in range(2):
            kind = 0 if own[par] == 2 * par + 1 else 1
            mk[:, par] = masks_all[:, kind]
        m = dict(shared)
        m.update(
            xall=np.ascontiguousarray(x_prompt[b]), xown=np.ascontiguousarray(x_prompt[b][rows]),
            pown=np.ascontiguousarray(p_prompt[b][rows]),
            xs=np.ascontiguousarray(x_sample[4 * c:4 * c + 4].reshape(128, D)),
            ps=np.ascontiguousarray(p_sample[4 * c:4 * c + 4].reshape(128, 256)),
            ck=np.ascontiguousarray(inp["cache_fox_k"][0, 4 * c:4 * c + 4].reshape(4, PAST, 512), f32),
            cv=np.ascontiguousarray(inp["cache_fox_v"][0, 4 * c:4 * c + 4].reshape(4, PAST, 512), f32),
            clf=np.ascontiguousarray(inp["cache_fox_logf"][0, 4 * c:4 * c + 4], f32),
            cckv=np.ascontiguousarray(inp["cache_mla_ckv"][0, 4 * c:4 * c + 4], f32),
            ckpe=np.ascontiguousarray(inp["cache_mla_kpe"][0, 4 * c:4 * c + 4], f32),
            rope_q=np.concatenate([cq * SC_M, sq * SC_M], axis=1).astype(f32),
            masks=mk.astype(bf),
            sel=np.ascontiguousarray(np.broadcast_to(sel.reshape(1, -1), (8, NSLOT * NBLK)), f32),
        )
        m["rope_q_s"] = (rope_q_s * SC_M).astype(f32)
        in_maps.append(m)
    return in_maps, owns


def kernel(**inp):
    global _NC
    if _NC is None:
        _NC = build()
    nc = _NC
    f32 = np.float32
    in_maps, owns = _prep(inp)
    res = run_bass_kernel_spmd(nc, in_maps, core_ids=list(range(8)))
    R = res.results
    y_p = np.zeros((4, SEQ, D), f32)
    for c in range(8):
        b = c // 2
        for i, o in enumerate(owns[c]):
            y_p[b, o * 512:(o + 1) * 512] = R[c]["y_own"][i * 512:(i + 1) * 512]
    y_s = np.concatenate([R[c]["ys_o"].reshape(4, 32, D) for c in range(8)], axis=0)
    fk_p = np.stack([R[2 * b]["fk_o"] for b in range(4)]).reshape(1, 4, SEQ, 8, 64)
    fv_p = np.stack([R[2 * b]["fv_o"] for b in range(4)]).reshape(1, 4, SEQ, 8, 64)
    lf_p = np.stack([R[2 * b]["lf_o"] for b in range(4)]).reshape(1, 4, SEQ, 8)
    ckv_p = np.stack([R[2 * b]["ckv_o"] for b in range(4)]).reshape(1, 4, SEQ, 256)
    kpe_p = np.stack([R[2 * b]["kpe_o"] for b in range(4)]).reshape(1, 4, SEQ, 32)
    fk_s = np.concatenate([R[c]["fks_o"].reshape(4, 32, 8, 64) for c in range(8)], axis=0)[None]
    fv_s = np.concatenate([R[c]["fvs_o"].reshape(4, 32, 8, 64) for c in range(8)], axis=0)[None]
    lf_s = np.concatenate([R[c]["lfs_o"].reshape(4, 32, 8) for c in range(8)], axis=0)[None]
    ckv_s = np.concatenate([R[c]["ckvs_o"].reshape(4, 32, 256) for c in range(8)], axis=0)[None]
    kpe_s = np.concatenate([R[c]["kpes_o"].reshape(4, 32, 32) for c in range(8)], axis=0)[None]
    return (y_p, y_s, fk_p, fv_p, lf_p, ckv_p, kpe_p, fk_s, fv_s, lf_s, ckv_s, kpe_s)
```

```python
from contextlib import ExitStack
import math
import numpy as np
import ml_dtypes
import concourse.bass as bass
import concourse.mybir as mybir
from concourse.bass_utils import run_bass_kernel_spmd

F32 = mybir.dt.float32
BF16 = mybir.dt.bfloat16
AF = mybir.ActivationFunctionType
ALU = mybir.AluOpType

NDMA_SEMS = 8
EPS = 1e-6
NEG = -30000.0


class Res:
    __slots__ = ("name", "w", "r", "excl")

    def __init__(self, name):
        self.name = name
        self.w = []
        self.r = []
        self.excl = False


class Eng:
    def __init__(self, name, handle):
        self.name = name
        self.h = handle
        self.n = 0
        self.sem = None
        self.seen = {}
        self.ndma = 0
        self.dsems = []


class Kern:
    def __init__(self, nc, stack):
        self.nc = nc
        self.stack = stack
        self.engs = {}
        for name, h in (("pe", nc.tensor), ("act", nc.scalar), ("dve", nc.vector),
                        ("pool", nc.gpsimd), ("sp", nc.sync)):
            e = Eng(name, h)
            e.sem = stack.enter_context(nc.semaphore("s_" + name))
            self.engs[name] = e
        for qn in ("sp", "pool"):
            e = self.engs[qn]
            e.dsems = [stack.enter_context(nc.semaphore("d_%s%d" % (qn, i))) for i in range(NDMA_SEMS)]

    def _wait(self, eng, tok):
        sem, val, src, idx = tok
        key = id(sem)
        if eng.seen.get(key, 0) >= val:
            return
        if src is eng and idx is not None:
            if eng.name == "pe" or idx < eng.n - 2:
                return
        eng.h.wait_ge(sem, val)
        eng.seen[key] = val

    def _deps(self, eng, reads, writes, pwrites):
        for r in reads:
            for t in r.w:
                self._wait(eng, t)
        for r in writes:
            for t in r.w:
                self._wait(eng, t)
            for t in r.r:
                self._wait(eng, t)
        for r in pwrites:
            for t in r.r:
                self._wait(eng, t)

    def _commit(self, tok, reads, writes, pwrites):
        for r in reads:
            if tok[3] is not None:
                r.r = [t for t in r.r if not (t[2] is tok[2] and t[3] is not None)]
            r.r.append(tok)
        for r in writes:
            r.w = [tok]
            r.r = []
        for r in pwrites:
            if r.r:
                r.w = [tok]
                r.r = []
            else:
                r.w.append(tok)

    def op(self, engname, fn, reads=(), writes=(), pwrites=()):
        eng = self.engs[engname]
        if any(r.excl for r in reads):
            writes = list(writes) + [r for r in reads if r.excl and r not in writes]
            reads = [r for r in reads if not r.excl]
        self._deps(eng, reads, writes, pwrites)
        ins = fn(eng.h)
        ins.then_inc(eng.sem, 1)
        idx = eng.n
        eng.n += 1
        tok = (eng.sem, idx + 1, eng, idx)
        self._commit(tok, reads, writes, pwrites)
        return tok

    def dma(self, qname, out, in_, reads=(), writes=(), pwrites=()):
        eng = self.engs[qname]
        i = eng.ndma
        sem = eng.dsems[i % NDMA_SEMS]
        prev = 16 * (i // NDMA_SEMS)
        if prev > 0 and eng.seen.get(id(sem), 0) < prev:
            eng.h.wait_ge(sem, prev)
            eng.seen[id(sem)] = prev
        self._deps(eng, reads, writes, pwrites)
        ins = eng.h.dma_start(out=out, in_=in_)
        ins.then_inc(sem, 16)
        eng.ndma += 1
        tok = (sem, 16 * (i // NDMA_SEMS + 1), eng, None)
        self._commit(tok, reads, writes, pwrites)
        return tok

    def barrier(self):
        for e in self.engs.values():
            for f in self.engs.values():
                if f is not e and f.n > 0 and e.seen.get(id(f.sem), 0) < f.n:
                    e.h.wait_ge(f.sem, f.n)
                    e.seen[id(f.sem)] = f.n
            for qn in ("sp", "pool"):
                q = self.engs[qn]
                for j in range(min(q.ndma, NDMA_SEMS)):
                    last = ((q.ndma - 1 - j) // NDMA_SEMS) * NDMA_SEMS + j
                    val = 16 * (last // NDMA_SEMS + 1)
                    sem = q.dsems[j]
                    if e.seen.get(id(sem), 0) < val:
                        e.h.wait_ge(sem, val)
                        e.seen[id(sem)] = val

    def finish(self):
        sp = self.engs["sp"]
        for qn in ("sp", "pool"):
            e = self.engs[qn]
            for j in range(min(e.ndma, NDMA_SEMS)):
                last = ((e.ndma - 1 - j) // NDMA_SEMS) * NDMA_SEMS + j
                val = 16 * (last // NDMA_SEMS + 1)
                sem = e.dsems[j]
                if sp.seen.get(id(sem), 0) < val:
                    sp.h.wait_ge(sem, val)
                    sp.seen[id(sem)] = val
        for name in ("pe", "act", "dve", "pool"):
            e = self.engs[name]
            if e.n > 0 and sp.seen.get(id(e.sem), 0) < e.n:
                sp.h.wait_ge(e.sem, e.n)
                sp.seen[id(e.sem)] = e.n


class Tile:
    uid = 0

    def __init__(self, K, stack, name, shape, dtype, nres=1, psum=False):
        nc = K.nc
        Tile.uid += 1
        name = "t%d_%s" % (Tile.uid, name)
        if psum:
            self.t = stack.enter_context(nc.psum_tensor(name, shape, dtype))
        else:
            self.t = stack.enter_context(nc.sbuf_tensor(name, shape, dtype))
        self.res = [Res("%s.%d" % (name, i)) for i in range(nres)]
        if psum:
            for r in self.res:
                r.excl = True

    def __getitem__(self, idx):
        return self.t[idx]

    @property
    def all(self):
        return self.res


class DT:
    def __init__(self, ap, name):
        self.ap = ap
        self.res = [Res(name)]

    @property
    def all(self):
        return self.res


D = 1024
NBLK = 16
NSLOT = 8
SEQ = 8192
NQ = NSLOT * 512
PAST = 2048
TS = 17
NKS = 4 * TS * 128
DIN = 2216
SC_F = 1.0 / 8.0
SC_M = 1.0 / math.sqrt(96.0)


DBG = dict(stop=None)


def build():
    nc = bass.Bass("TRN2", target_bir_lowering=False)

    def din(name, shape, dt=F32):
        return nc.dram_tensor(name, shape, dt, kind="ExternalInput").ap()

    def dout(name, shape):
        return nc.dram_tensor(name, shape, F32, kind="ExternalOutput").ap()

    def dscr(name, shape, dt=BF16):
        return DT(nc.dram_tensor(name, shape, dt, kind="Internal").ap(), name)

    xall = din("xall", [SEQ, D]); xown = din("xown", [NQ, D]); pown = din("pown", [NQ, 256])
    xs = din("xs", [128, D]); ps_in = din("ps", [128, 256])
    ck = din("ck", [4, PAST, 512]); cv = din("cv", [4, PAST, 512]); clf = din("clf", [4, PAST, 8])
    cckv = din("cckv", [4, PAST, 256]); ckpe = din("ckpe", [4, PAST, 32])
    w_in = din("w_in", [D, DIN]); w_qup = din("w_qup", [384, 768]); w_kvup = din("w_kvup", [256, 1024])
    w_out = din("w_out", [D, D]); w_up = din("w_up", [D, 4096]); w_down = din("w_down", [4096, D])
    w_gate = din("w_gate", [D, D]); w_proj = din("w_proj", [256, D])
    g_pre_mix = din("g_pre_mix", [D]); g_mla_q = din("g_mla_q", [384]); g_mla_kv = din("g_mla_kv", [256])
    g_out = din("g_out", [D]); g_post_mix = din("g_post_mix", [D]); g_pre_mlp = din("g_pre_mlp", [D])
    g_post_mlp = din("g_post_mlp", [D]); g_ple = din("g_ple", [D]); g_ple_post = din("g_ple_post", [D])
    b_f = din("b_f", [8])
    ident_d = din("ident", [128, 128]); su_d = din("su", [128, 128])
    rope_kv = din("rope_kv", [SEQ, 64]); rope_q = din("rope_q", [NQ, 32])
    rope_kv_s = din("rope_kv_s", [128, 64]); rope_q_s = din("rope_q_s", [128, 32])
    masks_d = din("masks", [2, 2, 8, 128, 512], BF16); smask_d = din("smask", [128, 32], BF16)
    sel_d = din("sel", [8, NSLOT * NBLK])

    y_own = dout("y_own", [NQ, D]); fk_o = dout("fk_o", [SEQ, 512]); fv_o = dout("fv_o", [SEQ, 512])
    lf_o = dout("lf_o", [SEQ, 8]); ckv_o = dout("ckv_o", [SEQ, 256]); kpe_o = dout("kpe_o", [SEQ, 32])
    ys_o = dout("ys_o", [128, D]); fks_o = dout("fks_o", [128, 512]); fvs_o = dout("fvs_o", [128, 512])
    lfs_o = dout("lfs_o", [128, 8]); ckvs_o = dout("ckvs_o", [128, 256]); kpes_o = dout("kpes_o", [128, 32])

    KTf = dscr("KTf", [4, 128, SEQ]); KTm = dscr("KTm", [4, 128, SEQ]); KPE = dscr("KPE", [32, SEQ])
    Vf = dscr("Vf", [SEQ, 512]); Vm = dscr("Vm", [SEQ, 512])
    QTf = dscr("QTf", [4, 128, NQ]); QTm = dscr("QTm", [8, 96, NQ]); CQ = dscr("CQ", [8, NQ])
    OS = dscr("OS", [NQ, D], F32)
    SKf = dscr("SKf", [3, 8, SEQ])
    KTfs = dscr("KTfs", [4, 128, NKS]); KTms = dscr("KTms", [4, 128, NKS]); KPEs = dscr("KPEs", [32, NKS])
    Vfs = dscr("Vfs", [NKS, 512]); Vms = dscr("Vms", [NKS, 512])
    QTfs = dscr("QTfs", [4, 128, 128]); QTms = dscr("QTms", [8, 96, 128]); CQs = dscr("CQs", [8, 128])
    OSs = dscr("OSs", [128, D], F32)
    WOUT = dscr("WOUT", [128, 8, D]); WGATE = dscr("WGATE", [128, 8, D]); WPROJ = dscr("WPROJ", [128, 2, D])
    WUP = dscr("WUP", [8, 128, 8, 512]); WDN = dscr("WDN", [8, 128, 32, 128])

    with ExitStack() as top:
        top.enter_context(nc.allow_non_contiguous_dma(reason="small strided layout DMAs"))
        K = Kern(nc, top)

        def T(stack, name, shape, dt, nres=1, psum=False):
            return Tile(K, stack, name, shape, dt, nres, psum)

        identf = T(top, "identf", [128, 128], F32)
        identb = T(top, "identb", [128, 128], BF16)
        suf = T(top, "suf", [128, 128], F32)
        onesf = T(top, "onesf", [128, 128], F32)
        onesrow = T(top, "onesrow", [8, 512], F32)
        WA = T(top, "WA", [128, 22656], BF16)
        gcol = T(top, "gcol", [128, 48], F32)
        gbc = T(top, "gbc", [128, 3 * D + 256], F32)
        bcol = T(top, "bcol", [8, 2], F32)
        negc = T(top, "negc", [128, 64 * 8], F32)
        negcs = T(top, "negcs", [128, 4 * TS * 8], F32)
        CB = T(top, "CB", [8, NBLK], F32)
        CO = T(top, "CO", [8, NSLOT], F32)
        selt = T(top, "selt", [8, NSLOT * NBLK], F32)
        smask = T(top, "smask", [128, 32], BF16)

        K.dma("sp", identf[:], ident_d, writes=identf.all)
        K.dma("sp", suf[:], su_d, writes=suf.all)
        K.dma("sp", selt[:], sel_d, writes=selt.all)
        K.dma("sp", smask[:], smask_d, writes=smask.all)
        K.op("dve", lambda e: e.tensor_copy(out=identb[:], in_=identf[:]), reads=identf.all, writes=identb.all)
        K.op("dve", lambda e: e.memset(onesf[:], 1.0), writes=onesf.all)
        K.op("dve", lambda e: e.memset(onesrow[:], 1.0), writes=onesrow.all)
        K.op("dve", lambda e: e.memset(CB[:], 0.0), writes=CB.all)
        K.dma("sp", gcol[:, 0:8], g_pre_mix.rearrange("(k p) -> p k", p=128), pwrites=gcol.all)
        K.dma("sp", gcol[:, 8:11], g_mla_q.rearrange("(k p) -> p k", p=128), pwrites=gcol.all)
        K.dma("sp", gcol[:, 11:19], g_out.rearrange("(k p) -> p k", p=128), pwrites=gcol.all)
        K.dma("sp", gcol[:, 19:27], g_pre_mlp.rearrange("(k p) -> p k", p=128), pwrites=gcol.all)
        K.dma("sp", gcol[:, 27:35], g_ple.rearrange("(k p) -> p k", p=128), pwrites=gcol.all)
        for i, g in enumerate((g_post_mix, g_post_mlp, g_ple_post)):
            K.dma("sp", gbc[:, i * D:(i + 1) * D], g.rearrange("(o n) -> o n", o=1).broadcast_to([128, D]), pwrites=gbc.all)
        K.dma("sp", gbc[:, 3 * D:3 * D + 256], g_mla_kv.rearrange("(o n) -> o n", o=1).broadcast_to([128, 256]), pwrites=gbc.all)
        K.dma("sp", bcol[:, 0:1], b_f.rearrange("(h o) -> h o", o=1), pwrites=bcol.all)
        K.op("dve", lambda e: e.tensor_scalar(out=bcol[:, 1:2], in0=bcol[:, 0:1], scalar1=-1.0, scalar2=None, op0=ALU.mult),
             reads=bcol.all, writes=bcol.all)

        w_in_sb = WA[:, 0:8 * DIN].rearrange("p (k n) -> p k n", k=8)
        o1 = 8 * DIN
        w_q_sb = WA[:, o1:o1 + 3 * 768].rearrange("p (k n) -> p k n", k=3)
        o2 = o1 + 3 * 768
        w_kn_sb = WA[:, o2:o2 + 1024].rearrange("p (k n) -> p k n", k=2)
        o3 = o2 + 1024
        w_kv_sb = WA[:, o3:o3 + 1024].rearrange("p (k n) -> p k n", k=2)
        w_out_sb = WA[:, 0:8192].rearrange("p (k n) -> p k n", k=8)
        w_gate_sb = WA[:, 8192:16384].rearrange("p (k n) -> p k n", k=8)
        w_proj_sb = WA[:, 16384:18432].rearrange("p (k n) -> p k n", k=2)

        p0 = ExitStack()
        if True:
            stg = [T(p0, "stg%d" % i, [128, DIN], F32) for i in range(2)]
            cvb = [T(p0, "cvb%d" % i, [128, 1024], BF16) for i in range(2)]
            cnt = [0]
            cnts = [0]

            def conv(src_ap, ncols, dst_fn, scale_ap=None):
                i = cnts[0] % 2
                cnts[0] += 1
                s = stg[i]
                K.dma("pool" if (cnts[0] <= 13 and i == 1) else "sp", s[:, 0:ncols], src_ap, writes=s.all)
                return s

            for kc in range(8):
                s = conv(w_in[kc * 128:(kc + 1) * 128, :], DIN, None)
                K.op("act", lambda e, s=s, kc=kc: e.activation(out=w_in_sb[:, kc, :], in_=s[:, 0:DIN], func=AF.Copy, scale=gcol[:, kc:kc + 1]),
                     reads=s.all + gcol.all, pwrites=WA.all)
            for kc in range(3):
                s = conv(w_qup[kc * 128:(kc + 1) * 128, :], 768, None)
                K.op("act", lambda e, s=s, kc=kc: e.activation(out=w_q_sb[:, kc, :], in_=s[:, 0:768], func=AF.Copy, scale=gcol[:, 8 + kc:9 + kc]),
                     reads=s.all + gcol.all, pwrites=WA.all)
            for kc in range(2):
                s = conv(w_kvup[kc * 128:(kc + 1) * 128, :], 1024, None)
                sv = s[:, 0:1024].rearrange("p (h t d) -> p h t d", h=8, t=2)
                K.op("act", lambda e, sv=sv, kc=kc: e.activation(out=w_kn_sb[:, kc, :].rearrange("p (h d) -> p h d", h=8), in_=sv[:, :, 0, :], func=AF.Copy),
                     reads=s.all, pwrites=WA.all)
                K.op("act", lambda e, sv=sv, kc=kc: e.activation(out=w_kv_sb[:, kc, :].rearrange("p (h d) -> p h d", h=8), in_=sv[:, :, 1, :], func=AF.Copy),
                     reads=s.all, pwrites=WA.all)

            def conv_to_scr(src_ap, ncols, dst_ap, dst_dt, scale_col):
                def load_fn():
                    return conv(src_ap, ncols, None)

                def fin_fn(s):
                    j = cnt[0] % 2
                    cnt[0] += 1
                    c = cvb[j]
                    if scale_col is None:
                        K.op("act", lambda e: e.activation(out=c[:, 0:ncols], in_=s[:, 0:ncols], func=AF.Copy), reads=s.all, writes=c.all)
                    else:
                        K.op("act", lambda e: e.activation(out=c[:, 0:ncols], in_=s[:, 0:ncols], func=AF.Copy, scale=gcol[:, scale_col:scale_col + 1]),
                             reads=s.all + gcol.all, writes=c.all)
                    K.dma("pool", dst_ap, c[:, 0:ncols] if dst_ap.shape[-1] == ncols else c[:, 0:ncols].rearrange("p (a b) -> p a b", b=dst_ap.shape[-1]),
                          reads=c.all, pwrites=dst_dt.all)
                return (load_fn, fin_fn)

            conv_state = {}

            def conv_step():
                if "pending" in conv_state:
                    conv_state.pop("pending")()
                if conv_jobs:
                    load_fn, fin_fn = conv_jobs.pop(0)()
                    s_ = load_fn()
                    conv_state["pending"] = lambda: fin_fn(s_)

            def conv_flush():
                while conv_jobs or "pending" in conv_state:
                    conv_step()

            conv_jobs = []
            for kc in range(8):
                conv_jobs.append(lambda kc=kc: conv_to_scr(w_out[kc * 128:(kc + 1) * 128, :], 1024, WOUT.ap[:, kc, :], WOUT, 11 + kc))
                conv_jobs.append(lambda kc=kc: conv_to_scr(w_gate[kc * 128:(kc + 1) * 128, :], 1024, WGATE.ap[:, kc, :], WGATE, 27 + kc))
            for kc in range(2):
                conv_jobs.append(lambda kc=kc: conv_to_scr(w_proj[kc * 128:(kc + 1) * 128, :], 1024, WPROJ.ap[:, kc, :], WPROJ, None))
            for kc in range(8):
                for q4 in range(4):
                    conv_jobs.append(lambda kc=kc, q4=q4: conv_to_scr(w_up[kc * 128:(kc + 1) * 128, q4 * 1024:(q4 + 1) * 1024], 1024,
                                                                     WUP.ap[2 * q4:2 * q4 + 2, :, kc, :].rearrange("c p n -> p c n"), WUP, 19 + kc))
            for kc in range(32):
                conv_jobs.append(lambda kc=kc: conv_to_scr(w_down[kc * 128:(kc + 1) * 128, :], 1024,
                                                           WDN.ap[:, :, kc, :].rearrange("c p n -> p c n"), WDN, None))

        def _done():
            K.finish()
            global _LASTK
            _LASTK = K
            return nc
        if DBG["stop"] == "0":
            conv_flush()
            return _done()

        def run_pairs(gens):
            gens = list(gens)
            for i in range(0, len(gens), 2):
                live = gens[i:i + 2]
                while live:
                    for g_ in list(live):
                        if DBG.get("gstop") is not None:
                            DBG["gcount"] = DBG.get("gcount", 0) + 1
                            if DBG["gcount"] > DBG["gstop"]:
                                return
                        try:
                            next(g_)
                        except StopIteration:
                            live.remove(g_)

        def rstd_from_ss(ss_ap, n, out_ap, tmp_ap, rd, wr):
            K.op("act", lambda e: e.activation(out=tmp_ap, in_=ss_ap, func=AF.Ln, scale=1.0 / n, bias=EPS), reads=rd, writes=wr)
            K.op("act", lambda e: e.activation(out=out_ap, in_=tmp_ap, func=AF.Exp, scale=-0.5), reads=wr, writes=wr)

        def phaseA(x_src, nblk, nsub, kvpass, out_aps, scr, rope_src, is_sample):
            W = nsub * 128
            with ExitStack() as pa:
                xt = [T(pa, "xA%d" % i, [128, nsub, D], F32, nres=nsub) for i in range(2)]
                xn = T(pa, "xnA", [128, nsub, D], BF16, nres=nsub)
                xnTs = [T(pa, "xnT%d" % i, [128, 8, W], BF16) for i in range(2)]
                junk = T(pa, "junkA", [128, D], BF16)
                sts = [T(pa, "stA%d" % i, [128, 16], F32) for i in range(2)]
                rp = T(pa, "ropeA", [128, nsub, 64], F32)
                ptr = [T(pa, "ptrA%d" % i, [128, 1024], BF16, psum=True) for i in range(2)]
                pfm = [T(pa, "pfmA%d" % i, [128, 512], F32, psum=True) for i in range(2)]
                ptm = [T(pa, "ptmA%d" % i, [128, 512], F32, psum=True) for i in range(4)]
                nptm = [0]

                def next_ptm():
                    t_ = ptm[nptm[0] % 4]
                    nptm[0] += 1
                    return t_
                fmb = [T(pa, "fmbA%d" % i, [128, 4, W], BF16) for i in range(2)]
                fmbm = [T(pa, "fmbmA%d" % i, [128, 4, W], BF16) for i in range(2)]
                pending_tail = []
                spT = T(pa, "spT", [8, W], F32)
                SsT = T(pa, "SsT", [8, W], F32)
                eT = T(pa, "eT", [8, W], F32)
                carry = T(pa, "carryA", [8, 1], F32)
                skb = T(pa, "skbA", [8, 3, W], BF16)
                skr = T(pa, "skrA", [8, W], F32)
                if kvpass:
                    stage = [T(pa, "stageA%d" % i, [128, 1024], F32) for i in range(2)]
                    vb = T(pa, "vbA", [128, nsub, 512], BF16, nres=nsub)
                    vmb = T(pa, "vmbA", [128, nsub, 512], BF16)
                    ckvf = [T(pa, "ckvfA%d" % i, [128, 256], F32) for i in range(2)]
                    ckvbs = [T(pa, "ckvbA%d" % i, [128, 256], BF16) for i in range(nsub)]
                    ckvT = T(pa, "ckvTA", [128, 2, W], BF16, nres=nsub)
                    kpef = [T(pa, "kpefA%d" % i, [128, 32], F32) for i in range(2)]
                    kpebs = [T(pa, "kpebA%d" % i, [128, 32], BF16) for i in range(nsub)]
                    kpets = [T(pa, "kpetA%d" % i, [128, 64], F32) for i in range(2)]
                    kpeT = T(pa, "kpeTA", [32, W], BF16, nres=nsub)
                    lst = T(pa, "lstA", [128, nsub, 8], F32)
                else:
                    cqb = T(pa, "cqbA", [8, W], BF16)
                    qnbs = [T(pa, "qnbA%d" % i, [128, 384], BF16) for i in range(nsub)]
                    qnTs = [T(pa, "qnTA%d" % i, [128, 3, 128], BF16) for i in range(2)]
                    qfs = [T(pa, "qfA%d" % i, [128, 768], F32) for i in range(2)]
                    qbs = [T(pa, "qbA%d" % i, [128, 768], BF16) for i in range(nsub)]
                    qt1s = [T(pa, "qt1A%d" % i, [128, 256], F32) for i in range(2)]
                    QmT = T(pa, "QmTA", [96, 8, W], BF16, nres=nsub)
                K.op("dve", lambda e: e.memset(carry[:], 0.0), writes=carry.all)

                nfm = [0]

                def load_x(blk):
                    xb = xt[blk % 2]
                    K.dma("sp", xb[:], x_src[blk * W:(blk + 1) * W, :].rearrange("(s p) d -> p s d", p=128), writes=xb.all)

                def emit_norm(blk):
                    xb = xt[blk % 2]
                    xnT = xnTs[blk % 2]

                    def sub_norm(s):
                        par = s % 2
                        st = sts[par]
                        K.op("act", lambda e: e.activation(out=junk[:], in_=xb[:, s, :], func=AF.Square, accum_out=st[:, 0:1]),
                             reads=[xb.res[s]], writes=st.all)
                        yield
                        rstd_from_ss(st[:, 0:1], D, st[:, 2:3], st[:, 1:2], st.all, st.all)
                        yield
                        K.op("dve", lambda e: e.tensor_scalar(out=xn[:, s, :], in0=xb[:, s, :], scalar1=st[:, 2:3], scalar2=None, op0=ALU.mult),
                             reads=[xb.res[s]] + st.all, writes=[xn.res[s]])
                        yield
                        pt = ptr[par]
                        for kc in range(8):
                            K.op("pe", lambda e, kc=kc: e.transpose(out=pt[:, kc * 128:(kc + 1) * 128], in_=xn[:, s, kc * 128:(kc + 1) * 128], identity=identb[:]),
                                 reads=[xn.res[s]] + identb.all, writes=pt.all)
                        yield
                        K.op("act", lambda e: e.activation(out=xnT[:, :, s * 128:(s + 1) * 128], in_=pt[:].rearrange("p (k n) -> p k n", k=8), func=AF.Copy),
                             reads=pt.all, pwrites=xnT.all)
                    run_pairs([sub_norm(s) for s in range(nsub)])

                load_x(0)
                emit_norm(0)
                for blk in range(nblk):
                    xnT = xnTs[blk % 2]
                    if blk + 1 < nblk:
                        load_x(blk + 1)
                    nr = 64 if kvpass else 32
                    K.dma("sp", rp[:, :, 0:nr], rope_src[blk * W:(blk + 1) * W, :].rearrange("(s p) d -> p s d", p=128), writes=rp.all)
                    if DBG.get("astop") == 1:
                        K.barrier()
                        return
                    cbase = 512 if kvpass else 0
                    fb = fmb[blk % 2]
                    for pr in range(4):
                        pf = pfm[nfm[0] % 2]
                        nfm[0] += 1
                        for kc in range(8):
                            K.op("pe", lambda e, pr=pr, kc=kc, pf=pf: e.matmul(pf[:, 0:W], lhsT=w_in_sb[:, kc, cbase + pr * 128:cbase + (pr + 1) * 128], rhs=xnT[:, kc, :], start=(kc == 0), stop=(kc == 7)),
                                 reads=WA.all + xnT.all, writes=pf.all)
                        if kvpass:
                            K.op("act", lambda e, pr=pr, pf=pf: e.activation(out=fb[:, pr, :], in_=pf[:, 0:W], func=AF.Copy), reads=pf.all, pwrites=fb.all)
                        else:
                            K.op("act", lambda e, pr=pr, pf=pf: e.activation(out=fb[:, pr, :], in_=pf[:, 0:W], func=AF.Copy, scale=SC_F), reads=pf.all, pwrites=fb.all)
                    if not is_sample:
                        dst = (scr["KTf"] if kvpass else scr["QTf"])
                        K.dma("pool", dst.ap[:, :, blk * W:(blk + 1) * W].rearrange("c p n -> p c n"), fb[:], reads=fb.all, pwrites=dst.all)
                    else:
                        if kvpass:
                            for sidx in range(4):
                                k0 = sidx * TS * 128 + PAST
                                K.dma("pool", scr["KTf"].ap[:, :, k0:k0 + 32].rearrange("c p n -> p c n"), fb[:, :, sidx * 32:(sidx + 1) * 32], reads=fb.all, pwrites=scr["KTf"].all)
                        else:
                            K.dma("pool", scr["QTf"].ap.rearrange("c p n -> p c n"), fb[:], reads=fb.all, pwrites=scr["QTf"].all)
                    if blk + 1 < nblk:
                        emit_norm(blk + 1)
                    while pending_tail:
                        pending_tail.pop(0)()
                    if DBG.get("astop") == 2:
                        K.barrier()
                        return
                    if kvpass and not is_sample:
                        conv_step()
                    pf = pfm[nfm[0] % 2]
                    nfm[0] += 1
                    for kc in range(8):
                        K.op("pe", lambda e, kc=kc, pf=pf: e.matmul(pf[0:8, 0:W], lhsT=w_in_sb[:, kc, 1536:1544], rhs=xnT[:, kc, :], start=(kc == 0), stop=(kc == 7)),
                             reads=WA.all + xnT.all, writes=pf.all)
                    K.op("act", lambda e, pf=pf: e.activation(out=eT[:], in_=pf[0:8, 0:W], func=AF.Exp, scale=-1.0, bias=bcol[:, 1:2]), reads=pf.all + bcol.all, writes=eT.all)
                    K.op("act", lambda e: e.activation(out=spT[:], in_=eT[:], func=AF.Ln, bias=1.0), reads=eT.all, writes=spT.all)
                    if is_sample:
                        for sidx in range(4):
                            K.op("dve", lambda e, sidx=sidx: e.tensor_tensor_scan(out=SsT[:, sidx * 32:(sidx + 1) * 32], data0=onesrow[:, 0:32], data1=spT[:, sidx * 32:(sidx + 1) * 32], initial=0.0, op0=ALU.mult, op1=ALU.add),
                                 reads=spT.all + onesrow.all, pwrites=SsT.all)
                    else:
                        if kvpass:
                            K.op("dve", lambda e, blk=blk: e.tensor_copy(out=CB[:, blk:blk + 1], in_=carry[:]), reads=carry.all, pwrites=CB.all)
                            init_ap = carry[:, 0:1]
                            rdi = carry.all
                        else:
                            init_ap = CO[:, blk:blk + 1]
                            rdi = CO.all
                        K.op("dve", lambda e, init_ap=init_ap: e.tensor_tensor_scan(out=SsT[:], data0=onesrow[:, 0:W], data1=spT[:], initial=init_ap, op0=ALU.mult, op1=ALU.add),
                             reads=spT.all + onesrow.all + rdi, writes=SsT.all)
                        if kvpass:
                            K.op("dve", lambda e: e.tensor_copy(out=carry[:], in_=SsT[:, W - 1:W]), reads=SsT.all, writes=carry.all)
                            K.op("dve", lambda e: e.tensor_copy(out=skb[:, 0, :], in_=SsT[:]), reads=SsT.all, writes=skb.all)
                            K.op("dve", lambda e: e.tensor_tensor(out=skr[:], in0=SsT[:], in1=skb[:, 0, :], op=ALU.subtract), reads=SsT.all + skb.all, writes=skr.all)
                            K.op("dve", lambda e: e.tensor_copy(out=skb[:, 1, :], in_=skr[:]), reads=skr.all, pwrites=skb.all)
                            K.op("dve", lambda e: e.tensor_tensor(out=skr[:], in0=skr[:], in1=skb[:, 1, :], op=ALU.subtract), reads=skb.all, writes=skr.all)
                            K.op("dve", lambda e: e.tensor_copy(out=skb[:, 2, :], in_=skr[:]), reads=skr.all, pwrites=skb.all)
                            K.dma("pool", SKf.ap[:, :, blk * W:(blk + 1) * W].rearrange("i h n -> h i n"), skb[:], reads=skb.all, pwrites=SKf.all)
                    if DBG.get("astop") == 3:
                        K.barrier()
                        return
                    if kvpass:
                        def lf_job(blk=blk):
                            psm = pfm[nfm[0] % 2]
                            nfm[0] += 1
                            for s in range(nsub):
                                K.op("pe", lambda e, s=s: e.transpose(out=psm[:, s * 8:(s + 1) * 8], in_=spT[0:8, s * 128:(s + 1) * 128], identity=identf[0:8, 0:8]),
                                     reads=spT.all + identf.all, writes=psm.all)
                                K.op("pe", lambda e, s=s: e.transpose(out=psm[:, 64 + s * 8:64 + (s + 1) * 8], in_=SsT[0:8, s * 128:(s + 1) * 128], identity=identf[0:8, 0:8]),
                                     reads=SsT.all + identf.all, writes=psm.all)
                            K.op("dve", lambda e: e.tensor_scalar(out=lst[:].rearrange("p s h -> p (s h)"), in0=psm[:, 0:nsub * 8], scalar1=-1.0, scalar2=None, op0=ALU.mult),
                                 reads=psm.all, writes=lst.all)
                            K.dma("pool", out_aps["lf"][blk * W:(blk + 1) * W, :].rearrange("(s p) h -> p s h", p=128), lst[:], reads=lst.all)
                            if is_sample:
                                for sidx in range(4):
                                    col = (sidx * TS + 16) * 8
                                    K.op("act", lambda e, sidx=sidx, col=col: e.activation(out=negcs[0:32, col:col + 8], in_=psm[sidx * 32:(sidx + 1) * 32, 64:72], func=AF.Copy),
                                         reads=psm.all, pwrites=negcs.all)
                            else:
                                K.op("dve", lambda e, blk=blk: e.tensor_copy(out=negc[:, blk * 32:(blk + 1) * 32], in_=psm[:, 64:64 + 32]), reads=psm.all, pwrites=negc.all)
                        pending_tail.append(lf_job)
                    else:
                        K.op("dve", lambda e: e.tensor_scalar(out=cqb[:], in0=SsT[:], scalar1=-1.0, scalar2=None, op0=ALU.mult), reads=SsT.all, writes=cqb.all)
                        if is_sample:
                            K.dma("pool", scr["CQ"].ap, cqb[:], reads=cqb.all, pwrites=scr["CQ"].all)
                        else:
                            K.dma("pool", scr["CQ"].ap[:, blk * W:(blk + 1) * W], cqb[:], reads=cqb.all, pwrites=scr["CQ"].all)
                    if DBG.get("astop") == 4:
                        K.barrier()
                        return
                    def sub_kv(s):
                        par = s % 2
                        st = sts[par]
                        r0 = blk * W + s * 128
                        sg = stage[par]
                        banks = []
                        for hf in range(2):
                            pb_ = next_ptm()
                            banks.append(pb_)
                            for kc in range(8):
                                K.op("pe", lambda e, hf=hf, kc=kc, pb_=pb_: e.matmul(pb_[:], lhsT=xnT[:, kc, s * 128:(s + 1) * 128], rhs=w_in_sb[:, kc, 512 + hf * 512:1024 + hf * 512], start=(kc == 0), stop=(kc == 7)),
                                     reads=WA.all + xnT.all, writes=pb_.all)
                        yield
                        K.op("act", lambda e: e.activation(out=sg[:, 0:512], in_=banks[0][:], func=AF.Copy), reads=banks[0].all, writes=sg.all)
                        K.op("act", lambda e: e.activation(out=sg[:, 512:1024], in_=banks[1][:], func=AF.Copy), reads=banks[1].all + sg.all, pwrites=sg.all)
                        K.op("dve", lambda e: e.tensor_copy(out=vb[:, s, :], in_=banks[1][:]), reads=banks[1].all, writes=[vb.res[s]])
                        yield
                        if not is_sample:
                            conv_step()
                        K.dma("pool", out_aps["fk"][r0:r0 + 128, :], sg[:, 0:512], reads=sg.all)
                        K.dma("pool", out_aps["fv"][r0:r0 + 128, :], sg[:, 512:1024], reads=sg.all)
                        p2 = next_ptm()
                        for kc in range(8):
                            K.op("pe", lambda e, kc=kc: e.matmul(p2[:, 0:288], lhsT=xnT[:, kc, s * 128:(s + 1) * 128], rhs=w_in_sb[:, kc, 1928:2216], start=(kc == 0), stop=(kc == 7)),
                                 reads=WA.all + xnT.all, writes=p2.all)
                        yield
                        K.op("act", lambda e: e.activation(out=junk[:, 0:256], in_=p2[:, 0:256], func=AF.Square, accum_out=st[:, 4:5]), reads=p2.all, writes=st.all)
                        yield
                        rstd_from_ss(st[:, 4:5], 256, st[:, 6:7], st[:, 5:6], st.all, st.all)
                        yield
                        cf = ckvf[par]
                        K.op("dve", lambda e: e.scalar_tensor_tensor(out=cf[:], in0=p2[:, 0:256], scalar=st[:, 6:7], in1=gbc[:, 3 * D:3 * D + 256], op0=ALU.mult, op1=ALU.mult),
                             reads=p2.all + st.all + gbc.all, writes=cf.all)
                        kf = kpef[par]
                        kpet = kpets[par]
                        K.op("dve", lambda e: e.tensor_tensor(out=kpet[:, 0:32], in0=p2[:, 256:288], in1=rp[:, s, 0:32], op=ALU.mult), reads=p2.all + rp.all, writes=kpet.all)
                        K.op("dve", lambda e: e.tensor_tensor(out=kpet[:, 32:48], in0=p2[:, 272:288], in1=rp[:, s, 32:48], op=ALU.mult), reads=p2.all + rp.all, pwrites=kpet.all)
                        K.op("dve", lambda e: e.tensor_tensor(out=kpet[:, 48:64], in0=p2[:, 256:272], in1=rp[:, s, 48:64], op=ALU.mult), reads=p2.all + rp.all, pwrites=kpet.all)
                        yield
                        K.dma("pool", out_aps["ckv"][r0:r0 + 128, :], cf[:], reads=cf.all)
                        ckvb = ckvbs[s]
                        kpeb = kpebs[s]
                        K.op("dve", lambda e: e.tensor_copy(out=ckvb[:], in_=cf[:]), reads=cf.all, writes=ckvb.all)
                        K.op("dve", lambda e: e.tensor_tensor(out=kf[:], in0=kpet[:, 0:32], in1=kpet[:, 32:64], op=ALU.add), reads=kpet.all, writes=kf.all)
                        yield
                        K.dma("pool", out_aps["kpe"][r0:r0 + 128, :], kf[:], reads=kf.all)
                        K.op("dve", lambda e: e.tensor_copy(out=kpeb[:], in_=kf[:]), reads=kf.all, writes=kpeb.all)

                        def part2():
                            pt = ptr[par]
                            for kc in range(2):
                                K.op("pe", lambda e, kc=kc: e.transpose(out=pt[:, kc * 128:(kc + 1) * 128], in_=ckvb[:, kc * 128:(kc + 1) * 128], identity=identb[:]),
                                     reads=ckvb.all + identb.all, writes=pt.all)
                            K.op("pe", lambda e: e.transpose(out=pt[0:32, 256:384], in_=kpeb[:, 0:32], identity=identb[:]), reads=kpeb.all + identb.all, writes=pt.all)
                            K.op("act", lambda e: e.activation(out=ckvT[:, :, s * 128:(s + 1) * 128], in_=pt[:, 0:256].rearrange("p (k n) -> p k n", k=2), func=AF.Copy),
                                 reads=pt.all, writes=[ckvT.res[s]])
                            K.op("act", lambda e: e.activation(out=kpeT[:, s * 128:(s + 1) * 128], in_=pt[0:32, 256:384], func=AF.Copy), reads=pt.all, writes=[kpeT.res[s]])
                        pending_tail.append(part2)

                    def q_partA(s):
                        par = s % 2
                        st = sts[par]
                        qnb = qnbs[s]
                        p2 = next_ptm()
                        for kc in range(8):
                            K.op("pe", lambda e, kc=kc: e.matmul(p2[:, 0:384], lhsT=xnT[:, kc, s * 128:(s + 1) * 128], rhs=w_in_sb[:, kc, 1544:1928], start=(kc == 0), stop=(kc == 7)),
                                 reads=WA.all + xnT.all, writes=p2.all)
                        yield
                        K.op("act", lambda e: e.activation(out=junk[:, 0:384], in_=p2[:, 0:384], func=AF.Square, accum_out=st[:, 4:5]), reads=p2.all, writes=st.all)
                        yield
                        rstd_from_ss(st[:, 4:5], 384, st[:, 6:7], st[:, 5:6], st.all, st.all)
                        yield
                        K.op("dve", lambda e: e.tensor_scalar(out=qnb[:], in0=p2[:, 0:384], scalar1=st[:, 6:7], scalar2=None, op0=ALU.mult), reads=p2.all + st.all, writes=qnb.all)

                    def q_partB(s):
                        par = s % 2
                        qnb, qnT, qf, qb, qt1 = qnbs[s], qnTs[par], qfs[par], qbs[s], qt1s[par]
                        pt = ptr[par]
                        for kc in range(3):
                            K.op("pe", lambda e, kc=kc: e.transpose(out=pt[:, kc * 128:(kc + 1) * 128], in_=qnb[:, kc * 128:(kc + 1) * 128], identity=identb[:]),
                                 reads=qnb.all + identb.all, writes=pt.all)
                        yield
                        K.op("act", lambda e: e.activation(out=qnT[:].rearrange("p k n -> p (k n)"), in_=pt[:, 0:384], func=AF.Copy), reads=pt.all, writes=qnT.all)
                        yield
                        banks = []
                        for hf in range(2):
                            pb_ = next_ptm()
                            banks.append(pb_)
                            for kc in range(3):
                                K.op("pe", lambda e, hf=hf, kc=kc, pb_=pb_: e.matmul(pb_[:, 0:384], lhsT=qnT[:, kc, :], rhs=w_q_sb[:, kc, hf * 384:(hf + 1) * 384], start=(kc == 0), stop=(kc == 2)),
                                     reads=WA.all + qnT.all, writes=pb_.all)
                        yield
                        K.op("act", lambda e: e.activation(out=qf[:, 0:384], in_=banks[0][:, 0:384], func=AF.Copy), reads=banks[0].all, writes=qf.all)
                        K.op("act", lambda e: e.activation(out=qf[:, 384:768], in_=banks[1][:, 0:384], func=AF.Copy), reads=banks[1].all + qf.all, pwrites=qf.all)
                        yield
                        qv = qf[:].rearrange("p (h n) -> p h n", h=8)
                        qbv = qb[:].rearrange("p (h n) -> p h n", h=8)
                        cs_ = rp[:, s, 0:16].rearrange("p (o n) -> p o n", o=1).broadcast_to([128, 8, 16])
                        sn_ = rp[:, s, 16:32].rearrange("p (o n) -> p o n", o=1).broadcast_to([128, 8, 16])
                        t1v = qt1[:, 0:128].rearrange("p (h n) -> p h n", h=8)
                        t2v = qt1[:, 128:256].rearrange("p (h n) -> p h n", h=8)
                        K.op("dve", lambda e: e.tensor_scalar(out=qbv[:, :, 0:64], in0=qv[:, :, 0:64], scalar1=SC_M, scalar2=None, op0=ALU.mult), reads=qf.all, writes=qb.all)
                        K.op("dve", lambda e: e.tensor_tensor(out=t1v, in0=qv[:, :, 64:80], in1=cs_, op=ALU.mult), reads=qf.all + rp.all, writes=qt1.all)
                        K.op("dve", lambda e: e.tensor_tensor(out=t2v, in0=qv[:, :, 80:96], in1=sn_, op=ALU.mult), reads=qf.all + rp.all, pwrites=qt1.all)
                        yield
                        K.op("dve", lambda e: e.tensor_tensor(out=qbv[:, :, 64:80], in0=t1v, in1=t2v, op=ALU.subtract), reads=qt1.all, pwrites=qb.all)
                        yield
                        K.op("dve", lambda e: e.tensor_tensor(out=t1v, in0=qv[:, :, 64:80], in1=sn_, op=ALU.mult), reads=qf.all + rp.all + qb.all, writes=qt1.all)
                        K.op("dve", lambda e: e.tensor_tensor(out=t2v, in0=qv[:, :, 80:96], in1=cs_, op=ALU.mult), reads=qf.all + rp.all, pwrites=qt1.all)
                        yield
                        K.op("dve", lambda e: e.tensor_tensor(out=qbv[:, :, 80:96], in0=t1v, in1=t2v, op=ALU.add), reads=qt1.all, pwrites=qb.all)

                    def q_partC(s):
                        par = s % 2
                        qb = qbs[s]
                        pt = ptr[par]
                        for h in range(8):
                            K.op("pe", lambda e, h=h: e.transpose(out=pt[0:96, h * 128:(h + 1) * 128], in_=qb[:, h * 96:(h + 1) * 96], identity=identb[:]),
                                 reads=qb.all + identb.all, writes=pt.all)
                        yield
                        K.op("act", lambda e: e.activation(out=QmT[:, :, s * 128:(s + 1) * 128], in_=pt[0:96, :].rearrange("p (h n) -> p h n", h=8), func=AF.Copy),
                             reads=pt.all, writes=[QmT.res[s]])

                    if kvpass:
                        run_pairs([sub_kv(s) for s in range(nsub)])
                    else:
                        run_pairs([q_partA(s) for s in range(nsub)])
                        run_pairs([q_partB(s) for s in range(nsub)])
                        run_pairs([q_partC(s) for s in range(nsub)])
                    if DBG.get("astop") == 6:
                        K.barrier()
                        return
                    def tail(blk=blk):
                        if kvpass:
                            if not is_sample:
                                K.dma("pool", scr["Vf"].ap[blk * W:(blk + 1) * W, :].rearrange("(s p) c -> p s c", p=128), vb[:], reads=vb.all, pwrites=scr["Vf"].all)
                            else:
                                for sidx in range(4):
                                    K.dma("pool", scr["Vf"].ap[sidx * TS * 128 + PAST:sidx * TS * 128 + PAST + 32, :], vb[sidx * 32:(sidx + 1) * 32, 0, :], reads=vb.all, pwrites=scr["Vf"].all)
                            fb2 = fmbm[blk % 2]
                            for pr in range(4):
                                pf = pfm[nfm[0] % 2]
                                nfm[0] += 1
                                for kc in range(2):
                                    K.op("pe", lambda e, pr=pr, kc=kc, pf=pf: e.matmul(pf[:, 0:W], lhsT=w_kn_sb[:, kc, pr * 128:(pr + 1) * 128], rhs=ckvT[:, kc, :], start=(kc == 0), stop=(kc == 1)),
                                         reads=WA.all + ckvT.all, writes=pf.all)
                                K.op("act", lambda e, pr=pr, pf=pf: e.activation(out=fb2[:, pr, :], in_=pf[:, 0:W], func=AF.Copy), reads=pf.all, pwrites=fb2.all)
                            for s in range(nsub):
                                p2 = next_ptm()
                                for kc in range(2):
                                    K.op("pe", lambda e, s=s, kc=kc, p2=p2: e.matmul(p2[:, 0:512], lhsT=ckvT[:, kc, s * 128:(s + 1) * 128], rhs=w_kv_sb[:, kc, :], start=(kc == 0), stop=(kc == 1)),
                                         reads=WA.all + ckvT.all, writes=p2.all)
                                K.op("dve", lambda e, s=s, p2=p2: e.tensor_copy(out=vmb[:, s, :], in_=p2[:, 0:512]), reads=p2.all, pwrites=vmb.all)
                            if not is_sample:
                                K.dma("pool", scr["KTm"].ap[:, :, blk * W:(blk + 1) * W].rearrange("c p n -> p c n"), fb2[:], reads=fb2.all, pwrites=scr["KTm"].all)
                                K.dma("pool", scr["KPE"].ap[:, blk * W:(blk + 1) * W], kpeT[:], reads=kpeT.all, pwrites=scr["KPE"].all)
                                K.dma("pool", scr["Vm"].ap[blk * W:(blk + 1) * W, :].rearrange("(s p) c -> p s c", p=128), vmb[:], reads=vmb.all, pwrites=scr["Vm"].all)
                            else:
                                for sidx in range(4):
                                    k0 = sidx * TS * 128 + PAST
                                    K.dma("pool", scr["KTm"].ap[:, :, k0:k0 + 32].rearrange("c p n -> p c n"), fb2[:, :, sidx * 32:(sidx + 1) * 32], reads=fb2.all, pwrites=scr["KTm"].all)
                                    K.dma("pool", scr["KPE"].ap[:, k0:k0 + 32], kpeT[:, sidx * 32:(sidx + 1) * 32], reads=kpeT.all, pwrites=scr["KPE"].all)
                                    K.dma("pool", scr["Vm"].ap[k0:k0 + 32, :], vmb[sidx * 32:(sidx + 1) * 32, 0, :], reads=vmb.all, pwrites=scr["Vm"].all)
                        else:
                            if is_sample:
                                K.dma("pool", scr["QTm"].ap.rearrange("h r n -> r h n"), QmT[:], reads=QmT.all, pwrites=scr["QTm"].all)
                            else:
                                K.dma("pool", scr["QTm"].ap[:, :, blk * W:(blk + 1) * W].rearrange("h r n -> r h n"), QmT[:], reads=QmT.all, pwrites=scr["QTm"].all)
                    pending_tail.append(tail)
                    if kvpass and not is_sample:
                        conv_step()
                while pending_tail:
                    pending_tail.pop(0)()

        _phaseA_raw = phaseA

        def phaseA(*a):
            _phaseA_raw(*a)
            K.barrier()

        scrP = dict(KTf=KTf, KTm=KTm, KPE=KPE, Vf=Vf, Vm=Vm, QTf=QTf, QTm=QTm, CQ=CQ)
        scrS = dict(KTf=KTfs, KTm=KTms, KPE=KPEs, Vf=Vfs, Vm=Vms, QTf=QTfs, QTm=QTms, CQ=CQs)
        outP = dict(fk=fk_o, fv=fv_o, lf=lf_o, ckv=ckv_o, kpe=kpe_o)
        outS = dict(fk=fks_o, fv=fvs_o, lf=lfs_o, ckv=ckvs_o, kpe=kpes_o)

        _phaseA_raw(xall, DBG.get("nblk", NBLK), 4, True, outP, scrP, rope_kv, False)
        conv_flush()
        K.barrier()
        p0.close()
        if DBG["stop"] == "A1":
            return _done()
        with ExitStack() as pc:
            tmpc = T(pc, "tmpc", [8, NSLOT * NBLK], F32)
            K.op("dve", lambda e: e.tensor_tensor(out=tmpc[:].rearrange("h (s b) -> h s b", s=NSLOT), in0=selt[:].rearrange("h (s b) -> h s b", s=NSLOT),
                                                  in1=CB[:].rearrange("h (o b) -> h o b", o=1).broadcast_to([8, NSLOT, NBLK]), op=ALU.mult),
                 reads=selt.all + CB.all, writes=tmpc.all)
            K.op("dve", lambda e: e.tensor_reduce(out=CO[:], in_=tmpc[:].rearrange("h (s b) -> h s b", s=NSLOT), axis=mybir.AxisListType.X, op=ALU.add),
                 reads=tmpc.all, writes=CO.all)
            K.barrier()
        phaseA(xown, DBG.get("nslot", NSLOT), 4, False, None, scrP, rope_q, False)
        if DBG["stop"] == "A2":
            return _done()
        phaseA(xs, 1, 1, True, outS, scrS, rope_kv_s, True)
        phaseA(xs, 1, 1, False, None, scrS, rope_q_s, True)
        if DBG["stop"] == "A4":
            return _done()

        with ExitStack() as pS:
            kcb = T(pS, "kcb", [128, 16, 512], BF16)
            ktS = [T(pS, "ktS%d" % i, [128, 2048], BF16) for i in range(2)]
            lfc = T(pS, "lfc", [128, 16 * 8], F32)
            lfs2 = T(pS, "lfs2", [128, 16 * 8], F32)
            ccb = T(pS, "ccb", [128, 16, 256], BF16)
            ckT = T(pS, "ckT", [128, 2, 2048], BF16)
            kpb = T(pS, "kpb", [128, 16, 32], BF16)
            kpf = T(pS, "kpf", [128, 16, 32], F32)
            kpT = T(pS, "kpT", [32, 2048], BF16)
            vsb = T(pS, "vsb", [128, 16, 512], BF16)
            vms = T(pS, "vms", [128, 16, 512], BF16)
            pT_ = [T(pS, "pTS%d" % i, [128, 1024], BF16, psum=True) for i in range(2)]
            pF_ = [T(pS, "pFS%d" % i, [128, 512], F32, psum=True) for i in range(2)]
            pB_ = T(pS, "pBS", [128, 512], F32, psum=True)
            npt = [0]
            npf = [0]
            for sidx in range(4):
                k0 = sidx * TS * 128
                K.dma("pool", kcb[:], ck[sidx].rearrange("(t p) c -> p t c", p=128), writes=kcb.all)
                for pr in range(4):
                    kt_ = ktS[pr % 2]
                    for half in range(2):
                        pt = pT_[npt[0] % 2]
                        npt[0] += 1
                        for j in range(8):
                            t = half * 8 + j
                            K.op("pe", lambda e, t=t, j=j, pr=pr, pt=pt: e.transpose(out=pt[:, j * 128:(j + 1) * 128], in_=kcb[:, t, pr * 128:(pr + 1) * 128], identity=identb[:]),
                                 reads=kcb.all + identb.all, writes=pt.all)
                        K.op("act", lambda e, half=half, pt=pt, kt_=kt_: e.activation(out=kt_[:, half * 1024:(half + 1) * 1024], in_=pt[:], func=AF.Copy), reads=pt.all, pwrites=kt_.all)
                    K.dma("sp", KTfs.ap[pr, :, k0:k0 + PAST], kt_[:], reads=kt_.all, pwrites=KTfs.all)
                if DBG.get("sstop") == 1:
                    break
                K.dma("pool", vsb[:], cv[sidx].rearrange("(t p) c -> p t c", p=128), writes=vsb.all)
                K.dma("sp", Vfs.ap[k0:k0 + PAST, :].rearrange("(t p) c -> p t c", p=128), vsb[:], reads=vsb.all, pwrites=Vfs.all)
                if DBG.get("sstop") == 2:
                    break
                K.dma("sp", lfc[:].rearrange("p (t h) -> p t h", h=8), clf[sidx].rearrange("(t p) h -> p t h", p=128), writes=lfc.all)
                K.op("dve", lambda e: e.memset(lfs2[:, 15 * 8:16 * 8], 0.0), pwrites=lfs2.all)
                for t in range(14, -1, -1):
                    K.op("dve", lambda e, t=t: e.tensor_tensor(out=lfs2[:, t * 8:(t + 1) * 8], in0=lfs2[:, (t + 1) * 8:(t + 2) * 8], in1=lfc[:, (t + 1) * 8:(t + 2) * 8], op=ALU.add),
                         reads=lfc.all + lfs2.all, writes=lfs2.all)
                K.op("pe", lambda e: e.matmul(pB_[:, 0:128], lhsT=suf[:], rhs=lfc[:], start=True, stop=False), reads=suf.all + lfc.all, writes=pB_.all)
                K.op("pe", lambda e: e.matmul(pB_[:, 0:128], lhsT=onesf[:], rhs=lfs2[:], start=False, stop=True), reads=onesf.all + lfs2.all, writes=pB_.all)
                K.op("dve", lambda e, sidx=sidx: e.tensor_copy(out=negcs[:, sidx * TS * 8:(sidx * TS + 16) * 8], in_=pB_[:, 0:128]), reads=pB_.all, pwrites=negcs.all)
                if DBG.get("sstop") == 3:
                    break
                K.dma("pool", ccb[:], cckv[sidx].rearrange("(t p) c -> p t c", p=128), writes=ccb.all)
                for kc in range(2):
                    for half in range(2):
                        pt = pT_[npt[0] % 2]
                        npt[0] += 1
                        for j in range(8):
                            t = half * 8 + j
                            K.op("pe", lambda e, t=t, j=j, kc=kc, pt=pt: e.transpose(out=pt[:, j * 128:(j + 1) * 128], in_=ccb[:, t, kc * 128:(kc + 1) * 128], identity=identb[:]),
                                 reads=ccb.all + identb.all, writes=pt.all)
                        K.op("act", lambda e, kc=kc, half=half, pt=pt: e.activation(out=ckT[:, kc, half * 1024:(half + 1) * 1024], in_=pt[:], func=AF.Copy), reads=pt.all, pwrites=ckT.all)
                if DBG.get("sstop") == 4:
                    break
                K.dma("sp", kpf[:], ckpe[sidx].rearrange("(t p) c -> p t c", p=128), writes=kpf.all)
                K.op("dve", lambda e: e.tensor_copy(out=kpb[:].rearrange("p t c -> p (t c)"), in_=kpf[:].rearrange("p t c -> p (t c)")), reads=kpf.all, writes=kpb.all)
                if DBG.get("sstop") == 9:
                    break
                for half in range(2):
                    pt = pT_[npt[0] % 2]
                    npt[0] += 1
                    for j in range(8):
                        t = half * 8 + j
                        K.op("pe", lambda e, t=t, j=j, pt=pt: e.transpose(out=pt[0:32, j * 128:(j + 1) * 128], in_=kpb[:, t, :], identity=identb[:]),
                             reads=kpb.all + identb.all, writes=pt.all)
                    K.op("act", lambda e, half=half, pt=pt: e.activation(out=kpT[:, half * 1024:(half + 1) * 1024], in_=pt[0:32, :], func=AF.Copy), reads=pt.all, pwrites=kpT.all)
                if DBG.get("sstop") == 8:
                    break
                K.dma("sp", KPEs.ap[:, k0:k0 + PAST], kpT[:], reads=kpT.all, pwrites=KPEs.all)
                if DBG.get("sstop") == 5:
                    break
                for pr in range(4):
                    kt_ = ktS[pr % 2]
                    for cb in range(4):
                        pf = pF_[npf[0] % 2]
                        npf[0] += 1
                        for kc in range(2):
                            K.op("pe", lambda e, pr=pr, cb=cb, kc=kc, pf=pf: e.matmul(pf[:], lhsT=w_kn_sb[:, kc, pr * 128:(pr + 1) * 128], rhs=ckT[:, kc, cb * 512:(cb + 1) * 512], start=(kc == 0), stop=(kc == 1)),
                                 reads=WA.all + ckT.all, writes=pf.all)
                        K.op("act", lambda e, cb=cb, pf=pf, kt_=kt_: e.activation(out=kt_[:, cb * 512:(cb + 1) * 512], in_=pf[:], func=AF.Copy), reads=pf.all, pwrites=kt_.all)
                    K.dma("sp", KTms.ap[pr, :, k0:k0 + PAST], kt_[:], reads=kt_.all, pwrites=KTms.all)
                if DBG.get("sstop") == 6:
                    break
                for t in range(16):
                    pf = pF_[npf[0] % 2]
                    npf[0] += 1
                    for kc in range(2):
                        K.op("pe", lambda e, t=t, kc=kc, pf=pf: e.matmul(pf[:], lhsT=ckT[:, kc, t * 128:(t + 1) * 128], rhs=w_kv_sb[:, kc, :], start=(kc == 0), stop=(kc == 1)),
                             reads=WA.all + ckT.all, writes=pf.all)
                    K.op("dve", lambda e, t=t, pf=pf: e.tensor_copy(out=vms[:, t, :], in_=pf[:]), reads=pf.all, pwrites=vms.all)
                K.dma("sp", Vms.ap[k0:k0 + PAST, :].rearrange("(t p) c -> p t c", p=128), vms[:], reads=vms.all, pwrites=Vms.all)
                if DBG.get("sstop") == 7:
                    break

        K.barrier()
        if DBG["stop"] == "S":
            return _done()
        with ExitStack() as pb:
            NKT = 4 * TS
            KTb = [T(pb, "KTb%d" % i, [96, NKS], BF16) for i in range(2)]
            Vb = [T(pb, "Vb%d" % i, [128, NKT, 65], BF16) for i in range(2)]
            QTb = [T(pb, "QTb%d" % i, [96, NQ], BF16) for i in range(2)]
            mk = T(pb, "mk", [128, 2 * 2 * 8 * 512], BF16)
            PT = [T(pb, "PT%d" % i, [128, 512], BF16) for i in range(6)]
            osb = [T(pb, "osb%d" % i, [128, 4, 64], F32) for i in range(2)]
            rcp = [T(pb, "rcp%d" % i, [128, 4], F32) for i in range(2)]
            pS_ = [T(pb, "pSB%d" % i, [128, 512], F32, psum=True) for i in range(6)]
            pO_raw = [T(pb, "pOB%d" % i, [128, 512], F32, psum=True) for i in range(2)]

            class _PO:
                def __init__(self, t):
                    self.t = t
                    self.all = t.all
                    self.v = t[:, 0:260].rearrange("p (s d) -> p s d", d=65)

                def __getitem__(self, idx):
                    return self.v[idx]
            pO_ = [_PO(t) for t in pO_raw]
            K.dma("sp", mk[:].rearrange("p (a n) -> p a n", n=512), masks_d.rearrange("g r t p n -> p (g r t) n"), writes=mk.all)
            for i in range(2):
                K.op("dve", lambda e, i=i: e.memset(KTb[i][64:65, :], 1.0), pwrites=KTb[i].all)
                K.op("dve", lambda e, i=i: e.memset(QTb[i][64:68, :], 1.0), pwrites=QTb[i].all)
                K.op("dve", lambda e, i=i: e.memset(Vb[i][:, :, 64:65], 1.0), pwrites=Vb[i].all)
            cnt = dict(u=0, o=0, po=0)
            fifo = []
            LAG = 5

            def attend(g, kt_ap, q_ap, W, kn, v_ap, bias_ap, bias_res, mask_ap, first, last, po, nq_sub, c0=0, last_fn=None, mask_w=None):
                u = cnt["u"]
                cnt["u"] += 1
                ps = pS_[u % 6]
                pt = PT[u % 6]
                K.op("pe", lambda e: e.matmul(ps[0:kn, c0:W], lhsT=kt_ap, rhs=q_ap[:, c0:W], start=True, stop=(mask_ap is None)),
                     reads=g["kt"].all + g["qt"].all, writes=ps.all)
                if mask_ap is not None:
                    mW = W if mask_w is None else mask_w
                    K.op("pe", lambda e: e.matmul(ps[0:kn, c0:mW], lhsT=identb[0:kn, 0:kn], rhs=mask_ap[:, c0:mW], start=False, stop=True),
                         reads=identb.all + mk.all + smask.all, writes=ps.all)
                if bias_ap is None:
                    K.op("act", lambda e: e.activation(out=pt[0:kn, c0:W], in_=ps[0:kn, c0:W], func=AF.Exp), reads=ps.all, writes=pt.all)
                else:
                    K.op("act", lambda e: e.activation(out=pt[0:kn, c0:W], in_=ps[0:kn, c0:W], func=AF.Exp, bias=bias_ap), reads=ps.all + bias_res, writes=pt.all)

                def stage2():
                    for sb in range(c0 // 128, nq_sub):
                        wq = min(128, W)
                        lst_ = last if last_fn is None else last_fn(sb)
                        K.op("pe", lambda e, sb=sb, lst_=lst_: e.matmul(po[0:wq, sb, :], lhsT=pt[0:kn, sb * 128:sb * 128 + wq], rhs=v_ap, start=(first and sb == 0), stop=lst_),
                             reads=pt.all + g["v"].all, writes=po.all)
                fifo.append(stage2)
                while len(fifo) > LAG:
                    fifo.pop(0)()

            def finish_o(po, wq, nq_sub, dst_ap, dst_dt):
                fifo.append(lambda: finish_o_now(po, wq, nq_sub, dst_ap, dst_dt))
                while len(fifo) > LAG:
                    fifo.pop(0)()

            def finish_o_now(po, wq, nq_sub, dst_ap, dst_dt):
                o = cnt["o"]
                cnt["o"] += 1
                rc = rcp[o % 2]
                ob = osb[o % 2]
                K.op("dve", lambda e: e.reciprocal(out=rc[0:wq, 0:nq_sub], in_=po[0:wq, 0:nq_sub, 64]), reads=po.all, writes=rc.all)
                K.op("dve", lambda e: e.tensor_tensor(out=ob[0:wq, 0:nq_sub, :], in0=po[0:wq, 0:nq_sub, 0:64],
                                                      in1=rc[0:wq, 0:nq_sub].rearrange("p (s o) -> p s o", o=1).broadcast_to([wq, nq_sub, 64]), op=ALU.mult),
                     reads=po.all + rc.all, writes=ob.all)
                K.dma("sp", dst_ap, ob[0:wq, 0:nq_sub, :], reads=ob.all, pwrites=dst_dt.all)

            def load_head(hh, sample):
                i = hh % 2
                kt, vt, qt = KTb[i], Vb[i], QTb[i]
                S = scrS if sample else scrP
                nk = NKS if sample else SEQ
                nkt = NKT if sample else 64
                nq = 128 if sample else NQ
                if hh < 8:
                    pr, two = hh // 2, hh % 2
                    K.dma("sp", kt[0:64, 0:nk], S["KTf"].ap[pr, two * 64:(two + 1) * 64, :], reads=S["KTf"].all, pwrites=kt.all)
                    K.dma("sp", vt[:, 0:nkt, 0:64], S["Vf"].ap[:, hh * 64:(hh + 1) * 64].rearrange("(t p) d -> p t d", p=128), reads=S["Vf"].all, pwrites=vt.all)
                    K.dma("sp", qt[0:64, 0:nq], S["QTf"].ap[pr, two * 64:(two + 1) * 64, :], reads=S["QTf"].all, pwrites=qt.all)
                    K.dma("sp", qt[64:65, 0:nq], S["CQ"].ap[hh:hh + 1, :], reads=S["CQ"].all, pwrites=qt.all)
                    if not sample:
                        K.dma("sp", kt[65:68, 0:nk], SKf.ap[:, hh, :], reads=SKf.all, pwrites=kt.all)
                else:
                    h = hh - 8
                    pr, two = h // 2, h % 2
                    K.dma("sp", kt[0:64, 0:nk], S["KTm"].ap[pr, two * 64:(two + 1) * 64, :], reads=S["KTm"].all, pwrites=kt.all)
                    K.dma("sp", kt[64:96, 0:nk], S["KPE"].ap, reads=S["KPE"].all, pwrites=kt.all)
                    K.dma("sp", vt[:, 0:nkt, 0:64], S["Vm"].ap[:, h * 64:(h + 1) * 64].rearrange("(t p) d -> p t d", p=128), reads=S["Vm"].all, pwrites=vt.all)
                    K.dma("sp", qt[0:96, 0:nq], S["QTm"].ap[h], reads=S["QTm"].all, pwrites=qt.all)
                return dict(kt=kt, v=vt, qt=qt)

            seq = [(hh, False) for hh in range(8)] + [(hh, True) for hh in range(8)] + [(hh, True) for hh in range(8, 16)] + [(hh, False) for hh in range(8, 16)]
            loaded = {}
            loaded[0] = load_head(*seq[0])
            for n, (hh, sample) in enumerate(seq):
                while fifo:
                    fifo.pop(0)()
                if n + 1 < len(seq):
                    loaded[n + 1] = load_head(*seq[n + 1])
                g = loaded.pop(n)
                fox = hh < 8
                R = 65 if fox else 96
                h = hh if fox else hh - 8
                gtype = 0 if fox else 1
                ocol = hh * 64
                if not sample:
                    if fox:
                        R = 68
                    for slot in range(NSLOT):
                        po = pO_[cnt["po"] % 2]
                        cnt["po"] += 1
                        nkt = 8 * (slot + 1)
                        for kt_i in range(nkt):
                            mask_ap = None
                            if kt_i >= 8 * slot:
                                mi = ((gtype * 2 + slot % 2) * 8 + (kt_i - 8 * slot)) * 512
                                mask_ap = mk[:, mi:mi + 512]
                            bias_ap = None
                            jd = kt_i - (8 * slot + 4)
                            c0 = 128 * jd if jd > 0 else 0
                            attend(g, g["kt"][0:R, kt_i * 128:(kt_i + 1) * 128], g["qt"][0:R, slot * 512:(slot + 1) * 512], 512, 128,
                                   g["v"][:, kt_i, :], bias_ap, negc.all, mask_ap, kt_i == 0, kt_i == nkt - 1, po, 4,
                                   c0=c0, last_fn=(lambda sb, kt_i=kt_i, slot=slot: kt_i == 8 * slot + 4 + sb),
                                   mask_w=(128 * (kt_i - 8 * slot + 1) if 0 <= kt_i - 8 * slot < 4 else None))
                        finish_o(po, 128, 4, OS.ap[slot * 512:(slot + 1) * 512, ocol:ocol + 64].rearrange("(s p) d -> p s d", p=128), OS)
                else:
                    for sidx in range(4):
                        po = pO_[cnt["po"] % 2]
                        cnt["po"] += 1
                        for t in range(TS):
                            kn = 128 if t < 16 else 32
                            kt_i = sidx * TS + t
                            mask_ap = smask[0:32, :] if (fox and t == 16) else None
                            bias_ap = negcs[0:kn, kt_i * 8 + h:kt_i * 8 + h + 1] if fox else None
                            attend(g, g["kt"][0:R, kt_i * 128:kt_i * 128 + kn], g["qt"][0:R, sidx * 32:(sidx + 1) * 32], 32, kn,
                                   g["v"][0:kn, kt_i, :], bias_ap, negcs.all, mask_ap, t == 0, t == TS - 1, po, 1)
                        finish_o(po, 32, 1, OSs.ap[sidx * 32:(sidx + 1) * 32, ocol:ocol + 64].rearrange("(s p) d -> p s d", p=32), OSs)
            while fifo:
                fifo.pop(0)()

        K.barrier()
        if DBG["stop"] == "B":
            return _done()
        K.dma("sp", w_out_sb, WOUT.ap, reads=WOUT.all + WA.all, writes=WA.all)
        K.dma("sp", w_gate_sb, WGATE.ap, reads=WGATE.all, pwrites=WA.all)
        K.dma("sp", w_proj_sb, WPROJ.ap, reads=WPROJ.all, pwrites=WA.all)

        def run_all(gens):
            live = list(gens)
            while live:
                for g_ in list(live):
                    try:
                        next(g_)
                    except StopIteration:
                        live.remove(g_)

        class View:
            def __init__(self, ap, res):
                self.ap_ = ap
                self.all = res

            def __getitem__(self, idx):
                return self.ap_[idx]

        def phaseC(x_src, p_src, o_src, y_dst, nblk, nsub):
            W = nsub * 128
            with ExitStack() as pc:
                xb = T(pc, "xC", [128, nsub, D], F32, nres=nsub)
                ob = T(pc, "oC", [128, nsub * D], F32)
                pb_ = T(pc, "pC", [128, nsub, 256], F32)
                abs_ = [T(pc, "aC%d" % i, [128, D], BF16) for i in range(nsub)]
                aT = T(pc, "aTC", [128, 8, W], BF16, nres=nsub)
                pT2 = T(pc, "pT2C", [128, 2, W], BF16, nres=nsub)
                u2T = T(pc, "u2TC", [128, 32, W], BF16)
                rl = [T(pc, "rlC%d" % i, [128, W], F32) for i in range(2)]
                tmp_real = [T(pc, "tmpC%d" % i, [128, D], F32) for i in range(2)]
                junk = T(pc, "junkC", [128, D], BF16)
                sts = [T(pc, "stC%d" % i, [128, 16], F32) for i in range(nsub)]
                wus = [T(pc, "wusC%d" % i, [128, 8, 512], BF16) for i in range(2)]
                wds = [T(pc, "wdsC%d" % i, [128, 32, 128], BF16) for i in range(2)]
                pq = [T(pc, "pqC%d" % i, [128, 1024], F32, psum=True, nres=2) for i in range(4)]
                tmp = [tmp_real[0], tmp_real[1]]
                for i in range(2):
                    tmp.append(View(wds[i][:].rearrange("p c w -> p (c w)").bitcast(F32)[:, 0:D], wds[i].all))
                ptrv = [View(pq[i][:, 0:512].bitcast(BF16), [pq[i].res[0]]) for i in range(4)]
                pfm = [View(pq[i][:, 0:512], [pq[i].res[0]]) for i in range(2)]
                c = dict(fm=0)
                u2flat = u2T[:].rearrange("p c w -> p (c w)").bitcast(F32)
                ostg = u2flat[:, 0:nsub * D]
                xstg = u2flat[:, nsub * D:2 * nsub * D].rearrange("p (s d) -> p s d", s=nsub)

                def prefetch(blk):
                    K.dma("sp", ostg.rearrange("p (s d) -> p s d", s=nsub), o_src.ap[blk * W:(blk + 1) * W, :].rearrange("(s p) d -> p s d", p=128), reads=o_src.all, writes=u2T.all)
                    K.dma("sp", xstg, x_src[blk * W:(blk + 1) * W, :].rearrange("(s p) d -> p s d", p=128), pwrites=u2T.all)

                def transposes(src_tile, ncol_chunks, dstT, s):
                    pt = ptrv[s]
                    for kc in range(ncol_chunks):
                        K.op("pe", lambda e, kc=kc: e.transpose(out=pt[:, kc * 128:(kc + 1) * 128], in_=src_tile[:, kc * 128:(kc + 1) * 128], identity=identb[:]),
                             reads=src_tile.all + identb.all, writes=pt.all)
                    yield
                    K.op("act", lambda e: e.activation(out=dstT[:, :, s * 128:(s + 1) * 128], in_=pt[:, 0:ncol_chunks * 128].rearrange("p (k n) -> p k n", k=ncol_chunks), func=AF.Copy),
                         reads=pt.all, writes=[dstT.res[s]])
                    yield

                def post_norm_residual(pk_ap, pk_res, s, gi, xin=None, xin_res=None):
                    st = sts[s]
                    K.op("act", lambda e: e.activation(out=junk[:], in_=pk_ap, func=AF.Square, accum_out=st[:, 0:1]), reads=pk_res, writes=st.all)
                    yield
                    rstd_from_ss(st[:, 0:1], D, st[:, 2:3], st[:, 1:2], st.all, st.all)
                    yield
                    tm = tmp[s]
                    K.op("dve", lambda e: e.scalar_tensor_tensor(out=tm[:], in0=pk_ap, scalar=st[:, 2:3], in1=gbc[:, gi * D:(gi + 1) * D], op0=ALU.mult, op1=ALU.mult),
                         reads=pk_res + st.all + gbc.all, writes=tm.all)
                    yield
                    if xin is None:
                        K.op("dve", lambda e: e.tensor_tensor(out=xb[:, s, :], in0=xb[:, s, :], in1=tm[:], op=ALU.add), reads=tm.all, writes=[xb.res[s]])
                    else:
                        K.op("dve", lambda e: e.tensor_tensor(out=xb[:, s, :], in0=xin, in1=tm[:], op=ALU.add), reads=tm.all + xin_res, writes=[xb.res[s]])
                    yield

                def pre_norm_T(s, dstT):
                    st = sts[s]
                    ab = abs_[s]
                    K.op("act", lambda e: e.activation(out=junk[:], in_=xb[:, s, :], func=AF.Square, accum_out=st[:, 4:5]), reads=[xb.res[s]], writes=st.all)
                    yield
                    rstd_from_ss(st[:, 4:5], D, st[:, 6:7], st[:, 5:6], st.all, st.all)
                    yield
                    K.op("dve", lambda e: e.tensor_scalar(out=ab[:], in0=xb[:, s, :], scalar1=st[:, 6:7], scalar2=None, op0=ALU.mult), reads=[xb.res[s]] + st.all, writes=ab.all)
                    yield
                    yield from transposes(ab, 8, dstT, s)

                def front(s):
                    st = sts[s]
                    ab = abs_[s]
                    for hf in range(2):
                        K.op("act", lambda e, hf=hf: e.activation(out=junk[:, 0:512], in_=ostg[:, s * D + hf * 512:s * D + (hf + 1) * 512], func=AF.Square, accum_out=st[:, 8 + hf:9 + hf]),
                             reads=u2T.all, writes=st.all)
                    yield
                    rstd_from_ss(st[:, 8:10], 512, st[:, 12:14], st[:, 10:12], st.all, st.all)
                    yield
                    for hf in range(2):
                        K.op("dve", lambda e, hf=hf: e.tensor_scalar(out=ab[:, hf * 512:(hf + 1) * 512], in0=ostg[:, s * D + hf * 512:s * D + (hf + 1) * 512], scalar1=st[:, 12 + hf:13 + hf], scalar2=None, op0=ALU.mult),
                             reads=u2T.all + st.all, writes=ab.all)
                    yield
                    yield from transposes(ab, 8, aT, s)
                    pk = pq[s]
                    for hf in range(2):
                        for kc in range(8):
                            K.op("pe", lambda e, hf=hf, kc=kc: e.matmul(pk[:, hf * 512:(hf + 1) * 512], lhsT=aT[:, kc, s * 128:(s + 1) * 128], rhs=w_out_sb[:, kc, hf * 512:(hf + 1) * 512], start=(kc == 0), stop=(kc == 7)),
                                 reads=[aT.res[s]] + WA.all, writes=[pk.res[hf]])
                    yield
                    yield from post_norm_residual(pk[:], pk.all, s, 0, xin=xstg[:, s, :], xin_res=u2T.all)
                    yield from pre_norm_T(s, aT)

                def back(s, blk, yT):
                    ab = abs_[s]
                    pk = pq[s]
                    for n_ in range(8):
                        K.op("pe", lambda e, n_=n_: e.transpose(out=pk[:, n_ * 128:(n_ + 1) * 128], in_=yT[:, n_, s * 128:(s + 1) * 128], identity=identf[:]),
                             reads=ob.all + identf.all, writes=[pk.res[n_ // 4]])
                    yield
                    yield from post_norm_residual(pk[:], pk.all, s, 1)
                    yield from pre_norm_T(s, aT)
                    K.op("pool", lambda e: e.tensor_copy(out=ab[:, 0:256], in_=pb_[:, s, :]), reads=pb_.all, writes=ab.all)
                    yield
                    yield from transposes(ab, 2, pT2, s)
                    for hf in range(2):
                        for kc in range(8):
                            K.op("pe", lambda e, hf=hf, kc=kc: e.matmul(pk[:, hf * 512:(hf + 1) * 512], lhsT=aT[:, kc, s * 128:(s + 1) * 128], rhs=w_gate_sb[:, kc, hf * 512:(hf + 1) * 512], start=(kc == 0), stop=(kc == 7)),
                                 reads=[aT.res[s]] + WA.all, writes=[pk.res[hf]])
                    yield
                    tg = tmp[s]
                    K.op("act", lambda e: e.activation(out=tg[:], in_=pk[:], func=AF.Exp, scale=-1.0), reads=pk.all, writes=tg.all)
                    yield
                    K.op("act", lambda e: e.activation(out=tg[:], in_=tg[:], func=AF.Ln, bias=1.0), reads=tg.all, writes=tg.all)
                    yield
                    K.op("act", lambda e: e.activation(out=tg[:], in_=tg[:], func=AF.Exp, scale=-1.0), reads=tg.all, writes=tg.all)
                    for hf in range(2):
                        for kc in range(2):
                            K.op("pe", lambda e, hf=hf, kc=kc: e.matmul(pk[:, hf * 512:(hf + 1) * 512], lhsT=pT2[:, kc, s * 128:(s + 1) * 128], rhs=w_proj_sb[:, kc, hf * 512:(hf + 1) * 512], start=(kc == 0), stop=(kc == 1)),
                                 reads=[pT2.res[s]] + WA.all, writes=[pk.res[hf]])
                    yield
                    K.op("dve", lambda e: e.tensor_tensor(out=tg[:], in0=pk[:], in1=tg[:], op=ALU.mult), reads=pk.all + tg.all, writes=tg.all)
                    yield
                    yield from post_norm_residual(tg[:], tg.all, s, 2)
                    K.dma("sp", y_dst[blk * W + s * 128:blk * W + (s + 1) * 128, :], xb[:, s, :], reads=[xb.res[s]])

                prefetch(0)
                for blk in range(nblk):
                    K.dma("sp", pb_[:], p_src[blk * W:(blk + 1) * W, :].rearrange("(s p) d -> p s d", p=128), writes=pb_.all)
                    run_all([front(s) for s in range(nsub)])
                    for cc in range(8):
                        wu = wus[cc % 2]
                        K.dma("sp", wu[:], WUP.ap[cc], reads=WUP.all, writes=wu.all)
                        for j in range(4):
                            ch = cc * 4 + j
                            pf = pfm[c["fm"] % 2]
                            c["fm"] += 1
                            for kc in range(8):
                                K.op("pe", lambda e, j=j, kc=kc: e.matmul(pf[:, 0:W], lhsT=wu[:, kc, j * 128:(j + 1) * 128], rhs=aT[:, kc, :], start=(kc == 0), stop=(kc == 7)),
                                     reads=wu.all + aT.all, writes=pf.all)
                            r_ = rl[ch % 2]
                            K.op("act", lambda e: e.activation(out=r_[:], in_=pf[:, 0:W], func=AF.Relu), reads=pf.all, writes=r_.all)
                            K.op("pool", lambda e, ch=ch: e.tensor_tensor(out=u2T[:, ch, :], in0=r_[:], in1=r_[:], op=ALU.mult), reads=r_.all, pwrites=u2T.all)
                    yT = ob[:].rearrange("p (n t) -> p n t", n=8)
                    for n_ in range(8):
                        wd = wds[n_ % 2]
                        K.dma("sp", wd[:], WDN.ap[n_], reads=WDN.all, writes=wd.all)
                        pf = pfm[c["fm"] % 2]
                        c["fm"] += 1
                        for kc in range(32):
                            K.op("pe", lambda e, kc=kc: e.matmul(pf[:, 0:W], lhsT=wd[:, kc, :], rhs=u2T[:, kc, :], start=(kc == 0), stop=(kc == 31)),
                                 reads=wd.all + u2T.all, writes=pf.all)
                        K.op("act", lambda e, n_=n_: e.activation(out=yT[:, n_, :], in_=pf[:, 0:W], func=AF.Copy), reads=pf.all, pwrites=ob.all)
                    if blk + 1 < nblk:
                        prefetch(blk + 1)
                    run_all([back(s, blk, yT) for s in range(nsub)])

        phaseC(xs, ps_in, OSs, ys_o, 1, 1)
        K.barrier()
        phaseC(xown, pown, OS, y_own, DBG.get("nslot", NSLOT), 4)
        K.finish()
        global _LASTK
        _LASTK = K
    return nc


_NC = None


def _own_blocks(r):
    out = []
    for i in range(NSLOT):
        odd = (i % 2 == 1)
        if r == 0:
            out.append(2 * i + (1 if odd else 0))
        else:
            out.append(2 * i + (0 if odd else 1))
    return out


def _consts():
    half = 16
    inv = (10000.0 ** (-np.arange(half, dtype=np.float32) / half)).astype(np.float32)

    def tables(pos):
        ang = pos.astype(np.float32)[:, None] * inv[None, :]
        return np.cos(ang).astype(np.float32), np.sin(ang).astype(np.float32)

    return tables


def _masks():
    k = np.arange(128)[:, None]
    q = np.arange(512)[None, :]
    out = np.zeros((2, 2, 8, 128, 512), np.float32)
    for g in range(2):
        for kind in range(2):
            for t in range(8):
                if kind == 0:
                    if t < 4:
                        m = np.zeros((128, 512), bool)
                    else:
                        kk = (t - 4) * 128 + k
                        m = (kk > q) if g == 0 else ((kk // 64) > (q // 64))
                else:
                    if t < 4:
                        kk = t * 128 + k
                        m = (kk > q) if g == 0 else ((kk // 64) > (q // 64))
                    else:
                        m = np.ones((128, 512), bool)
                out[g, kind, t] = np.where(m, NEG, 0.0)
    return out


def _prep(inp):
    f32 = np.float32
    bf = ml_dtypes.bfloat16
    tables = _consts()
    x_prompt = np.asarray(inp["x_prompt"], f32)
    x_sample = np.asarray(inp["x_sample"], f32)
    p_prompt = np.asarray(inp["p_prompt"], f32)[0]
    p_sample = np.asarray(inp["p_sample"], f32)[0]
    ident = np.eye(128, dtype=f32)
    su = (np.arange(128)[:, None] > np.arange(128)[None, :]).astype(f32)
    cos_all, sin_all = tables(np.arange(SEQ))
    rope_kv = np.concatenate([cos_all, cos_all, -sin_all, sin_all], axis=1).astype(f32)
    pos_s = PAST + (np.arange(128) % 32)
    cos_s, sin_s = tables(pos_s)
    rope_kv_s = np.concatenate([cos_s, cos_s, -sin_s, sin_s], axis=1).astype(f32)
    rope_q_s = np.concatenate([cos_s, sin_s], axis=1).astype(f32)
    masks_all = _masks()
    smask = np.zeros((128, 32), f32)
    smask[:32] = np.where(np.arange(32)[:, None] > np.arange(32)[None, :], NEG, 0.0)
    shared = dict(
        w_in=np.ascontiguousarray(inp["w_in"][0], f32), w_qup=np.ascontiguousarray(inp["w_mla_q_up"][0], f32),
        w_kvup=np.ascontiguousarray(inp["w_mla_kv_up"][0], f32), w_out=np.ascontiguousarray(inp["w_out"][0], f32),
        w_up=np.ascontiguousarray(inp["w_up"][0], f32), w_down=np.ascontiguousarray(inp["w_down"][0], f32),
        w_gate=np.ascontiguousarray(inp["w_ple_gate"][0], f32), w_proj=np.ascontiguousarray(inp["w_ple_proj"][0], f32),
        g_pre_mix=np.ascontiguousarray(inp["g_pre_mix"][0], f32), g_mla_q=np.ascontiguousarray(inp["g_mla_q"][0], f32),
        g_mla_kv=np.ascontiguousarray(inp["g_mla_kv"][0], f32),
        g_out=np.concatenate([inp["g_fox_out"][0], inp["g_mla_out"][0]]).astype(f32),
        g_post_mix=np.ascontiguousarray(inp["g_post_mix"][0], f32), g_pre_mlp=np.ascontiguousarray(inp["g_pre_mlp"][0], f32),
        g_post_mlp=np.ascontiguousarray(inp["g_post_mlp"][0], f32), g_ple=np.ascontiguousarray(inp["g_ple"][0], f32),
        g_ple_post=np.ascontiguousarray(inp["g_ple_post"][0], f32), b_f=np.ascontiguousarray(inp["b_fox_f"][0], f32),
        ident=ident, su=su, rope_kv=rope_kv, rope_kv_s=rope_kv_s, rope_q_s=rope_q_s, smask=smask.astype(bf),
    )
    in_maps = []
    owns = []
    for c in range(8):
        b, r = c // 2, c % 2
        own = _own_blocks(r)
        owns.append(own)
        rows = np.concatenate([np.arange(o * 512, (o + 1) * 512) for o in own])
        cq, sq = cos_all[rows], sin_all[rows]
        sel = np.zeros((NSLOT, NBLK), f32)
        for i, o in enumerate(own):
            sel[i, o] = 1.0
        mk = np.zeros((2, 2, 8, 128, 512), f32)
        for par in range(2):
            kind = 0 if own[par] == 2 * par + 1 else 1
            mk[:, par] = masks_all[:, kind]
        m = dict(shared)
        m.update(
            xall=np.ascontiguousarray(x_prompt[b]), xown=np.ascontiguousarray(x_prompt[b][rows]),
            pown=np.ascontiguousarray(p_prompt[b][rows]),
            xs=np.ascontiguousarray(x_sample[4 * c:4 * c + 4].reshape(128, D)),
            ps=np.ascontiguousarray(p_sample[4 * c:4 * c + 4].reshape(128, 256)),
            ck=np.ascontiguousarray(inp["cache_fox_k"][0, 4 * c:4 * c + 4].reshape(4, PAST, 512), f32),
            cv=np.ascontiguousarray(inp["cache_fox_v"][0, 4 * c:4 * c + 4].reshape(4, PAST, 512), f32),
            clf=np.ascontiguousarray(inp["cache_fox_logf"][0, 4 * c:4 * c + 4], f32),
            cckv=np.ascontiguousarray(inp["cache_mla_ckv"][0, 4 * c:4 * c + 4], f32),
            ckpe=np.ascontiguousarray(inp["cache_mla_kpe"][0, 4 * c:4 * c + 4], f32),
            rope_q=np.concatenate([cq * SC_M, sq * SC_M], axis=1).astype(f32),
            masks=mk.astype(bf),
            sel=np.ascontiguousarray(np.broadcast_to(sel.reshape(1, -1), (8, NSLOT * NBLK)), f32),
        )
        m["rope_q_s"] = (rope_q_s * SC_M).astype(f32)
        in_maps.append(m)
    return in_maps, owns


def kernel(**inp):
    global _NC
    if _NC is None:
        _NC = build()
    nc = _NC
    f32 = np.float32
    in_maps, owns = _prep(inp)
    res = run_bass_kernel_spmd(nc, in_maps, core_ids=list(range(8)))
    R = res.results
    y_p = np.zeros((4, SEQ, D), f32)
    for c in range(8):
        b = c // 2
        for i, o in enumerate(owns[c]):
            y_p[b, o * 512:(o + 1) * 512] = R[c]["y_own"][i * 512:(i + 1) * 512]
    y_s = np.concatenate([R[c]["ys_o"].reshape(4, 32, D) for c in range(8)], axis=0)
    fk_p = np.stack([R[2 * b]["fk_o"] for b in range(4)]).reshape(1, 4, SEQ, 8, 64)
    fv_p = np.stack([R[2 * b]["fv_o"] for b in range(4)]).reshape(1, 4, SEQ, 8, 64)
    lf_p = np.stack([R[2 * b]["lf_o"] for b in range(4)]).reshape(1, 4, SEQ, 8)
    ckv_p = np.stack([R[2 * b]["ckv_o"] for b in range(4)]).reshape(1, 4, SEQ, 256)
    kpe_p = np.stack([R[2 * b]["kpe_o"] for b in range(4)]).reshape(1, 4, SEQ, 32)
    fk_s = np.concatenate([R[c]["fks_o"].reshape(4, 32, 8, 64) for c in range(8)], axis=0)[None]
    fv_s = np.concatenate([R[c]["fvs_o"].reshape(4, 32, 8, 64) for c in range(8)], axis=0)[None]
    lf_s = np.concatenate([R[c]["lfs_o"].reshape(4, 32, 8) for c in range(8)], axis=0)[None]
    ckv_s = np.concatenate([R[c]["ckvs_o"].reshape(4, 32, 256) for c in range(8)], axis=0)[None]
    kpe_s = np.concatenate([R[c]["kpes_o"].reshape(4, 32, 32) for c in range(8)], axis=0)[None]
    return (y_p, y_s, fk_p, fv_p, lf_p, ckv_p, kpe_p, fk_s, fv_s, lf_s, ckv_s, kpe_s)
```
